# Optimizing a Trainium2 kernel written in Bass

```python
import math
import jax, jax.numpy as jnp
from jax import lax
import numpy as np

D_MODEL = 2048
BATCH = 4
SEQ = 4096
DEPTH = 1

MIX_WIDTH = D_MODEL
MLA_HEADS = 8
MLA_NOPE_DIM = 128
MLA_ROPE_DIM = 64
MLA_V_DIM = 128
MLA_QK_DIM = MLA_NOPE_DIM + MLA_ROPE_DIM
MLA_Q_RANK = D_MODEL // 4
MLA_KV_RANK = D_MODEL // 8
MLA_WIDTH = MLA_HEADS * MLA_V_DIM
ROPE_THETA = 10000.0
DIFF_WIDTH = MIX_WIDTH - MLA_WIDTH
DIFF_HEAD_DIM = 64
DIFF_HEADS = DIFF_WIDTH // (2 * DIFF_HEAD_DIM)

EPS = 1e-6
Q_BLOCK = 128

IN_SPLITS = [MLA_Q_RANK, MLA_KV_RANK, MLA_ROPE_DIM, MLA_WIDTH,
             DIFF_WIDTH, DIFF_WIDTH, DIFF_WIDTH, DIFF_WIDTH]
IN_WIDTH = sum(IN_SPLITS)
IN_SPLIT_IDX = [int(v) for v in np.cumsum(IN_SPLITS)[:-1]]

kernel_name = "hybrid_mla_diffattn_parallel_heads"


def rmsnorm(x, g):
    xf = x.astype(jnp.float32)
    y = xf * lax.rsqrt(jnp.mean(xf * xf, axis=-1, keepdims=True) + EPS)
    return (y * g.astype(jnp.float32)).astype(x.dtype)


def rope(t, pos):
    d = t.shape[-1]
    freqs = 1.0 / (ROPE_THETA ** (jnp.arange(0, d, 2, dtype=jnp.float32) / d))
    ang = pos.astype(jnp.float32)[:, None, :, None] * freqs
    cos, sin = jnp.cos(ang), jnp.sin(ang)
    tf = t.astype(jnp.float32)
    t1, t2 = tf[..., : d // 2], tf[..., d // 2:]
    return jnp.concatenate([t1 * cos - t2 * sin, t2 * cos + t1 * sin], axis=-1).astype(t.dtype)


def _to_blocks(t):
    b, h, s, d = t.shape
    return t.reshape(b, h, s // Q_BLOCK, Q_BLOCK, d).transpose(2, 0, 1, 3, 4)


def _from_blocks(t):
    nb, b, h, qb, d = t.shape
    return t.transpose(1, 0, 3, 2, 4).reshape(b, nb * qb, h, d)


def mla_attention(q, k, v):
    s_len = k.shape[2]
    scale = q.shape[-1] ** -0.5
    k_idx = jnp.arange(s_len)

    def step(args):
        qi, start = args
        s = jnp.einsum('bhqd,bhkd->bhqk', qi, k, preferred_element_type=jnp.float32) * scale
        causal = (start + jnp.arange(Q_BLOCK))[:, None] >= k_idx[None, :]
        s = jnp.where(causal, s, -jnp.inf)
        p = jax.nn.softmax(s, axis=-1)
        return jnp.einsum('bhqk,bhkd->bhqd', p.astype(v.dtype), v)

    starts = jnp.arange(s_len // Q_BLOCK) * Q_BLOCK
    return _from_blocks(lax.map(step, (_to_blocks(q), starts)))


def diff_attention(q1, q2, k1, k2, v, pos, lam):
    b, h, s_len, d = q1.shape
    scale = d ** -0.5
    k_idx = jnp.arange(s_len)
    slopes = 2.0 ** (-8.0 * (jnp.arange(h, dtype=jnp.float32) + 1.0) / h)
    pos_f = pos.astype(jnp.float32)
    pos_blocks = pos_f.reshape(b, s_len // Q_BLOCK, Q_BLOCK).transpose(1, 0, 2)

    def step(args):
        q1i, q2i, pi, start = args
        dist = jnp.abs(pi[:, None, :, None] - pos_f[:, None, None, :])
        bias = -slopes[None, :, None, None] * dist
        causal = (start + jnp.arange(Q_BLOCK))[:, None] >= k_idx[None, :]
        s1 = jnp.einsum('bhqd,bhkd->bhqk', q1i, k1, preferred_element_type=jnp.float32) * scale + bias
        s2 = jnp.einsum('bhqd,bhkd->bhqk', q2i, k2, preferred_element_type=jnp.float32) * scale + bias
        a = jax.nn.softmax(jnp.where(causal, s1, -jnp.inf), axis=-1) - lam * jax.nn.softmax(jnp.where(causal, s2, -jnp.inf), axis=-1)
        return jnp.einsum('bhqk,bhkd->bhqd', a.astype(v.dtype), v)

    starts = jnp.arange(s_len // Q_BLOCK) * Q_BLOCK
    return _from_blocks(lax.map(step, (_to_blocks(q1), _to_blocks(q2), pos_blocks, starts)))


def setup_inputs(seed: int = 0) -> dict:
    key = jax.random.key(seed)
    ks = jax.random.split(key, 16)
    f32 = jnp.float32

    def nrm(k, shape, fan_in):
        return jax.random.normal(k, shape, f32) * fan_in ** -0.5

    def gain(k, shape):
        return 1.0 + 0.05 * jax.random.normal(k, shape, f32)

    x = jax.random.normal(ks[0], (BATCH, SEQ, D_MODEL), f32)
    offs = jax.random.randint(ks[1], (BATCH, 1), 0, 1024, dtype=jnp.int32)
    positions = (jnp.arange(SEQ, dtype=jnp.int32)[None, :] + offs).astype(jnp.int32)
    return {
        "x": x,
        "positions": positions,
        "g_pre": gain(ks[2], (DEPTH, D_MODEL)),
        "w_in": nrm(ks[3], (DEPTH, D_MODEL, IN_WIDTH), D_MODEL),
        "g_q_a": gain(ks[4], (DEPTH, MLA_Q_RANK)),
        "w_q_b": nrm(ks[5], (DEPTH, MLA_Q_RANK, MLA_HEADS, MLA_QK_DIM), MLA_Q_RANK),
        "g_kv_a": gain(ks[6], (DEPTH, MLA_KV_RANK)),
        "w_kv_b": nrm(ks[7], (DEPTH, MLA_KV_RANK, MLA_HEADS, MLA_NOPE_DIM + MLA_V_DIM), MLA_KV_RANK),
        "lambda_q1": 0.1 * jax.random.normal(ks[8], (DEPTH, DIFF_HEAD_DIM), f32),
        "lambda_k1": 0.1 * jax.random.normal(ks[9], (DEPTH, DIFF_HEAD_DIM), f32),
        "lambda_q2": 0.1 * jax.random.normal(ks[10], (DEPTH, DIFF_HEAD_DIM), f32),
        "lambda_k2": 0.1 * jax.random.normal(ks[11], (DEPTH, DIFF_HEAD_DIM), f32),
        "g_diff_sub": gain(ks[12], (DEPTH, 2 * DIFF_HEAD_DIM)),
        "w_out": nrm(ks[13], (DEPTH, MIX_WIDTH, D_MODEL), MIX_WIDTH),
        "g_post": gain(ks[14], (DEPTH, D_MODEL)),
    }


def reference(x, positions, g_pre, w_in, g_q_a, w_q_b, g_kv_a, w_kv_b,
              lambda_q1, lambda_k1, lambda_q2, lambda_k2, g_diff_sub, w_out, g_post):
    b, s_len, _ = x.shape
    for l in range(DEPTH):
        h = rmsnorm(x, g_pre[l])
        proj = jnp.einsum('bsd,de->bse', h, w_in[l])
        q_lat, kv_lat, k_pe, gate_mla, dq, dk, dv, gate_diff = jnp.split(proj, IN_SPLIT_IDX, axis=-1)

        c_q = rmsnorm(q_lat, g_q_a[l])
        q = jnp.einsum('bsr,rhd->bhsd', c_q, w_q_b[l])
        q = jnp.concatenate([q[..., :MLA_NOPE_DIM], rope(q[..., MLA_NOPE_DIM:], positions)], axis=-1)
        c_kv = rmsnorm(kv_lat, g_kv_a[l])
        kv = jnp.einsum('bsr,rhd->bhsd', c_kv, w_kv_b[l])
        k_nope, v_mla = kv[..., :MLA_NOPE_DIM], kv[..., MLA_NOPE_DIM:]
        k_rot = rope(k_pe[:, None, :, :], positions)
        k = jnp.concatenate([k_nope, jnp.broadcast_to(k_rot, k_nope.shape[:-1] + (MLA_ROPE_DIM,))], axis=-1)
        o_mla = mla_attention(q, k, v_mla).reshape(b, s_len, MLA_WIDTH)

        dq = dq.reshape(b, s_len, DIFF_HEADS, 2, DIFF_HEAD_DIM).transpose(0, 2, 1, 3, 4)
        dk = dk.reshape(b, s_len, DIFF_HEADS, 2, DIFF_HEAD_DIM).transpose(0, 2, 1, 3, 4)
        dv = dv.reshape(b, s_len, DIFF_HEADS, 2 * DIFF_HEAD_DIM).transpose(0, 2, 1, 3)
        lam_init = 0.8 - 0.6 * math.exp(-0.3 * l)
        lam = (jnp.exp(jnp.sum(lambda_q1[l].astype(jnp.float32) * lambda_k1[l].astype(jnp.float32)))
               - jnp.exp(jnp.sum(lambda_q2[l].astype(jnp.float32) * lambda_k2[l].astype(jnp.float32)))
               + lam_init)
        o_diff = diff_attention(dq[..., 0, :], dq[..., 1, :], dk[..., 0, :], dk[..., 1, :], dv, positions, lam)
        o_diff = (rmsnorm(o_diff, g_diff_sub[l]) * (1.0 - lam_init)).reshape(b, s_len, DIFF_WIDTH)

        mixed = jnp.concatenate([o_mla * jax.nn.silu(gate_mla), o_diff * jax.nn.silu(gate_diff)], axis=-1)
        y = jnp.einsum('bse,ed->bsd', mixed, w_out[l])
        x = x + rmsnorm(y, g_post[l])
    return x
```

```python
import contextlib
import math
import numpy as np
import concourse.bass as bass
import concourse.mybir as mybir
from concourse.bass_utils import run_bass_kernel_spmd

F32 = mybir.dt.float32
BF16 = mybir.dt.bfloat16
I32 = mybir.dt.int32
AF = mybir.ActivationFunctionType
ALU = mybir.AluOpType
AX = mybir.AxisListType

D = 2048
INW = 5952
EPS = 1e-6
ENGS = ("pe", "act", "dve", "pool", "sp")
LAM_INIT = 0.8 - 0.6 * math.exp(-0.3 * 0)
TWO_PI = 2.0 * math.pi
C1 = 6.28125
C2 = float(np.float32(TWO_PI - C1))
PI_SAFE = 3.1415925


class Buf:
    __slots__ = ("name", "w", "r")

    def __init__(self, name=""):
        self.name = name
        self.w = None
        self.r = {}


class Sched:
    def __init__(self, nc, es):
        self.nc = nc
        self.es = es
        self.streams = {e: [] for e in ENGS}
        self.count = {e: 0 for e in ENGS}
        self.seen = {e: {} for e in ENGS}
        self.sem = {}
        self.dcount = {}
        for e in ENGS:
            self.sem[e] = es.enter_context(nc.semaphore("sem_" + e))

    def _dsem(self, key):
        if key not in self.sem:
            self.sem[key] = self.es.enter_context(self.nc.semaphore("dsem_" + key))
            self.dcount[key] = 0

    def _waits(self, eng, reads, writes):
        deps = {}

        def add(k, v):
            if deps.get(k, 0) < v:
                deps[k] = v
        for b in reads:
            if b.w is not None:
                add(*b.w)
        for b in writes:
            if b.w is not None:
                add(*b.w)
            for k, v in b.r.items():
                add(k, v)
        waits = []
        seen = self.seen[eng]
        for k, v in deps.items():
            if seen.get(k, 0) < v:
                seen[k] = v
                waits.append((k, v))
        return waits

    def _commit(self, tok, reads, writes):
        k, v = tok
        for b in reads:
            if b.r.get(k, 0) < v:
                b.r[k] = v
        for b in writes:
            b.w = tok
            b.r = {}

    def op(self, eng, fn, reads=(), writes=()):
        waits = self._waits(eng, reads, writes)
        self.count[eng] += 1
        tok = (eng, self.count[eng])
        self.streams[eng].append((waits, fn, (eng, 1)))
        self._commit(tok, reads, writes)
        return tok

    def dma(self, eng, key, fn, reads=(), writes=()):
        self._dsem(key)
        waits = self._waits(eng, reads, writes)
        self.dcount[key] += 16
        tok = (key, self.dcount[key])
        self.streams[eng].append((waits, fn, (key, 16)))
        self._commit(tok, reads, writes)
        return tok

    def wait_all(self, eng, toks):
        deps = {}
        for (k, v) in toks:
            if deps.get(k, 0) < v:
                deps[k] = v
        waits = [(k, v) for k, v in deps.items() if self.seen[eng].get(k, 0) < v]
        for k, v in waits:
            self.seen[eng][k] = v
        self.streams[eng].append((waits, None, None))

    def barrier(self):
        toks = [(e, self.count[e]) for e in ENGS if self.count[e] > 0]
        toks += [(k, v) for k, v in self.dcount.items() if v > 0]
        for e in ENGS:
            self.wait_all(e, toks)

    def emit(self):
        nc = self.nc
        with nc.Block() as block:
            def run(ename):
                def body(e):
                    for waits, fn, inc in self.streams[ename]:
                        for k, v in waits:
                            e.wait_ge(self.sem[k], v)
                        if fn is not None:
                            inst = fn(e)
                            inst.then_inc(self.sem[inc[0]], inc[1])
                return body
            block.tensor(run("pe"))
            block.scalar(run("act"))
            block.vector(run("dve"))
            block.gpsimd(run("pool"))
            block.sync(run("sp"))


class Arena:
    def __init__(self, nc):
        self.nc = nc
        self.base = ((nc.sbuf_base + 63) // 64) * 64
        self.top = nc.sbuf_top
        self.n = 0

    def at(self, off, shape, dtype, name=None):
        esz = 4 if dtype in (F32, I32) else 2
        n = esz
        for s in shape[1:]:
            n *= s
        assert self.base + off + n <= self.top, ("SBUF overflow", name, off, n)
        self.n += 1
        t = self.nc.alloc_sbuf_tensor_at(name or f"t{self.n}", list(shape), dtype, offset=self.base + off)
        return t, off + ((n + 63) // 64) * 64


class Bump:
    def __init__(self, arena, start, limit):
        self.a, self.p, self.limit = arena, start, limit

    def alloc(self, shape, dtype, name=None):
        t, self.p = self.a.at(self.p, shape, dtype, name)
        assert self.p <= self.limit, ("region overflow", name, self.p, self.limit)
        return t


def build_program(S, gws=(1, 1, 2, 4, 4, 4, 4, 4), mla_heads=8, diff_heads=8):
    NB = S // 128
    NBo = NB // 2
    Sh = S // 2
    NQ = Sh // 512
    NT = Sh // 512
    KB = 1024

    nc = bass.Bass("TRN2", target_bir_lowering=False)

    def din(name, shape, dt=F32):
        return nc.dram_tensor(name, list(shape), dt, kind="ExternalInput").ap()
    xp = din("xp", [S, D])
    posrow = din("posrow", [1, S], I32)
    postm = din("postm", [128, NB], I32)
    posmid = din("posmid", [1, NBo], I32)
    w_in = din("w_in", [D, INW])
    w_qb = din("w_qb", [512, 1536])
    w_kvb = din("w_kvb", [256, 2048])
    w_out = din("w_out", [D, D])
    g_pre = din("g_pre", [1, D])
    gqa_d = din("gqa", [128, 4])
    gkva_d = din("gkva", [128, 2])
    gsub_d = din("gsub", [128, 1])
    g_post = din("g_post", [1, D])
    lam_d = din("lam", [1, 256])
    ident_d = din("ident", [128, 128])
    masks_d = din("masks", [5, 128, 128])
    ropec_d = din("ropec", [128, 2])
    out = nc.dram_tensor("out", [Sh, D], F32, kind="ExternalOutput").ap()
    dkT = nc.dram_tensor("dkT_scr", [8, 128, S], BF16, kind="Internal").ap()
    dvs = nc.dram_tensor("dvs_scr", [NB, 128, 1024], BF16, kind="Internal").ap()
    dqT = nc.dram_tensor("dqT_scr", [8, 128, Sh], BF16, kind="Internal").ap()
    gate_scr = nc.dram_tensor("gate_scr", [16, 128, Sh], F32, kind="Internal").ap()

    w_in_v = w_in.rearrange("(c p) e -> p c e", p=128)
    w_qb_v = w_qb.rearrange("(c p) e -> p c e", p=128)
    w_kvb_v = w_kvb.rearrange("(c p) e -> p c e", p=128)
    w_out_v = w_out.rearrange("(c p) e -> p c e", p=128)

    with contextlib.ExitStack() as es:
        sch = Sched(nc, es)
        ar = Arena(nc)
        TOTAL = ar.top - ar.base
        ps = [es.enter_context(nc.psum_tensor(f"psb{i}", [128, 512], F32)) for i in range(8)]
        psB = [Buf(f"ps{i}") for i in range(8)]

        MIX_BYTES = 16 * Sh * 2
        CONST_END = 5120
        LAT_END = CONST_END + (2 * Sh * 4) + (2 * S * 2) + (S * 2) + (4 * Sh * 2) + 256
        MIX_OFF = ((TOTAL - MIX_BYTES) // 64) * 64
        assert LAT_END <= MIX_OFF

        cb = Bump(ar, 0, CONST_END)
        identb = cb.alloc([128, 128], BF16, "identb")
        onesb = cb.alloc([128, 128], BF16, "onesb")
        masksb = cb.alloc([128, 5, 128], BF16, "masksb")
        epsc = cb.alloc([128, 1], F32, "epsc")
        ropec = cb.alloc([128, 2], F32, "ropec")
        gqa = cb.alloc([128, 4], F32, "gqa")
        gkva = cb.alloc([128, 2], F32, "gkva")
        gsubs = cb.alloc([128, 1], F32, "gsubs")
        lamv = cb.alloc([128, 256], F32, "lamv")
        lamp = cb.alloc([128, 128], F32, "lamp")
        lams = cb.alloc([128, 2], F32, "lams")
        lame = cb.alloc([128, 2], F32, "lame")
        negl = cb.alloc([128, 1], F32, "negl")
        gsub8 = cb.alloc([128, 1], F32, "gsub8")
        poski = cb.alloc([128, NB], I32, "poski")
        poskf = cb.alloc([128, NB], F32, "poskf")
        pmidi = cb.alloc([128, NBo], I32, "pmidi")
        pmid1 = cb.alloc([128, NBo], F32, "pmid1")
        pmid2 = cb.alloc([128, NBo // 2], F32, "pmid2")
        pmid4 = cb.alloc([128, NBo // 4], F32, "pmid4")
        ssq = cb.alloc([128, 8], F32, "ssq")
        lnc = cb.alloc([128, 4], F32, "lnc")
        rsc = cb.alloc([128, 4], F32, "rsc")
        Bc = Buf("consts")

        lb = Bump(ar, CONST_END, LAT_END)
        cosT = lb.alloc([128, Sh], F32, "cosT")
        sinS = lb.alloc([128, Sh], F32, "sinS")
        ckvT = lb.alloc([128, 2, S], BF16, "ckvT")
        krT = lb.alloc([128, S], BF16, "krT")
        cqT = lb.alloc([128, 4, Sh], BF16, "cqT")
        Btab = Buf("tab")
        Bckv = [Buf(f"ckv{i}") for i in range(S // 512)]
        Bkr = [Buf(f"kr{i}") for i in range(S // 512)]
        Bcq = [Buf(f"cq{i}") for i in range(NT)]

        mixT, _ = ar.at(MIX_OFF, [128, 16, Sh], BF16, "mixT")
        Bmix = [Buf(f"mix{h}") for h in range(16)]

        toks_final = []

        sch.dma("pool", "c_id", lambda e: e.dma_start(out=identb[:], in_=ident_d[:, :]), writes=[Bc])
        sch.dma("pool", "c_mk", lambda e: e.dma_start(out=masksb[:], in_=masks_d.rearrange("m p q -> p m q")), writes=[Bc])
        sch.op("pool", lambda e: e.memset(onesb[:], 1.0), writes=[Bc])
        sch.op("pool", lambda e: e.memset(epsc[:], EPS), writes=[Bc])
        sch.op("pool", lambda e: e.memset(krT[64:128, :], 0.0), writes=[Bc])
        sch.dma("sp", "c_a", lambda e: e.dma_start(out=ropec[:], in_=ropec_d[:, :]), writes=[Bc])
        sch.dma("sp", "c_b", lambda e: e.dma_start(out=gqa[:], in_=gqa_d[:, :]), writes=[Bc])
        sch.dma("sp", "c_c", lambda e: e.dma_start(out=gkva[:], in_=gkva_d[:, :]), writes=[Bc])
        sch.dma("sp", "c_d", lambda e: e.dma_start(out=gsubs[:], in_=gsub_d[:, :]), writes=[Bc])
        sch.dma("sp", "c_e", lambda e: e.dma_start(out=lamv[:], in_=lam_d.broadcast_to([128, 256])), writes=[Bc])
        sch.dma("sp", "c_f", lambda e: e.dma_start(out=poski[:], in_=postm[:, :]), writes=[Bc])
        sch.dma("sp", "c_g", lambda e: e.dma_start(out=pmidi[:], in_=posmid.broadcast_to([128, NBo])), writes=[Bc])
        sch.op("dve", lambda e: e.tensor_tensor(out=lamp[:, 0:64], in0=lamv[:, 0:64], in1=lamv[:, 64:128], op=ALU.mult), reads=[Bc], writes=[Bc])
        sch.op("dve", lambda e: e.tensor_tensor(out=lamp[:, 64:128], in0=lamv[:, 128:192], in1=lamv[:, 192:256], op=ALU.mult), reads=[Bc], writes=[Bc])
        sch.op("dve", lambda e: e.tensor_reduce(out=lams[:], in_=lamp[:].rearrange("p (a b) -> p a b", a=2), axis=AX.X, op=ALU.add), reads=[Bc], writes=[Bc])
        sch.op("act", lambda e: e.activation(out=lame[:], in_=lams[:], func=AF.Exp), reads=[Bc], writes=[Bc])
        sch.op("dve", lambda e: e.tensor_tensor(out=negl[:], in0=lame[:, 1:2], in1=lame[:, 0:1], op=ALU.subtract), reads=[Bc], writes=[Bc])
        sch.op("dve", lambda e: e.tensor_scalar(out=negl[:], in0=negl[:], scalar1=-LAM_INIT, scalar2=None, op0=ALU.add), reads=[Bc], writes=[Bc])
        sch.op("dve", lambda e: e.tensor_scalar(out=gsub8[:], in0=gsubs[:], scalar1=(1.0 - LAM_INIT), scalar2=None, op0=ALU.mult), reads=[Bc], writes=[Bc])
        sch.op("dve", lambda e: e.tensor_copy(out=poskf[:], in_=poski[:]), reads=[Bc], writes=[Bc])
        sch.op("dve", lambda e: e.tensor_copy(out=pmid1[:], in_=pmidi[:]), reads=[Bc], writes=[Bc])
        sch.op("dve", lambda e: e.tensor_reduce(out=pmid2[:], in_=pmid1[:].rearrange("p (a b) -> p a b", b=2), axis=AX.X, op=ALU.add), reads=[Bc], writes=[Bc])
        sch.op("dve", lambda e: e.tensor_scalar(out=pmid2[:], in0=pmid2[:], scalar1=0.5, scalar2=None, op0=ALU.mult), reads=[Bc], writes=[Bc])
        sch.op("dve", lambda e: e.tensor_reduce(out=pmid4[:], in_=pmid1[:].rearrange("p (a b) -> p a b", b=4), axis=AX.X, op=ALU.add), reads=[Bc], writes=[Bc])
        sch.op("dve", lambda e: e.tensor_scalar(out=pmid4[:], in0=pmid4[:], scalar1=0.25, scalar2=None, op0=ALU.mult), reads=[Bc], writes=[Bc])
        pmids = {1: pmid1, 2: pmid2, 4: pmid4}

        s1 = Bump(ar, LAT_END, TOTAL)
        hT = s1.alloc([128, 16, Sh], BF16, "hT")
        BhT = [Buf(f"hT{i}") for i in range(NT)]
        wg = [s1.alloc([128, 16, 512], BF16, f"wg{i}") for i in range(2)]
        Bwg = [Buf("wg0"), Buf("wg1")]
        gpre_bc = s1.alloc([128, D], F32, "gpre_bc")
        xt = [s1.alloc([128, D], F32, f"xt{i}") for i in range(2)]
        Bxt = [Buf("xt0"), Buf("xt1")]
        xb = [s1.alloc([128, D], BF16, f"xb{i}") for i in range(2)]
        Bxb = [Buf("xb0"), Buf("xb1")]
        NSTG, NSTF = 3, 2
        stg = [s1.alloc([128, 512], BF16, f"stg{i}") for i in range(NSTG)]
        Bstg = [Buf(f"stg{i}") for i in range(NSTG)]
        stf = [s1.alloc([128, 512], F32, f"stf{i}") for i in range(NSTF)]
        Bstf = [Buf(f"stf{i}") for i in range(NSTF)]
        sq = s1.alloc([128, 4, 512], BF16, "sq")
        Bsq = Buf("sq")
        junk, _ = ar.at(s1.p - 4 * 512 * 2, [128, D], BF16, "junk")
        Bjunk = Bsq
        rst = s1.alloc([128, 512], F32, "rst")
        lnt = rst
        Brs = Buf("rs")
        rt1 = s1.alloc([128, 512], F32, "rt1")
        rt2 = s1.alloc([128, 512], F32, "rt2")
        Brt = Buf("rt")
        tmpo = LAT_END + 16 * Sh * 2
        t_posi, o2 = ar.at(tmpo, [128, KB], I32, "t_posi")
        t_x, o2 = ar.at(o2, [128, KB], F32, "t_x")
        t_ki, o2 = ar.at(o2, [128, KB], I32, "t_ki")
        t_kf, o2 = ar.at(o2, [128, KB], F32, "t_kf")
        t_r, o2 = ar.at(o2, [128, KB], F32, "t_r")
        assert o2 <= tmpo + 2 * 16 * 512 * 2

        sch.dma("sp", "c_gp", lambda e: e.dma_start(out=gpre_bc[:], in_=g_pre.broadcast_to([128, D])), writes=[Bc])

        cnt = {"stg": 0, "stf": 0, "blk": 0}
        Bdk = [Buf(f"dk{i}") for i in range(8)]
        Bdq = [Buf(f"dq{i}") for i in range(8)]
        Bgate = [Buf(f"gate{i}") for i in range(16)]
        Bdv = [Buf(f"dv{i}") for i in range(2)]

        def rope_tables(goff):
            for c0 in range(0, Sh, KB):
                n = min(KB, Sh - c0)
                sch.dma("sp", "t_pos", lambda e, c0=c0, n=n: e.dma_start(out=t_posi[:, 0:n], in_=posrow[0:1, goff + c0:goff + c0 + n].broadcast_to([128, n])), writes=[Bwg[0], Bwg[1]])
                for which in (0, 1):
                    sch.op("dve", lambda e, n=n: e.tensor_copy(out=t_x[:, 0:n], in_=t_posi[:, 0:n]), writes=[Bwg[0], Bwg[1]])
                    if which == 0:
                        sch.op("dve", lambda e, n=n: e.tensor_scalar(out=t_x[:, 0:n], in0=t_x[:, 0:n], scalar1=ropec[:, 0:1], scalar2=None, op0=ALU.mult), reads=[Bc], writes=[Bwg[0], Bwg[1]])
                    else:
                        sch.op("dve", lambda e, n=n: e.tensor_scalar(out=t_x[:, 0:n], in0=t_x[:, 0:n], scalar1=ropec[:, 0:1], scalar2=math.pi / 2, op0=ALU.mult, op1=ALU.add), reads=[Bc], writes=[Bwg[0], Bwg[1]])
                    sch.op("dve", lambda e, n=n: e.tensor_scalar(out=t_kf[:, 0:n], in0=t_x[:, 0:n], scalar1=1.0 / TWO_PI, scalar2=None, op0=ALU.mult), writes=[Bwg[0], Bwg[1]])
                    sch.op("dve", lambda e, n=n: e.tensor_copy(out=t_ki[:, 0:n], in_=t_kf[:, 0:n]), writes=[Bwg[0], Bwg[1]])
                    sch.op("dve", lambda e, n=n: e.tensor_copy(out=t_kf[:, 0:n], in_=t_ki[:, 0:n]), writes=[Bwg[0], Bwg[1]])
                    sch.op("dve", lambda e, n=n: e.scalar_tensor_tensor(out=t_r[:, 0:n], in0=t_kf[:, 0:n], scalar=-C1, in1=t_x[:, 0:n], op0=ALU.mult, op1=ALU.add), writes=[Bwg[0], Bwg[1]])
                    sch.op("dve", lambda e, n=n: e.scalar_tensor_tensor(out=t_r[:, 0:n], in0=t_kf[:, 0:n], scalar=-C2, in1=t_r[:, 0:n], op0=ALU.mult, op1=ALU.add), writes=[Bwg[0], Bwg[1]])
                    sch.op("dve", lambda e, n=n: e.tensor_scalar(out=t_r[:, 0:n], in0=t_r[:, 0:n], scalar1=PI_SAFE, scalar2=-PI_SAFE, op0=ALU.min, op1=ALU.max), writes=[Bwg[0], Bwg[1]])
                    if which == 0:
                        sch.op("act", lambda e, c0=c0, n=n: e.activation(out=sinS[:, c0:c0 + n], in_=t_r[:, 0:n], func=AF.Sin, scale=ropec[:, 1:2]), reads=[Bwg[0], Bwg[1], Bc], writes=[Btab])
                    else:
                        sch.op("act", lambda e, c0=c0, n=n: e.activation(out=cosT[:, c0:c0 + n], in_=t_r[:, 0:n], func=AF.Sin), reads=[Bwg[0], Bwg[1]], writes=[Btab])

        def stage0(goff):
            for tb in range(NBo):
                i = cnt["blk"] % 2
                cnt["blk"] += 1
                r0 = goff + tb * 128
                sch.dma("sp", f"xt{i}", lambda e, i=i, r0=r0: e.dma_start(out=xt[i][:], in_=xp[r0:r0 + 128, :]), writes=[Bxt[i]])
                sch.op("act", lambda e, i=i: e.activation(out=junk[:], in_=xt[i][:], func=AF.Square, accum_out=ssq[:, i:i + 1]), reads=[Bxt[i]], writes=[Bjunk, Bc])
                sch.op("act", lambda e, i=i: e.activation(out=lnc[:, i:i + 1], in_=ssq[:, i:i + 1], func=AF.Ln, scale=1.0 / D, bias=epsc[:, 0:1]), reads=[Bc], writes=[Bc])
                sch.op("act", lambda e, i=i: e.activation(out=rsc[:, i:i + 1], in_=lnc[:, i:i + 1], func=AF.Exp, scale=-0.5), reads=[Bc], writes=[Bc])
                sch.op("dve", lambda e, i=i: e.scalar_tensor_tensor(out=xb[i][:], in0=xt[i][:], scalar=rsc[:, i:i + 1], in1=gpre_bc[:], op0=ALU.mult, op1=ALU.mult), reads=[Bxt[i], Bc], writes=[Bxb[i]])
                for hb in range(2):
                    bank = 4 + 2 * i + hb
                    pv = ps[bank][:].bitcast(BF16).rearrange("p (c t) -> p c t", c=8)

                    def tr(e, i=i, hb=hb, pv=pv):
                        for c in range(8):
                            inst = e.transpose(pv[:, c, :], xb[i][:, (hb * 8 + c) * 128:(hb * 8 + c + 1) * 128], identb[:])
                        return inst
                    sch.op("pe", tr, reads=[Bxb[i], Bc], writes=[psB[bank]])
                    dst = hT[:, hb * 8:hb * 8 + 8, tb * 128:(tb + 1) * 128]
                    if hb == 0:
                        sch.op("act", lambda e, dst=dst, pv=pv: e.activation(out=dst, in_=pv[:, :, :], func=AF.Copy), reads=[psB[bank]], writes=[BhT[tb // 4]])
                    else:
                        sch.op("dve", lambda e, dst=dst, pv=pv: e.tensor_copy(out=dst, in_=pv[:, :, :]), reads=[psB[bank]], writes=[BhT[tb // 4]])

        def load_group(slot, pieces):
            for (d0, s0, n) in pieces:
                sch.dma("pool", f"wg{slot}", lambda e, d0=d0, s0=s0, n=n: e.dma_start(out=wg[slot][:, :, d0:d0 + n], in_=w_in_v[:, :, s0:s0 + n]), writes=[Bwg[slot]])

        def mm16(bank, lhs_fn, rhs_fn, reads, M=128, N=512):
            def f(e):
                for k in range(16):
                    inst = e.matmul(ps[bank][0:M, 0:N], lhsT=lhs_fn(k), rhs=rhs_fn(k), start=(k == 0), stop=(k == 15))
                return inst
            sch.op("pe", f, reads=reads, writes=[psB[bank]])

        def store_bf(bank, dst_ap, dbuf, scale=None, eng="dve"):
            i = cnt["stg"] % NSTG
            cnt["stg"] += 1
            if scale is not None:
                sch.op("dve", lambda e: e.tensor_scalar(out=stg[i][:], in0=ps[bank][:], scalar1=scale, scalar2=None, op0=ALU.mult), reads=[psB[bank]], writes=[Bstg[i]])
            elif eng == "dve":
                sch.op("dve", lambda e: e.tensor_copy(out=stg[i][:], in_=ps[bank][:]), reads=[psB[bank]], writes=[Bstg[i]])
            else:
                sch.op("act", lambda e: e.activation(out=stg[i][:], in_=ps[bank][:], func=AF.Copy), reads=[psB[bank]], writes=[Bstg[i]])
            sch.dma("pool", f"stg{i}", lambda e: e.dma_start(out=dst_ap, in_=stg[i][:]), reads=[Bstg[i]], writes=[dbuf])

        def store_gate(bank, dst_ap, dbuf):
            i = cnt["stf"] % NSTF
            cnt["stf"] += 1
            sch.op("act", lambda e: e.activation(out=stf[i][:], in_=ps[bank][:], func=AF.Silu), reads=[psB[bank]], writes=[Bstf[i]])
            sch.dma("pool", f"stf{i}", lambda e: e.dma_start(out=dst_ap, in_=stf[i][:]), reads=[Bstf[i]], writes=[dbuf])

        def rms_feature_major(nchunks, nfeat, gcols, dst_fn, dbufs_fn, tile):
            for c in range(nchunks):
                sch.op("act", lambda e, c=c: e.activation(out=sq[:, c, :], in_=ps[c][:], func=AF.Square), reads=[psB[c]], writes=[Bsq])

            def f(e):
                for c in range(nchunks):
                    inst = e.matmul(ps[4][:], lhsT=onesb[:], rhs=sq[:, c, :], start=(c == 0), stop=(c == nchunks - 1))
                return inst
            sch.op("pe", f, reads=[Bsq, Bc], writes=[psB[4]])
            sch.op("act", lambda e: e.activation(out=lnt[:], in_=ps[4][:], func=AF.Ln, scale=1.0 / nfeat, bias=epsc[:, 0:1]), reads=[psB[4], Bc], writes=[Brs])
            sch.op("act", lambda e: e.activation(out=rst[:], in_=lnt[:], func=AF.Exp, scale=-0.5), reads=[Brs], writes=[Brs])
            for c in range(nchunks):
                sch.op("dve", lambda e, c=c: e.scalar_tensor_tensor(out=dst_fn(c), in0=ps[c][:], scalar=gcols[:, c:c + 1], in1=rst[:], op0=ALU.mult, op1=ALU.mult),
                       reads=[psB[c], Brs, Bc], writes=dbufs_fn())

        def rope_combine(bankA, bankB, tcols, dst_ap, dbufs):
            sch.op("dve", lambda e: e.tensor_tensor(out=rt1[0:64, :], in0=ps[bankA][0:64, :], in1=cosT[0:64, tcols], op=ALU.mult), reads=[psB[bankA], Btab], writes=[Brt])
            sch.op("dve", lambda e: e.tensor_tensor(out=rt2[0:64, :], in0=ps[bankB][0:64, :], in1=sinS[0:64, tcols], op=ALU.mult), reads=[psB[bankB], Btab], writes=[Brt])
            sch.op("dve", lambda e: e.tensor_tensor(out=dst_ap, in0=rt1[0:64, :], in1=rt2[0:64, :], op=ALU.add), reads=[Brt], writes=dbufs)

        def stage1_groups(goff, own):
            groups = []
            pb = [0]

            def g_qlat(s):
                for t in range(NT):
                    tc = slice(t * 512, (t + 1) * 512)
                    for c in range(4):
                        mm16(c, lambda k, c=c, s=s: wg[s][:, k, c * 128:(c + 1) * 128], lambda k, tc=tc: hT[:, k, tc], [Bwg[s], BhT[t]])
                    rms_feature_major(4, 512, gqa, lambda c, tc=tc: cqT[:, c, tc], lambda t=t: [Bcq[t]], t)
            if own:
                groups.append(([(0, 0, 512)], g_qlat))

            def g_kvk(s):
                for t in range(NT):
                    tc = slice(t * 512, (t + 1) * 512)
                    gt_ = (goff // 512) + t
                    gc = slice(goff + t * 512, goff + (t + 1) * 512)
                    for c in range(2):
                        mm16(c, lambda k, c=c, s=s: wg[s][:, k, c * 128:(c + 1) * 128], lambda k, tc=tc: hT[:, k, tc], [Bwg[s], BhT[t]])
                    mm16(2, lambda k, s=s: wg[s][:, k, 256:320], lambda k, tc=tc: hT[:, k, tc], [Bwg[s], BhT[t]], M=64)
                    mm16(3, lambda k, s=s: wg[s][:, k, 320:384], lambda k, tc=tc: hT[:, k, tc], [Bwg[s], BhT[t]], M=64)
                    rms_feature_major(2, 256, gkva, lambda c, gc=gc: ckvT[:, c, gc], lambda gt_=gt_: [Bckv[gt_]], t)
                    rope_combine(2, 3, tc, krT[0:64, gc], [Bkr[gt_]])
            groups.append(([(0, 512, 320), (320, 800, 32), (352, 768, 32)], g_kvk))

            fm = []
            if own:
                fm += [("gate", 832, 0), ("gate", 832 + 512, 4), ("dq", 1856, 0), ("dq", 1856 + 512, 4)]
            fm += [("dk", 2880, 0), ("dk", 2880 + 512, 4)]
            if own:
                fm += [("gate", 4928, 8), ("gate", 4928 + 512, 12)]

            def mk_fm(kind, col0, hc0):
                def g_fm(s):
                    for t in range(NT):
                        tc = slice(t * 512, (t + 1) * 512)
                        gc = slice(goff + t * 512, goff + (t + 1) * 512)
                        base = (pb[0] % 2) * 4
                        pb[0] += 1
                        for c in range(4):
                            mm16(base + c, lambda k, c=c, s=s: wg[s][:, k, c * 128:(c + 1) * 128], lambda k, tc=tc: hT[:, k, tc], [Bwg[s], BhT[t]])
                        for c in range(4):
                            hc = hc0 + c
                            if kind == "gate":
                                store_gate(base + c, gate_scr[hc, :, tc], Bgate[hc])
                            elif kind == "dq":
                                store_bf(base + c, dqT[hc, :, tc], Bdq[hc], scale=0.125)
                            else:
                                store_bf(base + c, dkT[hc, :, gc], Bdk[hc], eng=("dve" if c % 2 == 0 else "act"))
                return g_fm
            for (kind, col0, hc0) in fm:
                groups.append(([(0, col0, 512)], mk_fm(kind, col0, hc0)))

            def mk_dv(g):
                def g_dv(s):
                    for tb in range(NBo):
                        bank = pb[0] % 8
                        pb[0] += 1
                        tcb = slice(tb * 128, (tb + 1) * 128)
                        mm16(bank, lambda k, tcb=tcb: hT[:, k, tcb], lambda k, s=s: wg[s][:, k, 0:512], [Bwg[s], BhT[tb // 4]])
                        nbg = goff // 128 + tb
                        store_bf(bank, dvs[nbg, :, g * 512:(g + 1) * 512], Bdv[g], eng=("dve" if tb % 2 == 0 else "act"))
                return g_dv
            for g in range(2):
                groups.append(([(0, 3904 + g * 512, 512)], mk_dv(g)))
            return groups

        gslot = [0]
        for (goff, own) in ((Sh, False), (0, True)):
            rope_tables(goff)
            groups = stage1_groups(goff, own)
            slots = []
            for gi in range(len(groups)):
                slots.append(gslot[0] % 2)
                gslot[0] += 1
            load_group(slots[0], groups[0][0])
            stage0(goff)
            for gi, (pieces, comp) in enumerate(groups):
                if gi + 1 < len(groups):
                    load_group(slots[gi + 1], groups[gi + 1][0])
                comp(slots[gi])

        sch.barrier()

        def build_steps():
            tiles = []
            for j in range(NQ):
                st = []
                for kb in range(4 * j):
                    st.append((kb, 0, None))
                    st.append((NBo + kb, 0, None))
                for i in range(4):
                    st.append((4 * j + i, i, 0))
                    st.append((NBo + 4 * j + i, i, 1 + i))
                tiles.append(st)
            return tiles
        steps_by_tile = build_steps()
        if mla_heads < 8 or diff_heads < 8:
            for hh in range(16):
                sch.op("pool", lambda e, hh=hh: e.memset(mixT[:, hh, :], 0.0), writes=[Bmix[hh]])

        def col_groups(c0, gw):
            res = []
            b = c0
            while b < 4:
                e_ = min(4, (b // gw + 1) * gw)
                res.append((b, e_))
                b = e_
            return res

        def attention(nbr, qk_fn, v_fn, exp_args_fn, gw, STpool, Obank_fn, Lbank_fn, PT, BPT, fin_a, fin_b, reads_qk, reads_v, la, reads_exp=(), defer=6):
            flat = [(j, t, stp) for j, st in enumerate(steps_by_tile) for t, stp in enumerate(st)]
            units = [(idx, br) for idx in range(len(flat)) for br in range(nbr)]
            NU = len(units)
            stc = [0]
            bank_of = {}
            pending = []

            def take_bank():
                bnk = STpool[stc[0] % len(STpool)]
                stc[0] += 1
                return bnk

            def emit_qk(u):
                idx, br = units[u]
                j, t, (kb, c0, mk) = flat[idx]
                bank = take_bank()
                bank_of[u] = bank
                sch.op("pe", lambda e: qk_fn(e, br, ps[bank], kb, j, c0), reads=reads_qk, writes=[psB[bank]])

            def emit_exp(u):
                idx, br = units[u]
                j, t, (kb, c0, mk) = flat[idx]
                bank = bank_of[u]
                pi = u % len(PT)
                for (b0, b1) in col_groups(c0, gw):
                    cs = slice(b0 * 128, b1 * 128)
                    kw = exp_args_fn(kb, (4 * j + b0) // gw)
                    sch.op("act", lambda e, cs=cs, kw=kw: e.activation(out=PT[pi][:, cs], in_=ps[bank][:, cs], func=AF.Exp, **kw),
                           reads=[psB[bank], Bc] + list(reads_exp), writes=[BPT[pi]])
                if mk is not None:
                    cs = slice(c0 * 128, (c0 + 1) * 128)
                    sch.op("dve", lambda e, cs=cs: e.tensor_tensor(out=PT[pi][:, cs], in0=PT[pi][:, cs], in1=masksb[:, mk, :], op=ALU.mult),
                           reads=[Bc], writes=[BPT[pi]])

            def emit_pv(u):
                idx, br = units[u]
                j, t, (kb, c0, mk) = flat[idx]
                first = (t == 0)
                last = (t == len(steps_by_tile[j]) - 1)
                cs = slice(c0 * 128, 512)
                pi = u % len(PT)
                ob = Obank_fn(br, j)
                lbk = Lbank_fn(br, j)

                def f(e):
                    e.matmul(ps[ob][:, cs], lhsT=v_fn(br, kb), rhs=PT[pi][:, cs], start=first, stop=last, skip_group_check=True)
                    return e.matmul(ps[lbk][:, cs], lhsT=onesb[:], rhs=PT[pi][:, cs], start=first, stop=last, skip_group_check=True)
                sch.op("pe", f, reads=[BPT[pi], Bc] + reads_v, writes=[psB[ob], psB[lbk]])
                if last and br == nbr - 1:
                    fin_a(j)
                    if fin_b is not None:
                        pending.append((u + defer, j))

            for u in range(min(la, NU)):
                emit_qk(u)
            for u in range(NU):
                emit_exp(u)
                if u + la < NU:
                    emit_qk(u + la)
                emit_pv(u)
                while pending and pending[0][0] <= u:
                    fin_b(pending.pop(0)[1], take_bank())
            while pending:
                fin_b(pending.pop(0)[1], take_bank())

        ab = Bump(ar, LAT_END, MIX_OFF)
        wq = [ab.alloc([128, 4, 256], BF16, f"wq{i}") for i in range(2)]
        wkv = [ab.alloc([128, 2, 256], BF16, f"wkv{i}") for i in range(2)]
        Bwq = [Buf("wq0"), Buf("wq1")]
        KnT = ab.alloc([128, S], BF16, "KnT")
        Vh = ab.alloc([128, NB, 128], BF16, "Vh")
        QnT = ab.alloc([128, Sh], BF16, "QnT")
        QrT = ab.alloc([128, Sh], BF16, "QrT")
        gtm = ab.alloc([128, Sh], F32, "gtm")
        BK, BV, BQ, Bg = Buf("K"), Buf("V"), Buf("Q"), Buf("g")
        PTm = [ab.alloc([128, 512], BF16, f"PTm{i}") for i in range(4)]
        BPTm = [Buf(f"PTm{i}") for i in range(4)]
        rL = [ab.alloc([128, 512], F32, f"rL{i}") for i in range(2)]
        tg = [ab.alloc([128, 512], F32, f"tg{i}") for i in range(2)]
        Bfin = [Buf("fin0"), Buf("fin1")]
        rt1m = ab.alloc([128, 512], F32, "rt1m")
        rt2m = ab.alloc([128, 512], F32, "rt2m")
        Brtm = Buf("rtm")
        sch.op("pool", lambda e: e.memset(QrT[64:128, :], 0.0), writes=[BQ])
        SCALE_MLA = 192.0 ** -0.5

        def load_mla_w(h):
            s = h % 2
            sch.dma("pool", f"wq{s}", lambda e: e.dma_start(out=wq[s][:, :, 0:192], in_=w_qb_v[:, :, h * 192:h * 192 + 192]), writes=[Bwq[s]])
            sch.dma("pool", f"wq{s}", lambda e: e.dma_start(out=wq[s][:, :, 192:224], in_=w_qb_v[:, :, h * 192 + 160:h * 192 + 192]), writes=[Bwq[s]])
            sch.dma("pool", f"wq{s}", lambda e: e.dma_start(out=wq[s][:, :, 224:256], in_=w_qb_v[:, :, h * 192 + 128:h * 192 + 160]), writes=[Bwq[s]])
            sch.dma("pool", f"wq{s}", lambda e: e.dma_start(out=wkv[s][:, :, :], in_=w_kvb_v[:, :, h * 256:(h + 1) * 256]), writes=[Bwq[s]])

        if mla_heads > 0:
            load_mla_w(0)
        for h in range(mla_heads):
            s = h % 2
            if h + 1 < mla_heads:
                load_mla_w(h + 1)
            sch.dma("sp", "gtm", lambda e, h=h: e.dma_start(out=gtm[:], in_=gate_scr[h, :, :]), reads=[Bgate[h]], writes=[Bg])
            prep_banks = [0, 1, 2, 7]
            pc = [0]

            def nb_():
                b = prep_banks[pc[0] % 4]
                pc[0] += 1
                return b
            for t in range(S // 512):
                bank = nb_()
                tc = slice(t * 512, (t + 1) * 512)

                def f(e, bank=bank, tc=tc, s=s):
                    for c in range(2):
                        inst = e.matmul(ps[bank][:], lhsT=wkv[s][:, c, 0:128], rhs=ckvT[:, c, tc], start=(c == 0), stop=(c == 1))
                    return inst
                sch.op("pe", f, reads=[Bwq[s], Bckv[t]], writes=[psB[bank]])
                if t % 2 == 0:
                    sch.op("dve", lambda e, bank=bank, tc=tc: e.tensor_copy(out=KnT[:, tc], in_=ps[bank][:]), reads=[psB[bank]], writes=[BK])
                else:
                    sch.op("act", lambda e, bank=bank, tc=tc: e.activation(out=KnT[:, tc], in_=ps[bank][:], func=AF.Copy), reads=[psB[bank]], writes=[BK])
            for q4 in range(NB // 4):
                bank = nb_()

                def f(e, bank=bank, q4=q4, s=s):
                    for jj in range(4):
                        blk = q4 * 4 + jj
                        for c in range(2):
                            inst = e.matmul(ps[bank][:, jj * 128:(jj + 1) * 128], lhsT=ckvT[:, c, blk * 128:(blk + 1) * 128], rhs=wkv[s][:, c, 128:256], start=(c == 0), stop=(c == 1))
                    return inst
                sch.op("pe", f, reads=[Bwq[s], Bckv[q4]], writes=[psB[bank]])
                dst = Vh[:, q4 * 4:q4 * 4 + 4, :]
                if q4 % 2 == 0:
                    sch.op("act", lambda e, bank=bank, dst=dst: e.activation(out=dst, in_=ps[bank][:].rearrange("p (a b) -> p a b", a=4), func=AF.Copy), reads=[psB[bank]], writes=[BV])
                else:
                    sch.op("dve", lambda e, bank=bank, dst=dst: e.tensor_copy(out=dst, in_=ps[bank][:].rearrange("p (a b) -> p a b", a=4)), reads=[psB[bank]], writes=[BV])
            for t in range(NT):
                tc = slice(t * 512, (t + 1) * 512)
                bank = nb_()

                def f(e, bank=bank, tc=tc, s=s):
                    for c in range(4):
                        inst = e.matmul(ps[bank][:], lhsT=wq[s][:, c, 0:128], rhs=cqT[:, c, tc], start=(c == 0), stop=(c == 3))
                    return inst
                sch.op("pe", f, reads=[Bwq[s], Bcq[t]], writes=[psB[bank]])
                sch.op("dve", lambda e, bank=bank, tc=tc: e.tensor_copy(out=QnT[:, tc], in_=ps[bank][:]), reads=[psB[bank]], writes=[BQ])
                bA = nb_()
                bB = nb_()

                def f2(e, bA=bA, bB=bB, tc=tc, s=s):
                    for c in range(4):
                        e.matmul(ps[bA][0:64, :], lhsT=wq[s][:, c, 128:192], rhs=cqT[:, c, tc], start=(c == 0), stop=(c == 3))
                    for c in range(4):
                        inst = e.matmul(ps[bB][0:64, :], lhsT=wq[s][:, c, 192:256], rhs=cqT[:, c, tc], start=(c == 0), stop=(c == 3))
                    return inst
                sch.op("pe", f2, reads=[Bwq[s], Bcq[t]], writes=[psB[bA], psB[bB]])
                sch.op("dve", lambda e, bA=bA, tc=tc: e.tensor_tensor(out=rt1m[0:64, :], in0=ps[bA][0:64, :], in1=cosT[0:64, tc], op=ALU.mult), reads=[psB[bA], Btab], writes=[Brtm])
                sch.op("dve", lambda e, bB=bB, tc=tc: e.tensor_tensor(out=rt2m[0:64, :], in0=ps[bB][0:64, :], in1=sinS[0:64, tc], op=ALU.mult), reads=[psB[bB], Btab], writes=[Brtm])
                sch.op("dve", lambda e, tc=tc: e.tensor_tensor(out=QrT[0:64, tc], in0=rt1m[0:64, :], in1=rt2m[0:64, :], op=ALU.add), reads=[Brtm], writes=[BQ])

            def qk_mla(e, br, pst, kb, j, c0):
                ks = slice(kb * 128, (kb + 1) * 128)
                qs = slice(j * 512 + c0 * 128, (j + 1) * 512)
                os_ = slice(c0 * 128, 512)
                e.matmul(pst[:, os_], lhsT=KnT[:, ks], rhs=QnT[:, qs], start=True, stop=False)
                return e.matmul(pst[:, os_], lhsT=krT[:, ks], rhs=QrT[:, qs], start=False, stop=True)

            def fin_mla(j, h=h):
                i = j % 2
                ob, lbk = 3 + i, 5 + i
                tc = slice(j * 512, (j + 1) * 512)
                sch.op("dve", lambda e: e.reciprocal(out=rL[i][:], in_=ps[lbk][:]), reads=[psB[lbk]], writes=[Bfin[i]])
                sch.op("dve", lambda e: e.tensor_tensor(out=tg[i][:], in0=rL[i][:], in1=gtm[:, tc], op=ALU.mult), reads=[Bg], writes=[Bfin[i]])
                sch.op("dve", lambda e: e.tensor_tensor(out=mixT[:, h, tc], in0=ps[ob][:], in1=tg[i][:], op=ALU.mult), reads=[psB[ob], Bfin[i]], writes=[Bmix[h]])

            attention(1, qk_mla, lambda br, kb: Vh[:, kb, :], lambda kb, g: dict(scale=SCALE_MLA), 4,
                      [0, 1, 2], lambda br, j: 3 + (j % 2), lambda br, j: 5 + (j % 2), PTm, BPTm, fin_mla, None,
                      [BK, BQ, Bkr[0]] + Bkr[1:], [BV], 2)

        sch.barrier()

        db = Bump(ar, CONST_END, MIX_OFF)
        wout = db.alloc([128, 16, D], BF16, "wout")
        Bwout = Buf("wout")
        fin_mark = db.p
        K12 = [db.alloc([128, S], BF16, f"K12_{i}") for i in range(2)]
        Q12 = [db.alloc([128, Sh], BF16, f"Q12_{i}") for i in range(2)]
        Vd = [db.alloc([128, NB, 128], BF16, f"Vd{i}") for i in range(2)]
        gtd = db.alloc([128, Sh], F32, "gtd")
        Bgtd = Buf("gtd")
        NGmax = NBo
        bt_single = db.alloc([128, NB, NGmax], F32, "bt")
        bt = [bt_single, bt_single]
        Bhd = [Buf("hd0"), Buf("hd1")]
        Bbt_single = Buf("bt")
        Bbt = [Bbt_single, Bbt_single]
        PTd = [db.alloc([128, 512], BF16, f"PTd{i}") for i in range(5)]
        BPTd = [Buf(f"PTd{i}") for i in range(5)]
        fA = [db.alloc([128, 512], F32, f"fA{i}") for i in range(2)]
        fB = [db.alloc([128, 512], F32, f"fB{i}") for i in range(2)]
        fC = [db.alloc([128, 512], F32, f"fC{i}") for i in range(2)]
        fD = [db.alloc([128, 512], F32, f"fD{i}") for i in range(2)]
        sqd = [db.alloc([128, 512], BF16, f"sqd{i}") for i in range(2)]
        Bfs = [Buf("fs0"), Buf("fs1")]

        def load_diff(h):
            s = h % 2
            sch.dma("sp", f"hd{s}", lambda e: e.dma_start(out=K12[s][:], in_=dkT[h, :, :]), reads=[Bdk[h]], writes=[Bhd[s]])
            sch.dma("sp", f"hd{s}", lambda e: e.dma_start(out=Q12[s][:], in_=dqT[h, :, :]), reads=[Bdq[h]], writes=[Bhd[s]])
            sch.dma("sp", f"hd{s}", lambda e: e.dma_start(out=Vd[s][:], in_=dvs[:, :, h * 128:(h + 1) * 128].rearrange("nb p d -> p nb d")), reads=[Bdv[h // 4]], writes=[Bhd[s]])

        def load_gtd(h):
            sch.dma("sp", "gtd", lambda e: e.dma_start(out=gtd[:], in_=gate_scr[8 + h, :, :]), reads=[Bgate[8 + h]], writes=[Bgtd])

        if diff_heads > 0:
            load_diff(0)
            load_gtd(0)
        for h in range(diff_heads):
            s = h % 2
            gw = gws[h]
            NG = NBo // gw
            slope = 2.0 ** (-(h + 1))
            if h + 1 < diff_heads:
                load_diff(h + 1)
            if h == 0:
                for g in range(4):
                    sch.dma("pool", "wout", lambda e, g=g: e.dma_start(out=wout[:, :, g * 512:(g + 1) * 512], in_=w_out_v[:, :, g * 512:(g + 1) * 512]), writes=[Bwout])
            btv = bt[s][:, :, 0:NG]
            sch.op("dve", lambda e, btv=btv, NG=NG, gw=gw: e.tensor_tensor(out=btv, in0=poskf[:].unsqueeze(2).broadcast_to([128, NB, NG]), in1=pmids[gw][:].unsqueeze(1).broadcast_to([128, NB, NG]), op=ALU.subtract),
                   reads=[Bc], writes=[Bbt[s]])
            sch.op("dve", lambda e, btv=btv, slope=slope: e.tensor_scalar(out=btv, in0=btv, scalar1=slope, scalar2=40.0, op0=ALU.mult, op1=ALU.min), writes=[Bbt[s]])

            def qk_diff(e, br, pst, kb, j, c0, s=s):
                ks = slice(kb * 128, (kb + 1) * 128)
                qs = slice(j * 512 + c0 * 128, (j + 1) * 512)
                os_ = slice(c0 * 128, 512)
                pr = slice(64 * br, 64 * br + 64)
                return e.matmul(pst[:, os_], lhsT=K12[s][pr, ks], rhs=Q12[s][pr, qs], start=True, stop=True)

            def fin_a(j, h=h, s=s):
                k = j % 2
                A, Bt, C, Dt = fA[k], fB[k], fC[k], fD[k]
                sch.op("dve", lambda e: e.tensor_copy(out=A[:], in_=ps[4][:]), reads=[psB[4]], writes=[Bfs[k]])
                sch.op("act", lambda e: e.activation(out=Bt[:], in_=ps[5][:], func=AF.Copy), reads=[psB[5]], writes=[Bfs[k]])
                sch.op("dve", lambda e: e.tensor_copy(out=C[:], in_=ps[6][:]), reads=[psB[6]], writes=[Bfs[k]])
                sch.op("act", lambda e: e.activation(out=Dt[:], in_=ps[7][:], func=AF.Copy), reads=[psB[7]], writes=[Bfs[k]])
                sch.op("dve", lambda e: e.reciprocal(out=C[:], in_=C[:]), writes=[Bfs[k]])
                sch.op("dve", lambda e: e.reciprocal(out=Dt[:], in_=Dt[:]), writes=[Bfs[k]])
                sch.op("dve", lambda e: e.tensor_tensor(out=A[:], in0=A[:], in1=C[:], op=ALU.mult), writes=[Bfs[k]])
                sch.op("dve", lambda e: e.tensor_tensor(out=Bt[:], in0=Bt[:], in1=Dt[:], op=ALU.mult), writes=[Bfs[k]])
                sch.op("dve", lambda e: e.scalar_tensor_tensor(out=A[:], in0=Bt[:], scalar=negl[:, 0:1], in1=A[:], op0=ALU.mult, op1=ALU.add), reads=[Bc], writes=[Bfs[k]])
                sch.op("act", lambda e: e.activation(out=sqd[k][:], in_=A[:], func=AF.Square), writes=[Bfs[k]])

            def fin_b(j, bank, h=h, s=s):
                k = j % 2
                A, Dt = fA[k], fD[k]
                tc = slice(j * 512, (j + 1) * 512)
                sch.op("pe", lambda e: e.matmul(ps[bank][:], lhsT=onesb[:], rhs=sqd[k][:], start=True, stop=True), reads=[Bfs[k], Bc], writes=[psB[bank]])
                sch.op("act", lambda e: e.activation(out=Dt[:], in_=ps[bank][:], func=AF.Ln, scale=1.0 / 128.0, bias=epsc[:, 0:1]), reads=[psB[bank], Bc], writes=[Bfs[k]])
                sch.op("act", lambda e: e.activation(out=Dt[:], in_=Dt[:], func=AF.Exp, scale=-0.5), writes=[Bfs[k]])
                sch.op("dve", lambda e: e.scalar_tensor_tensor(out=A[:], in0=A[:], scalar=gsub8[:, 0:1], in1=Dt[:], op0=ALU.mult, op1=ALU.mult), reads=[Bc], writes=[Bfs[k]])
                sch.op("dve", lambda e: e.tensor_tensor(out=mixT[:, 8 + h, tc], in0=A[:], in1=gtd[:, tc], op=ALU.mult), reads=[Bgtd, Bfs[k]], writes=[Bmix[8 + h]])

            attention(2, qk_diff, lambda br, kb, s=s: Vd[s][:, kb, :], lambda kb, g, s=s: dict(bias=bt[s][:, kb, g:g + 1], scale=1.0), gw,
                      [0, 1, 2, 3], lambda br, j: 4 + br, lambda br, j: 6 + br, PTd, BPTd, fin_a, fin_b,
                      [Bhd[s]], [Bhd[s]], 3, reads_exp=[Bbt[s]], defer=8)
            if h + 1 < diff_heads:
                load_gtd(h + 1)

        sch.barrier()

        fb = Bump(ar, fin_mark, MIX_OFF)
        gpost_bc = fb.alloc([128, D], F32, "gpost_bc")
        xo = [fb.alloc([128, D], F32, f"xo{i}") for i in range(2)]
        yo = [fb.alloc([128, D], F32, f"yo{i}") for i in range(2)]
        junk2 = fb.alloc([128, 512], BF16, "junk2")
        Bxo = [Buf("xo0"), Buf("xo1")]
        Byo = [Buf("yo0"), Buf("yo1")]
        Bj2 = Buf("junk2")
        Bgp = Buf("gpost")
        ss4 = [fb.alloc([128, 4], F32, f"ss4_{i}") for i in range(2)]
        ss1 = [fb.alloc([128, 1], F32, f"ss1_{i}") for i in range(2)]
        ln1 = [fb.alloc([128, 1], F32, f"ln1_{i}") for i in range(2)]
        rs1 = [fb.alloc([128, 1], F32, f"rs1_{i}") for i in range(2)]
        Bst = [Buf("st0"), Buf("st1")]
        sch.dma("sp", "gpost", lambda e: e.dma_start(out=gpost_bc[:], in_=g_post.broadcast_to([128, D])), writes=[Bgp])
        for tb in range(NBo):
            i = tb % 2
            tcb = slice(tb * 128, (tb + 1) * 128)
            sch.dma("sp", f"xo{i}", lambda e, i=i, tb=tb: e.dma_start(out=xo[i][:], in_=xp[tb * 128:(tb + 1) * 128, :]), writes=[Bxo[i]])
            for g in range(4):
                bank = 4 * i + g

                def f(e, bank=bank, g=g, tcb=tcb):
                    for k in range(16):
                        inst = e.matmul(ps[bank][:], lhsT=mixT[:, k, tcb], rhs=wout[:, k, g * 512:(g + 1) * 512], start=(k == 0), stop=(k == 15))
                    return inst
                sch.op("pe", f, reads=Bmix + [Bwout], writes=[psB[bank]])
                sch.op("act", lambda e, bank=bank, g=g, i=i: e.activation(out=junk2[:], in_=ps[bank][:], func=AF.Square, accum_out=ss4[i][:, g:g + 1]), reads=[psB[bank]], writes=[Bj2, Bst[i]])
            sch.op("dve", lambda e, i=i: e.tensor_reduce(out=ss1[i][:], in_=ss4[i][:], axis=AX.X, op=ALU.add), reads=[Bst[i]], writes=[Bst[i]])
            sch.op("act", lambda e, i=i: e.activation(out=ln1[i][:], in_=ss1[i][:], func=AF.Ln, scale=1.0 / D, bias=epsc[:, 0:1]), reads=[Bst[i], Bc], writes=[Bst[i]])
            sch.op("act", lambda e, i=i: e.activation(out=rs1[i][:], in_=ln1[i][:], func=AF.Exp, scale=-0.5), reads=[Bst[i]], writes=[Bst[i]])
            for g in range(4):
                bank = 4 * i + g
                gs = slice(g * 512, (g + 1) * 512)
                sch.op("dve", lambda e, bank=bank, gs=gs, i=i: e.scalar_tensor_tensor(out=yo[i][:, gs], in0=ps[bank][:], scalar=rs1[i][:, 0:1], in1=gpost_bc[:, gs], op0=ALU.mult, op1=ALU.mult),
                       reads=[psB[bank], Bst[i], Bgp], writes=[Byo[i]])
            sch.op("dve", lambda e, i=i: e.tensor_tensor(out=yo[i][:], in0=yo[i][:], in1=xo[i][:], op=ALU.add), reads=[Bxo[i]], writes=[Byo[i]])
            toks_final.append(sch.dma("sp", f"yo{i}", lambda e, i=i, tb=tb: e.dma_start(out=out[tb * 128:(tb + 1) * 128, :], in_=yo[i][:]), reads=[Byo[i]]))

        sch.wait_all("sp", toks_final)
        sch.emit()
    return nc


def own_blocks(NB, parity):
    own = [g for g in range(NB) if ((g % 4) in (0, 3)) == (parity == 0)]
    other = [g for g in range(NB) if g not in own]
    return own, other


def make_core_inputs(S, parity, xb, posb, shared):
    NB = S // 128
    own, other = own_blocks(NB, parity)
    order = own + other
    tok = np.concatenate([np.arange(g * 128, (g + 1) * 128) for g in order])
    xp = np.ascontiguousarray(xb[tok])
    pp = np.ascontiguousarray(posb[tok]).astype(np.int32)
    masks = np.zeros((5, 128, 128), np.float32)
    k = np.arange(128)[:, None]
    q = np.arange(128)[None, :]
    masks[0] = (q >= k).astype(np.float32)
    flags = [0, 1, 0, 1] if parity == 0 else [1, 0, 1, 0]
    for i in range(4):
        masks[1 + i] = float(flags[i])
    d = dict(shared)
    d.update({
        "xp": xp,
        "posrow": pp.reshape(1, S),
        "postm": np.ascontiguousarray(pp.reshape(NB, 128).T),
        "posmid": np.ascontiguousarray(pp.reshape(NB, 128)[:NB // 2, 64]).reshape(1, NB // 2),
        "masks": masks,
    })
    return d, tok[:S // 2]


def make_shared(g_pre, w_in, g_q_a, w_q_b, g_kv_a, w_kv_b, lambda_q1, lambda_k1, lambda_q2, lambda_k2,
                g_diff_sub, w_out, g_post):
    f = np.float32
    freq = (1.0 / (np.float32(10000.0) ** (np.arange(0, 64, 2, dtype=np.float32) / np.float32(64)))).astype(f)
    ropec = np.zeros((128, 2), f)
    ropec[0:32, 0] = freq
    ropec[32:64, 0] = freq
    ropec[0:32, 1] = -1.0
    ropec[32:64, 1] = 1.0
    return {
        "w_in": np.ascontiguousarray(w_in[0], dtype=f),
        "w_qb": np.ascontiguousarray(w_q_b[0].reshape(512, 1536), dtype=f),
        "w_kvb": np.ascontiguousarray(w_kv_b[0].reshape(256, 2048), dtype=f),
        "w_out": np.ascontiguousarray(w_out[0], dtype=f),
        "g_pre": np.ascontiguousarray(g_pre[0].reshape(1, D), dtype=f),
        "gqa": np.ascontiguousarray(g_q_a[0].reshape(4, 128).T, dtype=f),
        "gkva": np.ascontiguousarray(g_kv_a[0].reshape(2, 128).T, dtype=f),
        "gsub": np.ascontiguousarray(g_diff_sub[0].reshape(1, 128).T, dtype=f),
        "g_post": np.ascontiguousarray(g_post[0].reshape(1, D), dtype=f),
        "lam": np.concatenate([lambda_q1[0], lambda_k1[0], lambda_q2[0], lambda_k2[0]]).reshape(1, 256).astype(f),
        "ident": np.eye(128, dtype=f),
        "ropec": ropec,
    }


_PROG_CACHE = {}


def run_layer(x, positions, params, **bkw):
    B, S, _ = x.shape
    shared = make_shared(**params)
    key = (S, tuple(sorted(bkw.items())))
    if key not in _PROG_CACHE:
        _PROG_CACHE[key] = build_program(S, **bkw)
    nc = _PROG_CACHE[key]
    in_maps, toks = [], []
    for b in range(B):
        for par in range(2):
            d, tok = make_core_inputs(S, par, x[b], positions[b], shared)
            in_maps.append(d)
            toks.append((b, tok))
    ncores = len(in_maps)
    res = run_bass_kernel_spmd(nc, in_maps, core_ids=list(range(ncores)))
    outp = np.empty((B, S, D), np.float32)
    for ci, (b, tok) in enumerate(toks):
        outp[b, tok] = res.results[ci]["out"]
    return outp


def kernel(x, positions, g_pre, w_in, g_q_a, w_q_b, g_kv_a, w_kv_b, lambda_q1, lambda_k1, lambda_q2, lambda_k2,
           g_diff_sub, w_out, g_post):
    params = dict(g_pre=np.asarray(g_pre), w_in=np.asarray(w_in), g_q_a=np.asarray(g_q_a), w_q_b=np.asarray(w_q_b),
                  g_kv_a=np.asarray(g_kv_a), w_kv_b=np.asarray(w_kv_b), lambda_q1=np.asarray(lambda_q1),
                  lambda_k1=np.asarray(lambda_k1), lambda_q2=np.asarray(lambda_q2), lambda_k2=np.asarray(lambda_k2),
                  g_diff_sub=np.asarray(g_diff_sub), w_out=np.asarray(w_out), g_post=np.asarray(g_post))
    return run_layer(np.asarray(x, dtype=np.float32), np.asarray(positions), params)
```

```python
import contextlib
import math
import numpy as np
import concourse.bass as bass
import concourse.mybir as mybir
from concourse.bass_utils import run_bass_kernel_spmd

F32 = mybir.dt.float32
BF16 = mybir.dt.bfloat16
I32 = mybir.dt.int32
AF = mybir.ActivationFunctionType
ALU = mybir.AluOpType
AX = mybir.AxisListType

D = 2048
INW = 5952
EPS = 1e-6
ENGS = ("pe", "act", "dve", "pool", "sp")
LAM_INIT = 0.8 - 0.6 * math.exp(-0.3 * 0)
TWO_PI = 2.0 * math.pi
C1 = 6.28125
C2 = float(np.float32(TWO_PI - C1))
PI_SAFE = 3.1415925


class Buf:
    __slots__ = ("name", "w", "r")

    def __init__(self, name=""):
        self.name = name
        self.w = None
        self.r = {}


class Sched:
    def __init__(self, nc, es):
        self.nc = nc
        self.es = es
        self.streams = {e: [] for e in ENGS}
        self.count = {e: 0 for e in ENGS}
        self.seen = {e: {} for e in ENGS}
        self.sem = {}
        self.dcount = {}
        for e in ENGS:
            self.sem[e] = es.enter_context(nc.semaphore("sem_" + e))

    def _dsem(self, key):
        if key not in self.sem:
            self.sem[key] = self.es.enter_context(self.nc.semaphore("dsem_" + key))
            self.dcount[key] = 0

    def _waits(self, eng, reads, writes):
        deps = {}

        def add(k, v):
            if deps.get(k, 0) < v:
                deps[k] = v
        for b in reads:
            if b.w is not None:
                add(*b.w)
        for b in writes:
            if b.w is not None:
                add(*b.w)
            for k, v in b.r.items():
                add(k, v)
        waits = []
        seen = self.seen[eng]
        for k, v in deps.items():
            if seen.get(k, 0) < v:
                seen[k] = v
                waits.append((k, v))
        return waits

    def _commit(self, tok, reads, writes):
        k, v = tok
        for b in reads:
            if b.r.get(k, 0) < v:
                b.r[k] = v
        for b in writes:
            b.w = tok
            b.r = {}

    def op(self, eng, fn, reads=(), writes=()):
        waits = self._waits(eng, reads, writes)
        self.count[eng] += 1
        tok = (eng, self.count[eng])
        self.streams[eng].append((waits, fn, (eng, 1)))
        self._commit(tok, reads, writes)
        return tok

    def dma(self, eng, key, fn, reads=(), writes=()):
        self._dsem(key)
        waits = self._waits(eng, reads, writes)
        self.dcount[key] += 16
        tok = (key, self.dcount[key])
        self.streams[eng].append((waits, fn, (key, 16)))
        self._commit(tok, reads, writes)
        return tok

    def wait_all(self, eng, toks):
        deps = {}
        for (k, v) in toks:
            if deps.get(k, 0) < v:
                deps[k] = v
        waits = [(k, v) for k, v in deps.items() if self.seen[eng].get(k, 0) < v]
        for k, v in waits:
            self.seen[eng][k] = v
        self.streams[eng].append((waits, None, None))

    def barrier(self):
        toks = [(e, self.count[e]) for e in ENGS if self.count[e] > 0]
        toks += [(k, v) for k, v in self.dcount.items() if v > 0]
        for e in ENGS:
            self.wait_all(e, toks)

    def emit(self):
        nc = self.nc
        with nc.Block() as block:
            def run(ename):
                def body(e):
                    for waits, fn, inc in self.streams[ename]:
                        for k, v in waits:
                            e.wait_ge(self.sem[k], v)
                        if fn is not None:
                            inst = fn(e)
                            inst.then_inc(self.sem[inc[0]], inc[1])
                return body
            block.tensor(run("pe"))
            block.scalar(run("act"))
            block.vector(run("dve"))
            block.gpsimd(run("pool"))
            block.sync(run("sp"))


class Arena:
    def __init__(self, nc):
        self.nc = nc
        self.base = ((nc.sbuf_base + 63) // 64) * 64
        self.top = nc.sbuf_top
        self.n = 0

    def at(self, off, shape, dtype, name=None):
        esz = 4 if dtype in (F32, I32) else 2
        n = esz
        for s in shape[1:]:
            n *= s
        assert self.base + off + n <= self.top, ("SBUF overflow", name, off, n)
        self.n += 1
        t = self.nc.alloc_sbuf_tensor_at(name or f"t{self.n}", list(shape), dtype, offset=self.base + off)
        return t, off + ((n + 63) // 64) * 64


class Bump:
    def __init__(self, arena, start, limit):
        self.a, self.p, self.limit = arena, start, limit

    def alloc(self, shape, dtype, name=None):
        t, self.p = self.a.at(self.p, shape, dtype, name)
        assert self.p <= self.limit, ("region overflow", name, self.p, self.limit)
        return t


def build_program(S, gws=(1, 1, 2, 4, 4, 4, 4, 4), mla_heads=8, diff_heads=8):
    NB = S // 128
    NBo = NB // 2
    Sh = S // 2
    NQ = Sh // 512
    NT = Sh // 512
    KB = 1024

    nc = bass.Bass("TRN2", target_bir_lowering=False)

    def din(name, shape, dt=F32):
        return nc.dram_tensor(name, list(shape), dt, kind="ExternalInput").ap()
    xp = din("xp", [S, D])
    posrow = din("posrow", [1, S], I32)
    postm = din("postm", [128, NB], I32)
    posmid = din("posmid", [1, NBo], I32)
    w_in = din("w_in", [D, INW])
    w_qb = din("w_qb", [512, 1536])
    w_kvb = din("w_kvb", [256, 2048])
    w_out = din("w_out", [D, D])
    g_pre = din("g_pre", [1, D])
    gqa_d = din("gqa", [128, 4])
    gkva_d = din("gkva", [128, 2])
    gsub_d = din("gsub", [128, 1])
    g_post = din("g_post", [1, D])
    lam_d = din("lam", [1, 256])
    ident_d = din("ident", [128, 128])
    masks_d = din("masks", [5, 128, 128])
    ropec_d = din("ropec", [128, 2])
    out = nc.dram_tensor("out", [Sh, D], F32, kind="ExternalOutput").ap()
    dkT = nc.dram_tensor("dkT_scr", [8, 128, S], BF16, kind="Internal").ap()
    dvs = nc.dram_tensor("dvs_scr", [NB, 128, 1024], BF16, kind="Internal").ap()
    dqT = nc.dram_tensor("dqT_scr", [8, 128, Sh], BF16, kind="Internal").ap()
    gate_scr = nc.dram_tensor("gate_scr", [16, 128, Sh], F32, kind="Internal").ap()

    w_in_v = w_in.rearrange("(c p) e -> p c e", p=128)
    w_qb_v = w_qb.rearrange("(c p) e -> p c e", p=128)
    w_kvb_v = w_kvb.rearrange("(c p) e -> p c e", p=128)
    w_out_v = w_out.rearrange("(c p) e -> p c e", p=128)

    with contextlib.ExitStack() as es:
        sch = Sched(nc, es)
        ar = Arena(nc)
        TOTAL = ar.top - ar.base
        ps = [es.enter_context(nc.psum_tensor(f"psb{i}", [128, 512], F32)) for i in range(8)]
        psB = [Buf(f"ps{i}") for i in range(8)]

        MIX_BYTES = 16 * Sh * 2
        CONST_END = 5120
        LAT_END = CONST_END + (2 * Sh * 4) + (2 * S * 2) + (S * 2) + (4 * Sh * 2) + 256
        MIX_OFF = ((TOTAL - MIX_BYTES) // 64) * 64
        assert LAT_END <= MIX_OFF

        cb = Bump(ar, 0, CONST_END)
        identb = cb.alloc([128, 128], BF16, "identb")
        onesb = cb.alloc([128, 128], BF16, "onesb")
        masksb = cb.alloc([128, 5, 128], BF16, "masksb")
        epsc = cb.alloc([128, 1], F32, "epsc")
        ropec = cb.alloc([128, 2], F32, "ropec")
        gqa = cb.alloc([128, 4], F32, "gqa")
        gkva = cb.alloc([128, 2], F32, "gkva")
        gsubs = cb.alloc([128, 1], F32, "gsubs")
        lamv = cb.alloc([128, 256], F32, "lamv")
        lamp = cb.alloc([128, 128], F32, "lamp")
        lams = cb.alloc([128, 2], F32, "lams")
        lame = cb.alloc([128, 2], F32, "lame")
        negl = cb.alloc([128, 1], F32, "negl")
        gsub8 = cb.alloc([128, 1], F32, "gsub8")
        poski = cb.alloc([128, NB], I32, "poski")
        poskf = cb.alloc([128, NB], F32, "poskf")
        pmidi = cb.alloc([128, NBo], I32, "pmidi")
        pmid1 = cb.alloc([128, NBo], F32, "pmid1")
        pmid2 = cb.alloc([128, NBo // 2], F32, "pmid2")
        pmid4 = cb.alloc([128, NBo // 4], F32, "pmid4")
        ssq = cb.alloc([128, 8], F32, "ssq")
        lnc = cb.alloc([128, 4], F32, "lnc")
        rsc = cb.alloc([128, 4], F32, "rsc")
        Bc = Buf("consts")

        lb = Bump(ar, CONST_END, LAT_END)
        cosT = lb.alloc([128, Sh], F32, "cosT")
        sinS = lb.alloc([128, Sh], F32, "sinS")
        ckvT = lb.alloc([128, 2, S], BF16, "ckvT")
        krT = lb.alloc([128, S], BF16, "krT")
        cqT = lb.alloc([128, 4, Sh], BF16, "cqT")
        Btab = Buf("tab")
        Bckv = [Buf(f"ckv{i}") for i in range(S // 512)]
        Bkr = [Buf(f"kr{i}") for i in range(S // 512)]
        Bcq = [Buf(f"cq{i}") for i in range(NT)]

        mixT, _ = ar.at(MIX_OFF, [128, 16, Sh], BF16, "mixT")
        Bmix = [Buf(f"mix{h}") for h in range(16)]

        toks_final = []

        sch.dma("pool", "c_id", lambda e: e.dma_start(out=identb[:], in_=ident_d[:, :]), writes=[Bc])
        sch.dma("pool", "c_mk", lambda e: e.dma_start(out=masksb[:], in_=masks_d.rearrange("m p q -> p m q")), writes=[Bc])
        sch.op("pool", lambda e: e.memset(onesb[:], 1.0), writes=[Bc])
        sch.op("pool", lambda e: e.memset(epsc[:], EPS), writes=[Bc])
        sch.op("pool", lambda e: e.memset(krT[64:128, :], 0.0), writes=[Bc])
        sch.dma("sp", "c_a", lambda e: e.dma_start(out=ropec[:], in_=ropec_d[:, :]), writes=[Bc])
        sch.dma("sp", "c_b", lambda e: e.dma_start(out=gqa[:], in_=gqa_d[:, :]), writes=[Bc])
        sch.dma("sp", "c_c", lambda e: e.dma_start(out=gkva[:], in_=gkva_d[:, :]), writes=[Bc])
        sch.dma("sp", "c_d", lambda e: e.dma_start(out=gsubs[:], in_=gsub_d[:, :]), writes=[Bc])
        sch.dma("sp", "c_e", lambda e: e.dma_start(out=lamv[:], in_=lam_d.broadcast_to([128, 256])), writes=[Bc])
        sch.dma("sp", "c_f", lambda e: e.dma_start(out=poski[:], in_=postm[:, :]), writes=[Bc])
        sch.dma("sp", "c_g", lambda e: e.dma_start(out=pmidi[:], in_=posmid.broadcast_to([128, NBo])), writes=[Bc])
        sch.op("dve", lambda e: e.tensor_tensor(out=lamp[:, 0:64], in0=lamv[:, 0:64], in1=lamv[:, 64:128], op=ALU.mult), reads=[Bc], writes=[Bc])
        sch.op("dve", lambda e: e.tensor_tensor(out=lamp[:, 64:128], in0=lamv[:, 128:192], in1=lamv[:, 192:256], op=ALU.mult), reads=[Bc], writes=[Bc])
        sch.op("dve", lambda e: e.tensor_reduce(out=lams[:], in_=lamp[:].rearrange("p (a b) -> p a b", a=2), axis=AX.X, op=ALU.add), reads=[Bc], writes=[Bc])
        sch.op("act", lambda e: e.activation(out=lame[:], in_=lams[:], func=AF.Exp), reads=[Bc], writes=[Bc])
        sch.op("dve", lambda e: e.tensor_tensor(out=negl[:], in0=lame[:, 1:2], in1=lame[:, 0:1], op=ALU.subtract), reads=[Bc], writes=[Bc])
        sch.op("dve", lambda e: e.tensor_scalar(out=negl[:], in0=negl[:], scalar1=-LAM_INIT, scalar2=None, op0=ALU.add), reads=[Bc], writes=[Bc])
        sch.op("dve", lambda e: e.tensor_scalar(out=gsub8[:], in0=gsubs[:], scalar1=(1.0 - LAM_INIT), scalar2=None, op0=ALU.mult), reads=[Bc], writes=[Bc])
        sch.op("dve", lambda e: e.tensor_copy(out=poskf[:], in_=poski[:]), reads=[Bc], writes=[Bc])
        sch.op("dve", lambda e: e.tensor_copy(out=pmid1[:], in_=pmidi[:]), reads=[Bc], writes=[Bc])
        sch.op("dve", lambda e: e.tensor_reduce(out=pmid2[:], in_=pmid1[:].rearrange("p (a b) -> p a b", b=2), axis=AX.X, op=ALU.add), reads=[Bc], writes=[Bc])
        sch.op("dve", lambda e: e.tensor_scalar(out=pmid2[:], in0=pmid2[:], scalar1=0.5, scalar2=None, op0=ALU.mult), reads=[Bc], writes=[Bc])
        sch.op("dve", lambda e: e.tensor_reduce(out=pmid4[:], in_=pmid1[:].rearrange("p (a b) -> p a b", b=4), axis=AX.X, op=ALU.add), reads=[Bc], writes=[Bc])
        sch.op("dve", lambda e: e.tensor_scalar(out=pmid4[:], in0=pmid4[:], scalar1=0.25, scalar2=None, op0=ALU.mult), reads=[Bc], writes=[Bc])
        pmids = {1: pmid1, 2: pmid2, 4: pmid4}

        s1 = Bump(ar, LAT_END, TOTAL)
        hT = s1.alloc([128, 16, Sh], BF16, "hT")
        BhT = [Buf(f"hT{i}") for i in range(NT)]
        wg = [s1.alloc([128, 16, 512], BF16, f"wg{i}") for i in range(2)]
        Bwg = [Buf("wg0"), Buf("wg1")]
        gpre_bc = s1.alloc([128, D], F32, "gpre_bc")
        xt = [s1.alloc([128, D], F32, f"xt{i}") for i in range(2)]
        Bxt = [Buf("xt0"), Buf("xt1")]
        xb = [s1.alloc([128, D], BF16, f"xb{i}") for i in range(2)]
        Bxb = [Buf("xb0"), Buf("xb1")]
        NSTG, NSTF = 3, 2
        stg = [s1.alloc([128, 512], BF16, f"stg{i}") for i in range(NSTG)]
        Bstg = [Buf(f"stg{i}") for i in range(NSTG)]
        stf = [s1.alloc([128, 512], F32, f"stf{i}") for i in range(NSTF)]
        Bstf = [Buf(f"stf{i}") for i in range(NSTF)]
        sq = s1.alloc([128, 4, 512], BF16, "sq")
        Bsq = Buf("sq")
        junk, _ = ar.at(s1.p - 4 * 512 * 2, [128, D], BF16, "junk")
        Bjunk = Bsq
        rst = s1.alloc([128, 512], F32, "rst")
        lnt = rst
        Brs = Buf("rs")
        rt1 = s1.alloc([128, 512], F32, "rt1")
        rt2 = s1.alloc([128, 512], F32, "rt2")
        Brt = Buf("rt")
        tmpo = LAT_END + 16 * Sh * 2
        t_posi, o2 = ar.at(tmpo, [128, KB], I32, "t_posi")
        t_x, o2 = ar.at(o2, [128, KB], F32, "t_x")
        t_ki, o2 = ar.at(o2, [128, KB], I32, "t_ki")
        t_kf, o2 = ar.at(o2, [128, KB], F32, "t_kf")
        t_r, o2 = ar.at(o2, [128, KB], F32, "t_r")
        assert o2 <= tmpo + 2 * 16 * 512 * 2

        sch.dma("sp", "c_gp", lambda e: e.dma_start(out=gpre_bc[:], in_=g_pre.broadcast_to([128, D])), writes=[Bc])

        cnt = {"stg": 0, "stf": 0, "blk": 0}
        Bdk = [Buf(f"dk{i}") for i in range(8)]
        Bdq = [Buf(f"dq{i}") for i in range(8)]
        Bgate = [Buf(f"gate{i}") for i in range(16)]
        Bdv = [Buf(f"dv{i}") for i in range(2)]

        def rope_tables(goff):
            for c0 in range(0, Sh, KB):
                n = min(KB, Sh - c0)
                sch.dma("sp", "t_pos", lambda e, c0=c0, n=n: e.dma_start(out=t_posi[:, 0:n], in_=posrow[0:1, goff + c0:goff + c0 + n].broadcast_to([128, n])), writes=[Bwg[0], Bwg[1]])
                for which in (0, 1):
                    sch.op("dve", lambda e, n=n: e.tensor_copy(out=t_x[:, 0:n], in_=t_posi[:, 0:n]), writes=[Bwg[0], Bwg[1]])
                    if which == 0:
                        sch.op("dve", lambda e, n=n: e.tensor_scalar(out=t_x[:, 0:n], in0=t_x[:, 0:n], scalar1=ropec[:, 0:1], scalar2=None, op0=ALU.mult), reads=[Bc], writes=[Bwg[0], Bwg[1]])
                    else:
                        sch.op("dve", lambda e, n=n: e.tensor_scalar(out=t_x[:, 0:n], in0=t_x[:, 0:n], scalar1=ropec[:, 0:1], scalar2=math.pi / 2, op0=ALU.mult, op1=ALU.add), reads=[Bc], writes=[Bwg[0], Bwg[1]])
                    sch.op("dve", lambda e, n=n: e.tensor_scalar(out=t_kf[:, 0:n], in0=t_x[:, 0:n], scalar1=1.0 / TWO_PI, scalar2=None, op0=ALU.mult), writes=[Bwg[0], Bwg[1]])
                    sch.op("dve", lambda e, n=n: e.tensor_copy(out=t_ki[:, 0:n], in_=t_kf[:, 0:n]), writes=[Bwg[0], Bwg[1]])
                    sch.op("dve", lambda e, n=n: e.tensor_copy(out=t_kf[:, 0:n], in_=t_ki[:, 0:n]), writes=[Bwg[0], Bwg[1]])
                    sch.op("dve", lambda e, n=n: e.scalar_tensor_tensor(out=t_r[:, 0:n], in0=t_kf[:, 0:n], scalar=-C1, in1=t_x[:, 0:n], op0=ALU.mult, op1=ALU.add), writes=[Bwg[0], Bwg[1]])
                    sch.op("dve", lambda e, n=n: e.scalar_tensor_tensor(out=t_r[:, 0:n], in0=t_kf[:, 0:n], scalar=-C2, in1=t_r[:, 0:n], op0=ALU.mult, op1=ALU.add), writes=[Bwg[0], Bwg[1]])
                    sch.op("dve", lambda e, n=n: e.tensor_scalar(out=t_r[:, 0:n], in0=t_r[:, 0:n], scalar1=PI_SAFE, scalar2=-PI_SAFE, op0=ALU.min, op1=ALU.max), writes=[Bwg[0], Bwg[1]])
                    if which == 0:
                        sch.op("act", lambda e, c0=c0, n=n: e.activation(out=sinS[:, c0:c0 + n], in_=t_r[:, 0:n], func=AF.Sin, scale=ropec[:, 1:2]), reads=[Bwg[0], Bwg[1], Bc], writes=[Btab])
                    else:
                        sch.op("act", lambda e, c0=c0, n=n: e.activation(out=cosT[:, c0:c0 + n], in_=t_r[:, 0:n], func=AF.Sin), reads=[Bwg[0], Bwg[1]], writes=[Btab])

        def stage0(goff):
            for tb in range(NBo):
                i = cnt["blk"] % 2
                cnt["blk"] += 1
                r0 = goff + tb * 128
                sch.dma("sp", f"xt{i}", lambda e, i=i, r0=r0: e.dma_start(out=xt[i][:], in_=xp[r0:r0 + 128, :]), writes=[Bxt[i]])
                sch.op("act", lambda e, i=i: e.activation(out=junk[:], in_=xt[i][:], func=AF.Square, accum_out=ssq[:, i:i + 1]), reads=[Bxt[i]], writes=[Bjunk, Bc])
                sch.op("act", lambda e, i=i: e.activation(out=lnc[:, i:i + 1], in_=ssq[:, i:i + 1], func=AF.Ln, scale=1.0 / D, bias=epsc[:, 0:1]), reads=[Bc], writes=[Bc])
                sch.op("act", lambda e, i=i: e.activation(out=rsc[:, i:i + 1], in_=lnc[:, i:i + 1], func=AF.Exp, scale=-0.5), reads=[Bc], writes=[Bc])
                sch.op("dve", lambda e, i=i: e.scalar_tensor_tensor(out=xb[i][:], in0=xt[i][:], scalar=rsc[:, i:i + 1], in1=gpre_bc[:], op0=ALU.mult, op1=ALU.mult), reads=[Bxt[i], Bc], writes=[Bxb[i]])
                for hb in range(2):
                    bank = 4 + 2 * i + hb
                    pv = ps[bank][:].bitcast(BF16).rearrange("p (c t) -> p c t", c=8)

                    def tr(e, i=i, hb=hb, pv=pv):
                        for c in range(8):
                            inst = e.transpose(pv[:, c, :], xb[i][:, (hb * 8 + c) * 128:(hb * 8 + c + 1) * 128], identb[:])
                        return inst
                    sch.op("pe", tr, reads=[Bxb[i], Bc], writes=[psB[bank]])
                    dst = hT[:, hb * 8:hb * 8 + 8, tb * 128:(tb + 1) * 128]
                    if hb == 0:
                        sch.op("act", lambda e, dst=dst, pv=pv: e.activation(out=dst, in_=pv[:, :, :], func=AF.Copy), reads=[psB[bank]], writes=[BhT[tb // 4]])
                    else:
                        sch.op("dve", lambda e, dst=dst, pv=pv: e.tensor_copy(out=dst, in_=pv[:, :, :]), reads=[psB[bank]], writes=[BhT[tb // 4]])

        def load_group(slot, pieces):
            for (d0, s0, n) in pieces:
                sch.dma("pool", f"wg{slot}", lambda e, d0=d0, s0=s0, n=n: e.dma_start(out=wg[slot][:, :, d0:d0 + n], in_=w_in_v[:, :, s0:s0 + n]), writes=[Bwg[slot]])

        def mm16(bank, lhs_fn, rhs_fn, reads, M=128, N=512):
            def f(e):
                for k in range(16):
                    inst = e.matmul(ps[bank][0:M, 0:N], lhsT=lhs_fn(k), rhs=rhs_fn(k), start=(k == 0), stop=(k == 15))
                return inst
            sch.op("pe", f, reads=reads, writes=[psB[bank]])

        def store_bf(bank, dst_ap, dbuf, scale=None, eng="dve"):
            i = cnt["stg"] % NSTG
            cnt["stg"] += 1
            if scale is not None:
                sch.op("dve", lambda e: e.tensor_scalar(out=stg[i][:], in0=ps[bank][:], scalar1=scale, scalar2=None, op0=ALU.mult), reads=[psB[bank]], writes=[Bstg[i]])
            elif eng == "dve":
                sch.op("dve", lambda e: e.tensor_copy(out=stg[i][:], in_=ps[bank][:]), reads=[psB[bank]], writes=[Bstg[i]])
            else:
                sch.op("act", lambda e: e.activation(out=stg[i][:], in_=ps[bank][:], func=AF.Copy), reads=[psB[bank]], writes=[Bstg[i]])
            sch.dma("pool", f"stg{i}", lambda e: e.dma_start(out=dst_ap, in_=stg[i][:]), reads=[Bstg[i]], writes=[dbuf])

        def store_gate(bank, dst_ap, dbuf):
            i = cnt["stf"] % NSTF
            cnt["stf"] += 1
            sch.op("act", lambda e: e.activation(out=stf[i][:], in_=ps[bank][:], func=AF.Silu), reads=[psB[bank]], writes=[Bstf[i]])
            sch.dma("pool", f"stf{i}", lambda e: e.dma_start(out=dst_ap, in_=stf[i][:]), reads=[Bstf[i]], writes=[dbuf])

        def rms_feature_major(nchunks, nfeat, gcols, dst_fn, dbufs_fn, tile):
            for c in range(nchunks):
                sch.op("act", lambda e, c=c: e.activation(out=sq[:, c, :], in_=ps[c][:], func=AF.Square), reads=[psB[c]], writes=[Bsq])

            def f(e):
                for c in range(nchunks):
                    inst = e.matmul(ps[4][:], lhsT=onesb[:], rhs=sq[:, c, :], start=(c == 0), stop=(c == nchunks - 1))
                return inst
            sch.op("pe", f, reads=[Bsq, Bc], writes=[psB[4]])
            sch.op("act", lambda e: e.activation(out=lnt[:], in_=ps[4][:], func=AF.Ln, scale=1.0 / nfeat, bias=epsc[:, 0:1]), reads=[psB[4], Bc], writes=[Brs])
            sch.op("act", lambda e: e.activation(out=rst[:], in_=lnt[:], func=AF.Exp, scale=-0.5), reads=[Brs], writes=[Brs])
            for c in range(nchunks):
                sch.op("dve", lambda e, c=c: e.scalar_tensor_tensor(out=dst_fn(c), in0=ps[c][:], scalar=gcols[:, c:c + 1], in1=rst[:], op0=ALU.mult, op1=ALU.mult),
                       reads=[psB[c], Brs, Bc], writes=dbufs_fn())

        def rope_combine(bankA, bankB, tcols, dst_ap, dbufs):
            sch.op("dve", lambda e: e.tensor_tensor(out=rt1[0:64, :], in0=ps[bankA][0:64, :], in1=cosT[0:64, tcols], op=ALU.mult), reads=[psB[bankA], Btab], writes=[Brt])
            sch.op("dve", lambda e: e.tensor_tensor(out=rt2[0:64, :], in0=ps[bankB][0:64, :], in1=sinS[0:64, tcols], op=ALU.mult), reads=[psB[bankB], Btab], writes=[Brt])
            sch.op("dve", lambda e: e.tensor_tensor(out=dst_ap, in0=rt1[0:64, :], in1=rt2[0:64, :], op=ALU.add), reads=[Brt], writes=dbufs)

        def stage1_groups(goff, own):
            groups = []
            pb = [0]

            def g_qlat(s):
                for t in range(NT):
                    tc = slice(t * 512, (t + 1) * 512)
                    for c in range(4):
                        mm16(c, lambda k, c=c, s=s: wg[s][:, k, c * 128:(c + 1) * 128], lambda k, tc=tc: hT[:, k, tc], [Bwg[s], BhT[t]])
                    rms_feature_major(4, 512, gqa, lambda c, tc=tc: cqT[:, c, tc], lambda t=t: [Bcq[t]], t)
            if own:
                groups.append(([(0, 0, 512)], g_qlat))

            def g_kvk(s):
                for t in range(NT):
                    tc = slice(t * 512, (t + 1) * 512)
                    gt_ = (goff // 512) + t
                    gc = slice(goff + t * 512, goff + (t + 1) * 512)
                    for c in range(2):
                        mm16(c, lambda k, c=c, s=s: wg[s][:, k, c * 128:(c + 1) * 128], lambda k, tc=tc: hT[:, k, tc], [Bwg[s], BhT[t]])
                    mm16(2, lambda k, s=s: wg[s][:, k, 256:320], lambda k, tc=tc: hT[:, k, tc], [Bwg[s], BhT[t]], M=64)
                    mm16(3, lambda k, s=s: wg[s][:, k, 320:384], lambda k, tc=tc: hT[:, k, tc], [Bwg[s], BhT[t]], M=64)
                    rms_feature_major(2, 256, gkva, lambda c, gc=gc: ckvT[:, c, gc], lambda gt_=gt_: [Bckv[gt_]], t)
                    rope_combine(2, 3, tc, krT[0:64, gc], [Bkr[gt_]])
            groups.append(([(0, 512, 320), (320, 800, 32), (352, 768, 32)], g_kvk))

            fm = []
            if own:
                fm += [("gate", 832, 0), ("gate", 832 + 512, 4), ("dq", 1856, 0), ("dq", 1856 + 512, 4)]
            fm += [("dk", 2880, 0), ("dk", 2880 + 512, 4)]
            if own:
                fm += [("gate", 4928, 8), ("gate", 4928 + 512, 12)]

            def mk_fm(kind, col0, hc0):
                def g_fm(s):
                    for t in range(NT):
                        tc = slice(t * 512, (t + 1) * 512)
                        gc = slice(goff + t * 512, goff + (t + 1) * 512)
                        base = (pb[0] % 2) * 4
                        pb[0] += 1
                        for c in range(4):
                            mm16(base + c, lambda k, c=c, s=s: wg[s][:, k, c * 128:(c + 1) * 128], lambda k, tc=tc: hT[:, k, tc], [Bwg[s], BhT[t]])
                        for c in range(4):
                            hc = hc0 + c
                            if kind == "gate":
                                store_gate(base + c, gate_scr[hc, :, tc], Bgate[hc])
                            elif kind == "dq":
                                store_bf(base + c, dqT[hc, :, tc], Bdq[hc], scale=0.125)
                            else:
                                store_bf(base + c, dkT[hc, :, gc], Bdk[hc], eng=("dve" if c % 2 == 0 else "act"))
                return g_fm
            for (kind, col0, hc0) in fm:
                groups.append(([(0, col0, 512)], mk_fm(kind, col0, hc0)))

            def mk_dv(g):
                def g_dv(s):
                    for tb in range(NBo):
                        bank = pb[0] % 8
                        pb[0] += 1
                        tcb = slice(tb * 128, (tb + 1) * 128)
                        mm16(bank, lambda k, tcb=tcb: hT[:, k, tcb], lambda k, s=s: wg[s][:, k, 0:512], [Bwg[s], BhT[tb // 4]])
                        nbg = goff // 128 + tb
                        store_bf(bank, dvs[nbg, :, g * 512:(g + 1) * 512], Bdv[g], eng=("dve" if tb % 2 == 0 else "act"))
                return g_dv
            for g in range(2):
                groups.append(([(0, 3904 + g * 512, 512)], mk_dv(g)))
            return groups

        gslot = [0]
        for (goff, own) in ((Sh, False), (0, True)):
            rope_tables(goff)
            groups = stage1_groups(goff, own)
            slots = []
            for gi in range(len(groups)):
                slots.append(gslot[0] % 2)
                gslot[0] += 1
            load_group(slots[0], groups[0][0])
            stage0(goff)
            for gi, (pieces, comp) in enumerate(groups):
                if gi + 1 < len(groups):
                    load_group(slots[gi + 1], groups[gi + 1][0])
                comp(slots[gi])

        sch.barrier()

        def build_steps():
            tiles = []
            for j in range(NQ):
                st = []
                for kb in range(4 * j):
                    st.append((kb, 0, None))
                    st.append((NBo + kb, 0, None))
                for i in range(4):
                    st.append((4 * j + i, i, 0))
                    st.append((NBo + 4 * j + i, i, 1 + i))
                tiles.append(st)
            return tiles
        steps_by_tile = build_steps()
        if mla_heads < 8 or diff_heads < 8:
            for hh in range(16):
                sch.op("pool", lambda e, hh=hh: e.memset(mixT[:, hh, :], 0.0), writes=[Bmix[hh]])

        def col_groups(c0, gw):
            res = []
            b = c0
            while b < 4:
                e_ = min(4, (b // gw + 1) * gw)
                res.append((b, e_))
                b = e_
            return res

        def attention(nbr, qk_fn, v_fn, exp_args_fn, gw, STpool, Obank_fn, Lbank_fn, PT, BPT, fin_a, fin_b, reads_qk, reads_v, la, reads_exp=(), defer=6):
            flat = [(j, t, stp) for j, st in enumerate(steps_by_tile) for t, stp in enumerate(st)]
            units = [(idx, br) for idx in range(len(flat)) for br in range(nbr)]
            NU = len(units)
            stc = [0]
            bank_of = {}
            pending = []

            def take_bank():
                bnk = STpool[stc[0] % len(STpool)]
                stc[0] += 1
                return bnk

            def emit_qk(u):
                idx, br = units[u]
                j, t, (kb, c0, mk) = flat[idx]
                bank = take_bank()
                bank_of[u] = bank
                sch.op("pe", lambda e: qk_fn(e, br, ps[bank], kb, j, c0), reads=reads_qk, writes=[psB[bank]])

            def emit_exp(u):
                idx, br = units[u]
                j, t, (kb, c0, mk) = flat[idx]
                bank = bank_of[u]
                pi = u % len(PT)
                for (b0, b1) in col_groups(c0, gw):
                    cs = slice(b0 * 128, b1 * 128)
                    kw = exp_args_fn(kb, (4 * j + b0) // gw)
                    sch.op("act", lambda e, cs=cs, kw=kw: e.activation(out=PT[pi][:, cs], in_=ps[bank][:, cs], func=AF.Exp, **kw),
                           reads=[psB[bank], Bc] + list(reads_exp), writes=[BPT[pi]])
                if mk is not None:
                    cs = slice(c0 * 128, (c0 + 1) * 128)
                    sch.op("dve", lambda e, cs=cs: e.tensor_tensor(out=PT[pi][:, cs], in0=PT[pi][:, cs], in1=masksb[:, mk, :], op=ALU.mult),
                           reads=[Bc], writes=[BPT[pi]])

            def emit_pv(u):
                idx, br = units[u]
                j, t, (kb, c0, mk) = flat[idx]
                first = (t == 0)
                last = (t == len(steps_by_tile[j]) - 1)
                cs = slice(c0 * 128, 512)
                pi = u % len(PT)
                ob = Obank_fn(br, j)
                lbk = Lbank_fn(br, j)

                def f(e):
                    e.matmul(ps[ob][:, cs], lhsT=v_fn(br, kb), rhs=PT[pi][:, cs], start=first, stop=last, skip_group_check=True)
                    return e.matmul(ps[lbk][:, cs], lhsT=onesb[:], rhs=PT[pi][:, cs], start=first, stop=last, skip_group_check=True)
                sch.op("pe", f, reads=[BPT[pi], Bc] + reads_v, writes=[psB[ob], psB[lbk]])
                if last and br == nbr - 1:
                    fin_a(j)
                    if fin_b is not None:
                        pending.append((u + defer, j))

            for u in range(min(la, NU)):
                emit_qk(u)
            for u in range(NU):
                emit_exp(u)
                if u + la < NU:
                    emit_qk(u + la)
                emit_pv(u)
                while pending and pending[0][0] <= u:
                    fin_b(pending.pop(0)[1], bank_of[u])
            while pending:
                fin_b(pending.pop(0)[1], bank_of[NU - 1])

        ab = Bump(ar, LAT_END, MIX_OFF)
        wq = [ab.alloc([128, 4, 256], BF16, f"wq{i}") for i in range(2)]
        wkv = [ab.alloc([128, 2, 256], BF16, f"wkv{i}") for i in range(2)]
        Bwq = [Buf("wq0"), Buf("wq1")]
        KnT = ab.alloc([128, S], BF16, "KnT")
        Vh = ab.alloc([128, NB, 128], BF16, "Vh")
        QnT = ab.alloc([128, Sh], BF16, "QnT")
        QrT = ab.alloc([128, Sh], BF16, "QrT")
        gtm = ab.alloc([128, Sh], F32, "gtm")
        BK, BV, BQ, Bg = Buf("K"), Buf("V"), Buf("Q"), Buf("g")
        PTm = [ab.alloc([128, 512], BF16, f"PTm{i}") for i in range(4)]
        BPTm = [Buf(f"PTm{i}") for i in range(4)]
        rL = [ab.alloc([128, 512], F32, f"rL{i}") for i in range(2)]
        tg = [ab.alloc([128, 512], F32, f"tg{i}") for i in range(2)]
        Bfin = [Buf("fin0"), Buf("fin1")]
        rt1m = ab.alloc([128, 512], F32, "rt1m")
        rt2m = ab.alloc([128, 512], F32, "rt2m")
        Brtm = Buf("rtm")
        sch.op("pool", lambda e: e.memset(QrT[64:128, :], 0.0), writes=[BQ])
        SCALE_MLA = 192.0 ** -0.5

        def load_mla_w(h):
            s = h % 2
            sch.dma("pool", f"wq{s}", lambda e: e.dma_start(out=wq[s][:, :, 0:192], in_=w_qb_v[:, :, h * 192:h * 192 + 192]), writes=[Bwq[s]])
            sch.dma("pool", f"wq{s}", lambda e: e.dma_start(out=wq[s][:, :, 192:224], in_=w_qb_v[:, :, h * 192 + 160:h * 192 + 192]), writes=[Bwq[s]])
            sch.dma("pool", f"wq{s}", lambda e: e.dma_start(out=wq[s][:, :, 224:256], in_=w_qb_v[:, :, h * 192 + 128:h * 192 + 160]), writes=[Bwq[s]])
            sch.dma("pool", f"wq{s}", lambda e: e.dma_start(out=wkv[s][:, :, :], in_=w_kvb_v[:, :, h * 256:(h + 1) * 256]), writes=[Bwq[s]])

        if mla_heads > 0:
            load_mla_w(0)
        for h in range(mla_heads):
            s = h % 2
            if h + 1 < mla_heads:
                load_mla_w(h + 1)
            sch.dma("sp", "gtm", lambda e, h=h: e.dma_start(out=gtm[:], in_=gate_scr[h, :, :]), reads=[Bgate[h]], writes=[Bg])
            prep_banks = [0, 1, 2, 7]
            pc = [0]

            def nb_():
                b = prep_banks[pc[0] % 4]
                pc[0] += 1
                return b
            for t in range(S // 512):
                bank = nb_()
                tc = slice(t * 512, (t + 1) * 512)

                def f(e, bank=bank, tc=tc, s=s):
                    for c in range(2):
                        inst = e.matmul(ps[bank][:], lhsT=wkv[s][:, c, 0:128], rhs=ckvT[:, c, tc], start=(c == 0), stop=(c == 1))
                    return inst
                sch.op("pe", f, reads=[Bwq[s], Bckv[t]], writes=[psB[bank]])
                if t % 2 == 0:
                    sch.op("dve", lambda e, bank=bank, tc=tc: e.tensor_copy(out=KnT[:, tc], in_=ps[bank][:]), reads=[psB[bank]], writes=[BK])
                else:
                    sch.op("act", lambda e, bank=bank, tc=tc: e.activation(out=KnT[:, tc], in_=ps[bank][:], func=AF.Copy), reads=[psB[bank]], writes=[BK])
            for q4 in range(NB // 4):
                bank = nb_()

                def f(e, bank=bank, q4=q4, s=s):
                    for jj in range(4):
                        blk = q4 * 4 + jj
                        for c in range(2):
                            inst = e.matmul(ps[bank][:, jj * 128:(jj + 1) * 128], lhsT=ckvT[:, c, blk * 128:(blk + 1) * 128], rhs=wkv[s][:, c, 128:256], start=(c == 0), stop=(c == 1))
                    return inst
                sch.op("pe", f, reads=[Bwq[s], Bckv[q4]], writes=[psB[bank]])
                dst = Vh[:, q4 * 4:q4 * 4 + 4, :]
                if q4 % 2 == 0:
                    sch.op("act", lambda e, bank=bank, dst=dst: e.activation(out=dst, in_=ps[bank][:].rearrange("p (a b) -> p a b", a=4), func=AF.Copy), reads=[psB[bank]], writes=[BV])
                else:
                    sch.op("dve", lambda e, bank=bank, dst=dst: e.tensor_copy(out=dst, in_=ps[bank][:].rearrange("p (a b) -> p a b", a=4)), reads=[psB[bank]], writes=[BV])
            for t in range(NT):
                tc = slice(t * 512, (t + 1) * 512)
                bank = nb_()

                def f(e, bank=bank, tc=tc, s=s):
                    for c in range(4):
                        inst = e.matmul(ps[bank][:], lhsT=wq[s][:, c, 0:128], rhs=cqT[:, c, tc], start=(c == 0), stop=(c == 3))
                    return inst
                sch.op("pe", f, reads=[Bwq[s], Bcq[t]], writes=[psB[bank]])
                sch.op("dve", lambda e, bank=bank, tc=tc: e.tensor_copy(out=QnT[:, tc], in_=ps[bank][:]), reads=[psB[bank]], writes=[BQ])
                bA = nb_()
                bB = nb_()

                def f2(e, bA=bA, bB=bB, tc=tc, s=s):
                    for c in range(4):
                        e.matmul(ps[bA][0:64, :], lhsT=wq[s][:, c, 128:192], rhs=cqT[:, c, tc], start=(c == 0), stop=(c == 3))
                    for c in range(4):
                        inst = e.matmul(ps[bB][0:64, :], lhsT=wq[s][:, c, 192:256], rhs=cqT[:, c, tc], start=(c == 0), stop=(c == 3))
                    return inst
                sch.op("pe", f2, reads=[Bwq[s], Bcq[t]], writes=[psB[bA], psB[bB]])
                sch.op("dve", lambda e, bA=bA, tc=tc: e.tensor_tensor(out=rt1m[0:64, :], in0=ps[bA][0:64, :], in1=cosT[0:64, tc], op=ALU.mult), reads=[psB[bA], Btab], writes=[Brtm])
                sch.op("dve", lambda e, bB=bB, tc=tc: e.tensor_tensor(out=rt2m[0:64, :], in0=ps[bB][0:64, :], in1=sinS[0:64, tc], op=ALU.mult), reads=[psB[bB], Btab], writes=[Brtm])
                sch.op("dve", lambda e, tc=tc: e.tensor_tensor(out=QrT[0:64, tc], in0=rt1m[0:64, :], in1=rt2m[0:64, :], op=ALU.add), reads=[Brtm], writes=[BQ])

            def qk_mla(e, br, pst, kb, j, c0):
                ks = slice(kb * 128, (kb + 1) * 128)
                qs = slice(j * 512 + c0 * 128, (j + 1) * 512)
                os_ = slice(c0 * 128, 512)
                e.matmul(pst[:, os_], lhsT=KnT[:, ks], rhs=QnT[:, qs], start=True, stop=False)
                return e.matmul(pst[:, os_], lhsT=krT[:, ks], rhs=QrT[:, qs], start=False, stop=True)

            def fin_mla(j, h=h):
                i = j % 2
                ob, lbk = 3 + i, 5 + i
                tc = slice(j * 512, (j + 1) * 512)
                sch.op("dve", lambda e: e.reciprocal(out=rL[i][:], in_=ps[lbk][:]), reads=[psB[lbk]], writes=[Bfin[i]])
                sch.op("dve", lambda e: e.tensor_tensor(out=tg[i][:], in0=rL[i][:], in1=gtm[:, tc], op=ALU.mult), reads=[Bg], writes=[Bfin[i]])
                sch.op("dve", lambda e: e.tensor_tensor(out=mixT[:, h, tc], in0=ps[ob][:], in1=tg[i][:], op=ALU.mult), reads=[psB[ob], Bfin[i]], writes=[Bmix[h]])

            attention(1, qk_mla, lambda br, kb: Vh[:, kb, :], lambda kb, g: dict(scale=SCALE_MLA), 4,
                      [0, 1, 2], lambda br, j: 3 + (j % 2), lambda br, j: 5 + (j % 2), PTm, BPTm, fin_mla, None,
                      [BK, BQ, Bkr[0]] + Bkr[1:], [BV], 2)

        sch.barrier()

        db = Bump(ar, CONST_END, MIX_OFF)
        wout = db.alloc([128, 16, D], BF16, "wout")
        Bwout = Buf("wout")
        fin_mark = db.p
        K12 = [db.alloc([128, S], BF16, f"K12_{i}") for i in range(2)]
        Q12 = [db.alloc([128, Sh], BF16, f"Q12_{i}") for i in range(2)]
        Vd = [db.alloc([128, NB, 128], BF16, f"Vd{i}") for i in range(2)]
        gtd = db.alloc([128, Sh], F32, "gtd")
        Bgtd = Buf("gtd")
        NGmax = NBo
        bt_single = db.alloc([128, NB, NGmax], F32, "bt")
        bt = [bt_single, bt_single]
        Bhd = [Buf("hd0"), Buf("hd1")]
        Bbt_single = Buf("bt")
        Bbt = [Bbt_single, Bbt_single]
        PTd = [db.alloc([128, 512], BF16, f"PTd{i}") for i in range(5)]
        BPTd = [Buf(f"PTd{i}") for i in range(5)]
        fA = [db.alloc([128, 512], F32, f"fA{i}") for i in range(2)]
        fB = [db.alloc([128, 512], F32, f"fB{i}") for i in range(2)]
        fC = [db.alloc([128, 512], F32, f"fC{i}") for i in range(2)]
        fD = [db.alloc([128, 512], F32, f"fD{i}") for i in range(2)]
        sqd = [db.alloc([128, 512], BF16, f"sqd{i}") for i in range(2)]
        Bfs = [Buf("fs0"), Buf("fs1")]

        def load_diff(h):
            s = h % 2
            sch.dma("sp", f"hd{s}", lambda e: e.dma_start(out=K12[s][:], in_=dkT[h, :, :]), reads=[Bdk[h]], writes=[Bhd[s]])
            sch.dma("sp", f"hd{s}", lambda e: e.dma_start(out=Q12[s][:], in_=dqT[h, :, :]), reads=[Bdq[h]], writes=[Bhd[s]])
            sch.dma("sp", f"hd{s}", lambda e: e.dma_start(out=Vd[s][:], in_=dvs[:, :, h * 128:(h + 1) * 128].rearrange("nb p d -> p nb d")), reads=[Bdv[h // 4]], writes=[Bhd[s]])

        def load_gtd(h):
            sch.dma("sp", "gtd", lambda e: e.dma_start(out=gtd[:], in_=gate_scr[8 + h, :, :]), reads=[Bgate[8 + h]], writes=[Bgtd])

        if diff_heads > 0:
            load_diff(0)
            load_gtd(0)
        for h in range(diff_heads):
            s = h % 2
            gw = gws[h]
            NG = NBo // gw
            slope = 2.0 ** (-(h + 1))
            if h + 1 < diff_heads:
                load_diff(h + 1)
            if h == 0:
                for g in range(4):
                    sch.dma("pool", "wout", lambda e, g=g: e.dma_start(out=wout[:, :, g * 512:(g + 1) * 512], in_=w_out_v[:, :, g * 512:(g + 1) * 512]), writes=[Bwout])
            btv = bt[s][:, :, 0:NG]
            sch.op("dve", lambda e, btv=btv, NG=NG, gw=gw: e.tensor_tensor(out=btv, in0=poskf[:].unsqueeze(2).broadcast_to([128, NB, NG]), in1=pmids[gw][:].unsqueeze(1).broadcast_to([128, NB, NG]), op=ALU.subtract),
                   reads=[Bc], writes=[Bbt[s]])
            sch.op("dve", lambda e, btv=btv, slope=slope: e.tensor_scalar(out=btv, in0=btv, scalar1=slope, scalar2=40.0, op0=ALU.mult, op1=ALU.min), writes=[Bbt[s]])

            def qk_diff(e, br, pst, kb, j, c0, s=s):
                ks = slice(kb * 128, (kb + 1) * 128)
                qs = slice(j * 512 + c0 * 128, (j + 1) * 512)
                os_ = slice(c0 * 128, 512)
                pr = slice(64 * br, 64 * br + 64)
                return e.matmul(pst[:, os_], lhsT=K12[s][pr, ks], rhs=Q12[s][pr, qs], start=True, stop=True)

            def fin_a(j, h=h, s=s):
                k = j % 2
                A, Bt, C, Dt = fA[k], fB[k], fC[k], fD[k]
                sch.op("dve", lambda e: e.tensor_copy(out=A[:], in_=ps[4][:]), reads=[psB[4]], writes=[Bfs[k]])
                sch.op("act", lambda e: e.activation(out=Bt[:], in_=ps[5][:], func=AF.Copy), reads=[psB[5]], writes=[Bfs[k]])
                sch.op("dve", lambda e: e.tensor_copy(out=C[:], in_=ps[6][:]), reads=[psB[6]], writes=[Bfs[k]])
                sch.op("act", lambda e: e.activation(out=Dt[:], in_=ps[7][:], func=AF.Copy), reads=[psB[7]], writes=[Bfs[k]])
                sch.op("dve", lambda e: e.reciprocal(out=C[:], in_=C[:]), writes=[Bfs[k]])
                sch.op("dve", lambda e: e.reciprocal(out=Dt[:], in_=Dt[:]), writes=[Bfs[k]])
                sch.op("dve", lambda e: e.tensor_tensor(out=A[:], in0=A[:], in1=C[:], op=ALU.mult), writes=[Bfs[k]])
                sch.op("dve", lambda e: e.tensor_tensor(out=Bt[:], in0=Bt[:], in1=Dt[:], op=ALU.mult), writes=[Bfs[k]])
                sch.op("dve", lambda e: e.scalar_tensor_tensor(out=A[:], in0=Bt[:], scalar=negl[:, 0:1], in1=A[:], op0=ALU.mult, op1=ALU.add), reads=[Bc], writes=[Bfs[k]])
                sch.op("act", lambda e: e.activation(out=sqd[k][:], in_=A[:], func=AF.Square), writes=[Bfs[k]])

            def fin_b(j, bank, h=h, s=s):
                k = j % 2
                A, Dt = fA[k], fD[k]
                tc = slice(j * 512, (j + 1) * 512)
                sch.op("pe", lambda e: e.matmul(ps[bank][:], lhsT=onesb[:], rhs=sqd[k][:], start=True, stop=True), reads=[Bfs[k], Bc], writes=[psB[bank]])
                sch.op("act", lambda e: e.activation(out=Dt[:], in_=ps[bank][:], func=AF.Ln, scale=1.0 / 128.0, bias=epsc[:, 0:1]), reads=[psB[bank], Bc], writes=[Bfs[k]])
                sch.op("act", lambda e: e.activation(out=Dt[:], in_=Dt[:], func=AF.Exp, scale=-0.5), writes=[Bfs[k]])
                sch.op("dve", lambda e: e.scalar_tensor_tensor(out=A[:], in0=A[:], scalar=gsub8[:, 0:1], in1=Dt[:], op0=ALU.mult, op1=ALU.mult), reads=[Bc], writes=[Bfs[k]])
                sch.op("dve", lambda e: e.tensor_tensor(out=mixT[:, 8 + h, tc], in0=A[:], in1=gtd[:, tc], op=ALU.mult), reads=[Bgtd, Bfs[k]], writes=[Bmix[8 + h]])

            attention(2, qk_diff, lambda br, kb, s=s: Vd[s][:, kb, :], lambda kb, g, s=s: dict(bias=bt[s][:, kb, g:g + 1], scale=1.0), gw,
                      [0, 1, 2, 3], lambda br, j: 4 + br, lambda br, j: 6 + br, PTd, BPTd, fin_a, fin_b,
                      [Bhd[s]], [Bhd[s]], 3, reads_exp=[Bbt[s]], defer=8)
            if h + 1 < diff_heads:
                load_gtd(h + 1)

        sch.barrier()

        fb = Bump(ar, fin_mark, MIX_OFF)
        gpost_bc = fb.alloc([128, D], F32, "gpost_bc")
        xo = [fb.alloc([128, D], F32, f"xo{i}") for i in range(2)]
        yo = [fb.alloc([128, D], F32, f"yo{i}") for i in range(2)]
        junk2 = fb.alloc([128, 512], BF16, "junk2")
        Bxo = [Buf("xo0"), Buf("xo1")]
        Byo = [Buf("yo0"), Buf("yo1")]
        Bj2 = Buf("junk2")
        Bgp = Buf("gpost")
        ss4 = [fb.alloc([128, 4], F32, f"ss4_{i}") for i in range(2)]
        ss1 = [fb.alloc([128, 1], F32, f"ss1_{i}") for i in range(2)]
        ln1 = [fb.alloc([128, 1], F32, f"ln1_{i}") for i in range(2)]
        rs1 = [fb.alloc([128, 1], F32, f"rs1_{i}") for i in range(2)]
        Bst = [Buf("st0"), Buf("st1")]
        sch.dma("sp", "gpost", lambda e: e.dma_start(out=gpost_bc[:], in_=g_post.broadcast_to([128, D])), writes=[Bgp])
        for tb in range(NBo):
            i = tb % 2
            tcb = slice(tb * 128, (tb + 1) * 128)
            sch.dma("sp", f"xo{i}", lambda e, i=i, tb=tb: e.dma_start(out=xo[i][:], in_=xp[tb * 128:(tb + 1) * 128, :]), writes=[Bxo[i]])
            for g in range(4):
                bank = 4 * i + g

                def f(e, bank=bank, g=g, tcb=tcb):
                    for k in range(16):
                        inst = e.matmul(ps[bank][:], lhsT=mixT[:, k, tcb], rhs=wout[:, k, g * 512:(g + 1) * 512], start=(k == 0), stop=(k == 15))
                    return inst
                sch.op("pe", f, reads=Bmix + [Bwout], writes=[psB[bank]])
                sch.op("act", lambda e, bank=bank, g=g, i=i: e.activation(out=junk2[:], in_=ps[bank][:], func=AF.Square, accum_out=ss4[i][:, g:g + 1]), reads=[psB[bank]], writes=[Bj2, Bst[i]])
            sch.op("dve", lambda e, i=i: e.tensor_reduce(out=ss1[i][:], in_=ss4[i][:], axis=AX.X, op=ALU.add), reads=[Bst[i]], writes=[Bst[i]])
            sch.op("act", lambda e, i=i: e.activation(out=ln1[i][:], in_=ss1[i][:], func=AF.Ln, scale=1.0 / D, bias=epsc[:, 0:1]), reads=[Bst[i], Bc], writes=[Bst[i]])
            sch.op("act", lambda e, i=i: e.activation(out=rs1[i][:], in_=ln1[i][:], func=AF.Exp, scale=-0.5), reads=[Bst[i]], writes=[Bst[i]])
            for g in range(4):
                bank = 4 * i + g
                gs = slice(g * 512, (g + 1) * 512)
                sch.op("dve", lambda e, bank=bank, gs=gs, i=i: e.scalar_tensor_tensor(out=yo[i][:, gs], in0=ps[bank][:], scalar=rs1[i][:, 0:1], in1=gpost_bc[:, gs], op0=ALU.mult, op1=ALU.mult),
                       reads=[psB[bank], Bst[i], Bgp], writes=[Byo[i]])
            sch.op("dve", lambda e, i=i: e.tensor_tensor(out=yo[i][:], in0=yo[i][:], in1=xo[i][:], op=ALU.add), reads=[Bxo[i]], writes=[Byo[i]])
            toks_final.append(sch.dma("sp", f"yo{i}", lambda e, i=i, tb=tb: e.dma_start(out=out[tb * 128:(tb + 1) * 128, :], in_=yo[i][:]), reads=[Byo[i]]))

        sch.wait_all("sp", toks_final)
        sch.emit()
    return nc


def own_blocks(NB, parity):
    own = [g for g in range(NB) if ((g % 4) in (0, 3)) == (parity == 0)]
    other = [g for g in range(NB) if g not in own]
    return own, other


def make_core_inputs(S, parity, xb, posb, shared):
    NB = S // 128
    own, other = own_blocks(NB, parity)
    order = own + other
    tok = np.concatenate([np.arange(g * 128, (g + 1) * 128) for g in order])
    xp = np.ascontiguousarray(xb[tok])
    pp = np.ascontiguousarray(posb[tok]).astype(np.int32)
    masks = np.zeros((5, 128, 128), np.float32)
    k = np.arange(128)[:, None]
    q = np.arange(128)[None, :]
    masks[0] = (q >= k).astype(np.float32)
    flags = [0, 1, 0, 1] if parity == 0 else [1, 0, 1, 0]
    for i in range(4):
        masks[1 + i] = float(flags[i])
    d = dict(shared)
    d.update({
        "xp": xp,
        "posrow": pp.reshape(1, S),
        "postm": np.ascontiguousarray(pp.reshape(NB, 128).T),
        "posmid": np.ascontiguousarray(pp.reshape(NB, 128)[:NB // 2, 64]).reshape(1, NB // 2),
        "masks": masks,
    })
    return d, tok[:S // 2]


def make_shared(g_pre, w_in, g_q_a, w_q_b, g_kv_a, w_kv_b, lambda_q1, lambda_k1, lambda_q2, lambda_k2,
                g_diff_sub, w_out, g_post):
    f = np.float32
    freq = (1.0 / (np.float32(10000.0) ** (np.arange(0, 64, 2, dtype=np.float32) / np.float32(64)))).astype(f)
    ropec = np.zeros((128, 2), f)
    ropec[0:32, 0] = freq
    ropec[32:64, 0] = freq
    ropec[0:32, 1] = -1.0
    ropec[32:64, 1] = 1.0
    return {
        "w_in": np.ascontiguousarray(w_in[0], dtype=f),
        "w_qb": np.ascontiguousarray(w_q_b[0].reshape(512, 1536), dtype=f),
        "w_kvb": np.ascontiguousarray(w_kv_b[0].reshape(256, 2048), dtype=f),
        "w_out": np.ascontiguousarray(w_out[0], dtype=f),
        "g_pre": np.ascontiguousarray(g_pre[0].reshape(1, D), dtype=f),
        "gqa": np.ascontiguousarray(g_q_a[0].reshape(4, 128).T, dtype=f),
        "gkva": np.ascontiguousarray(g_kv_a[0].reshape(2, 128).T, dtype=f),
        "gsub": np.ascontiguousarray(g_diff_sub[0].reshape(1, 128).T, dtype=f),
        "g_post": np.ascontiguousarray(g_post[0].reshape(1, D), dtype=f),
        "lam": np.concatenate([lambda_q1[0], lambda_k1[0], lambda_q2[0], lambda_k2[0]]).reshape(1, 256).astype(f),
        "ident": np.eye(128, dtype=f),
        "ropec": ropec,
    }


_PROG_CACHE = {}


def run_layer(x, positions, params, **bkw):
    B, S, _ = x.shape
    shared = make_shared(**params)
    key = (S, tuple(sorted(bkw.items())))
    if key not in _PROG_CACHE:
        _PROG_CACHE[key] = build_program(S, **bkw)
    nc = _PROG_CACHE[key]
    in_maps, toks = [], []
    for b in range(B):
        for par in range(2):
            d, tok = make_core_inputs(S, par, x[b], positions[b], shared)
            in_maps.append(d)
            toks.append((b, tok))
    ncores = len(in_maps)
    res = run_bass_kernel_spmd(nc, in_maps, core_ids=list(range(ncores)))
    outp = np.empty((B, S, D), np.float32)
    for ci, (b, tok) in enumerate(toks):
        outp[b, tok] = res.results[ci]["out"]
    return outp


def kernel(x, positions, g_pre, w_in, g_q_a, w_q_b, g_kv_a, w_kv_b, lambda_q1, lambda_k1, lambda_q2, lambda_k2,
           g_diff_sub, w_out, g_post):
    params = dict(g_pre=np.asarray(g_pre), w_in=np.asarray(w_in), g_q_a=np.asarray(g_q_a), w_q_b=np.asarray(w_q_b),
                  g_kv_a=np.asarray(g_kv_a), w_kv_b=np.asarray(w_kv_b), lambda_q1=np.asarray(lambda_q1),
                  lambda_k1=np.asarray(lambda_k1), lambda_q2=np.asarray(lambda_q2), lambda_k2=np.asarray(lambda_k2),
                  g_diff_sub=np.asarray(g_diff_sub), w_out=np.asarray(w_out), g_post=np.asarray(g_post))
    return run_layer(np.asarray(x, dtype=np.float32), np.asarray(positions), params)
```

```python
import contextlib
import math
import numpy as np
import concourse.bass as bass
import concourse.mybir as mybir
from concourse.bass_utils import run_bass_kernel_spmd

F32 = mybir.dt.float32
BF16 = mybir.dt.bfloat16
I32 = mybir.dt.int32
AF = mybir.ActivationFunctionType
ALU = mybir.AluOpType
AX = mybir.AxisListType

D = 2048
INW = 5952
EPS = 1e-6
ENGS = ("pe", "act", "dve", "pool", "sp")
LAM_INIT = 0.8 - 0.6 * math.exp(-0.3 * 0)
TWO_PI = 2.0 * math.pi
C1 = 6.28125
C2 = float(np.float32(TWO_PI - C1))
PI_SAFE = 3.1415925


class Buf:
    __slots__ = ("name", "w", "r")

    def __init__(self, name=""):
        self.name = name
        self.w = None
        self.r = {}


class Sched:
    def __init__(self, nc, es):
        self.nc = nc
        self.es = es
        self.streams = {e: [] for e in ENGS}
        self.count = {e: 0 for e in ENGS}
        self.seen = {e: {} for e in ENGS}
        self.sem = {}
        self.dcount = {}
        for e in ENGS:
            self.sem[e] = es.enter_context(nc.semaphore("sem_" + e))

    def _dsem(self, key):
        if key not in self.sem:
            self.sem[key] = self.es.enter_context(self.nc.semaphore("dsem_" + key))
            self.dcount[key] = 0

    def _waits(self, eng, reads, writes):
        deps = {}

        def add(k, v):
            if deps.get(k, 0) < v:
                deps[k] = v
        for b in reads:
            if b.w is not None:
                add(*b.w)
        for b in writes:
            if b.w is not None:
                add(*b.w)
            for k, v in b.r.items():
                add(k, v)
        waits = []
        seen = self.seen[eng]
        for k, v in deps.items():
            if seen.get(k, 0) < v:
                seen[k] = v
                waits.append((k, v))
        return waits

    def _commit(self, tok, reads, writes):
        k, v = tok
        for b in reads:
            if b.r.get(k, 0) < v:
                b.r[k] = v
        for b in writes:
            b.w = tok
            b.r = {}

    def op(self, eng, fn, reads=(), writes=()):
        waits = self._waits(eng, reads, writes)
        self.count[eng] += 1
        tok = (eng, self.count[eng])
        self.streams[eng].append((waits, fn, (eng, 1)))
        self._commit(tok, reads, writes)
        return tok

    def dma(self, eng, key, fn, reads=(), writes=()):
        self._dsem(key)
        waits = self._waits(eng, reads, writes)
        self.dcount[key] += 16
        tok = (key, self.dcount[key])
        self.streams[eng].append((waits, fn, (key, 16)))
        self._commit(tok, reads, writes)
        return tok

    def wait_all(self, eng, toks):
        deps = {}
        for (k, v) in toks:
            if deps.get(k, 0) < v:
                deps[k] = v
        waits = [(k, v) for k, v in deps.items() if self.seen[eng].get(k, 0) < v]
        for k, v in waits:
            self.seen[eng][k] = v
        self.streams[eng].append((waits, None, None))

    def barrier(self):
        toks = [(e, self.count[e]) for e in ENGS if self.count[e] > 0]
        toks += [(k, v) for k, v in self.dcount.items() if v > 0]
        for e in ENGS:
            self.wait_all(e, toks)

    def emit(self):
        nc = self.nc
        with nc.Block() as block:
            def run(ename):
                def body(e):
                    for waits, fn, inc in self.streams[ename]:
                        for k, v in waits:
                            e.wait_ge(self.sem[k], v)
                        if fn is not None:
                            inst = fn(e)
                            inst.then_inc(self.sem[inc[0]], inc[1])
                return body
            block.tensor(run("pe"))
            block.scalar(run("act"))
            block.vector(run("dve"))
            block.gpsimd(run("pool"))
            block.sync(run("sp"))


class Arena:
    def __init__(self, nc):
        self.nc = nc
        self.base = ((nc.sbuf_base + 63) // 64) * 64
        self.top = nc.sbuf_top
        self.n = 0

    def at(self, off, shape, dtype, name=None):
        esz = 4 if dtype in (F32, I32) else 2
        n = esz
        for s in shape[1:]:
            n *= s
        assert self.base + off + n <= self.top, ("SBUF overflow", name, off, n)
        self.n += 1
        t = self.nc.alloc_sbuf_tensor_at(name or f"t{self.n}", list(shape), dtype, offset=self.base + off)
        return t, off + ((n + 63) // 64) * 64


class Bump:
    def __init__(self, arena, start, limit):
        self.a, self.p, self.limit = arena, start, limit

    def alloc(self, shape, dtype, name=None):
        t, self.p = self.a.at(self.p, shape, dtype, name)
        assert self.p <= self.limit, ("region overflow", name, self.p, self.limit)
        return t


def build_program(S, gws=(1, 1, 2, 4, 4, 4, 4, 4), mla_heads=8, diff_heads=8):
    NB = S // 128
    NBo = NB // 2
    Sh = S // 2
    NQ = Sh // 512
    NT = Sh // 512
    KB = 1024

    nc = bass.Bass("TRN2", target_bir_lowering=False)

    def din(name, shape, dt=F32):
        return nc.dram_tensor(name, list(shape), dt, kind="ExternalInput").ap()
    xp = din("xp", [S, D])
    posrow = din("posrow", [1, S], I32)
    postm = din("postm", [128, NB], I32)
    posmid = din("posmid", [1, NBo], I32)
    w_in = din("w_in", [D, INW])
    w_qb = din("w_qb", [512, 1536])
    w_kvb = din("w_kvb", [256, 2048])
    w_out = din("w_out", [D, D])
    g_pre = din("g_pre", [1, D])
    gqa_d = din("gqa", [128, 4])
    gkva_d = din("gkva", [128, 2])
    gsub_d = din("gsub", [128, 1])
    g_post = din("g_post", [1, D])
    lam_d = din("lam", [1, 256])
    ident_d = din("ident", [128, 128])
    masks_d = din("masks", [5, 128, 128])
    ropec_d = din("ropec", [128, 2])
    out = nc.dram_tensor("out", [Sh, D], F32, kind="ExternalOutput").ap()
    dkT = nc.dram_tensor("dkT_scr", [8, 128, S], BF16, kind="Internal").ap()
    dvs = nc.dram_tensor("dvs_scr", [NB, 128, 1024], BF16, kind="Internal").ap()
    dqT = nc.dram_tensor("dqT_scr", [8, 128, Sh], BF16, kind="Internal").ap()
    gate_scr = nc.dram_tensor("gate_scr", [16, 128, Sh], F32, kind="Internal").ap()

    w_in_v = w_in.rearrange("(c p) e -> p c e", p=128)
    w_qb_v = w_qb.rearrange("(c p) e -> p c e", p=128)
    w_kvb_v = w_kvb.rearrange("(c p) e -> p c e", p=128)
    w_out_v = w_out.rearrange("(c p) e -> p c e", p=128)

    with contextlib.ExitStack() as es:
        sch = Sched(nc, es)
        ar = Arena(nc)
        TOTAL = ar.top - ar.base
        ps = [es.enter_context(nc.psum_tensor(f"psb{i}", [128, 512], F32)) for i in range(8)]
        psB = [Buf(f"ps{i}") for i in range(8)]

        MIX_BYTES = 16 * Sh * 2
        CONST_END = 5120
        LAT_END = CONST_END + (2 * Sh * 4) + (2 * S * 2) + (S * 2) + (4 * Sh * 2) + 256
        MIX_OFF = ((TOTAL - MIX_BYTES) // 64) * 64
        assert LAT_END <= MIX_OFF

        cb = Bump(ar, 0, CONST_END)
        identb = cb.alloc([128, 128], BF16, "identb")
        onesb = cb.alloc([128, 128], BF16, "onesb")
        masksb = cb.alloc([128, 5, 128], BF16, "masksb")
        epsc = cb.alloc([128, 1], F32, "epsc")
        ropec = cb.alloc([128, 2], F32, "ropec")
        gqa = cb.alloc([128, 4], F32, "gqa")
        gkva = cb.alloc([128, 2], F32, "gkva")
        gsubs = cb.alloc([128, 1], F32, "gsubs")
        lamv = cb.alloc([128, 256], F32, "lamv")
        lamp = cb.alloc([128, 128], F32, "lamp")
        lams = cb.alloc([128, 2], F32, "lams")
        lame = cb.alloc([128, 2], F32, "lame")
        negl = cb.alloc([128, 1], F32, "negl")
        gsub8 = cb.alloc([128, 1], F32, "gsub8")
        poski = cb.alloc([128, NB], I32, "poski")
        poskf = cb.alloc([128, NB], F32, "poskf")
        pmidi = cb.alloc([128, NBo], I32, "pmidi")
        pmid1 = cb.alloc([128, NBo], F32, "pmid1")
        pmid2 = cb.alloc([128, NBo // 2], F32, "pmid2")
        pmid4 = cb.alloc([128, NBo // 4], F32, "pmid4")
        ssq = cb.alloc([128, 8], F32, "ssq")
        lnc = cb.alloc([128, 4], F32, "lnc")
        rsc = cb.alloc([128, 4], F32, "rsc")
        Bc = Buf("consts")

        lb = Bump(ar, CONST_END, LAT_END)
        cosT = lb.alloc([128, Sh], F32, "cosT")
        sinS = lb.alloc([128, Sh], F32, "sinS")
        ckvT = lb.alloc([128, 2, S], BF16, "ckvT")
        krT = lb.alloc([128, S], BF16, "krT")
        cqT = lb.alloc([128, 4, Sh], BF16, "cqT")
        Btab = Buf("tab")
        Bckv = [Buf(f"ckv{i}") for i in range(S // 512)]
        Bkr = [Buf(f"kr{i}") for i in range(S // 512)]
        Bcq = [Buf(f"cq{i}") for i in range(NT)]

        mixT, _ = ar.at(MIX_OFF, [128, 16, Sh], BF16, "mixT")
        Bmix = [Buf(f"mix{h}") for h in range(16)]

        toks_final = []

        sch.dma("pool", "c_id", lambda e: e.dma_start(out=identb[:], in_=ident_d[:, :]), writes=[Bc])
        sch.dma("pool", "c_mk", lambda e: e.dma_start(out=masksb[:], in_=masks_d.rearrange("m p q -> p m q")), writes=[Bc])
        sch.op("pool", lambda e: e.memset(onesb[:], 1.0), writes=[Bc])
        sch.op("pool", lambda e: e.memset(epsc[:], EPS), writes=[Bc])
        sch.op("pool", lambda e: e.memset(krT[64:128, :], 0.0), writes=[Bc])
        sch.dma("sp", "c_a", lambda e: e.dma_start(out=ropec[:], in_=ropec_d[:, :]), writes=[Bc])
        sch.dma("sp", "c_b", lambda e: e.dma_start(out=gqa[:], in_=gqa_d[:, :]), writes=[Bc])
        sch.dma("sp", "c_c", lambda e: e.dma_start(out=gkva[:], in_=gkva_d[:, :]), writes=[Bc])
        sch.dma("sp", "c_d", lambda e: e.dma_start(out=gsubs[:], in_=gsub_d[:, :]), writes=[Bc])
        sch.dma("sp", "c_e", lambda e: e.dma_start(out=lamv[:], in_=lam_d.broadcast_to([128, 256])), writes=[Bc])
        sch.dma("sp", "c_f", lambda e: e.dma_start(out=poski[:], in_=postm[:, :]), writes=[Bc])
        sch.dma("sp", "c_g", lambda e: e.dma_start(out=pmidi[:], in_=posmid.broadcast_to([128, NBo])), writes=[Bc])
        sch.op("dve", lambda e: e.tensor_tensor(out=lamp[:, 0:64], in0=lamv[:, 0:64], in1=lamv[:, 64:128], op=ALU.mult), reads=[Bc], writes=[Bc])
        sch.op("dve", lambda e: e.tensor_tensor(out=lamp[:, 64:128], in0=lamv[:, 128:192], in1=lamv[:, 192:256], op=ALU.mult), reads=[Bc], writes=[Bc])
        sch.op("dve", lambda e: e.tensor_reduce(out=lams[:], in_=lamp[:].rearrange("p (a b) -> p a b", a=2), axis=AX.X, op=ALU.add), reads=[Bc], writes=[Bc])
        sch.op("act", lambda e: e.activation(out=lame[:], in_=lams[:], func=AF.Exp), reads=[Bc], writes=[Bc])
        sch.op("dve", lambda e: e.tensor_tensor(out=negl[:], in0=lame[:, 1:2], in1=lame[:, 0:1], op=ALU.subtract), reads=[Bc], writes=[Bc])
        sch.op("dve", lambda e: e.tensor_scalar(out=negl[:], in0=negl[:], scalar1=-LAM_INIT, scalar2=None, op0=ALU.add), reads=[Bc], writes=[Bc])
        sch.op("dve", lambda e: e.tensor_scalar(out=gsub8[:], in0=gsubs[:], scalar1=(1.0 - LAM_INIT), scalar2=None, op0=ALU.mult), reads=[Bc], writes=[Bc])
        sch.op("dve", lambda e: e.tensor_copy(out=poskf[:], in_=poski[:]), reads=[Bc], writes=[Bc])
        sch.op("dve", lambda e: e.tensor_copy(out=pmid1[:], in_=pmidi[:]), reads=[Bc], writes=[Bc])
        sch.op("dve", lambda e: e.tensor_reduce(out=pmid2[:], in_=pmid1[:].rearrange("p (a b) -> p a b", b=2), axis=AX.X, op=ALU.add), reads=[Bc], writes=[Bc])
        sch.op("dve", lambda e: e.tensor_scalar(out=pmid2[:], in0=pmid2[:], scalar1=0.5, scalar2=None, op0=ALU.mult), reads=[Bc], writes=[Bc])
        sch.op("dve", lambda e: e.tensor_reduce(out=pmid4[:], in_=pmid1[:].rearrange("p (a b) -> p a b", b=4), axis=AX.X, op=ALU.add), reads=[Bc], writes=[Bc])
        sch.op("dve", lambda e: e.tensor_scalar(out=pmid4[:], in0=pmid4[:], scalar1=0.25, scalar2=None, op0=ALU.mult), reads=[Bc], writes=[Bc])
        pmids = {1: pmid1, 2: pmid2, 4: pmid4}

        s1 = Bump(ar, LAT_END, TOTAL)
        hT = s1.alloc([128, 16, Sh], BF16, "hT")
        BhT = [Buf(f"hT{i}") for i in range(NT)]
        wg = [s1.alloc([128, 16, 512], BF16, f"wg{i}") for i in range(2)]
        Bwg = [Buf("wg0"), Buf("wg1")]
        gpre_bc = s1.alloc([128, D], F32, "gpre_bc")
        xt = [s1.alloc([128, D], F32, f"xt{i}") for i in range(2)]
        wg1_off = LAT_END + 16 * Sh * 2 + 16 * 512 * 2
        xt2, _ = ar.at(wg1_off, [128, D], F32, "xt2")
        xt3, _ = ar.at(wg1_off + D * 4, [128, D], F32, "xt3")
        xt += [xt2, xt3]
        Bxt = [Buf("xt0"), Buf("xt1"), Buf("xt2"), Buf("xt3")]
        xb = [s1.alloc([128, D], BF16, f"xb{i}") for i in range(2)]
        Bxb = [Buf("xb0"), Buf("xb1")]
        NSTG, NSTF = 3, 2
        stg = [s1.alloc([128, 512], BF16, f"stg{i}") for i in range(NSTG)]
        Bstg = [Buf(f"stg{i}") for i in range(NSTG)]
        stf = [s1.alloc([128, 512], F32, f"stf{i}") for i in range(NSTF)]
        Bstf = [Buf(f"stf{i}") for i in range(NSTF)]
        sq = s1.alloc([128, 4, 512], BF16, "sq")
        Bsq = Buf("sq")
        junk, _ = ar.at(s1.p - 4 * 512 * 2, [128, D], BF16, "junk")
        Bjunk = Bsq
        rst = s1.alloc([128, 512], F32, "rst")
        lnt = rst
        Brs = Buf("rs")
        rt1 = s1.alloc([128, 512], F32, "rt1")
        rt2 = s1.alloc([128, 512], F32, "rt2")
        Brt = Buf("rt")
        tmpo = LAT_END + 16 * Sh * 2
        t_posi, o2 = ar.at(tmpo, [128, KB], I32, "t_posi")
        t_x, o2 = ar.at(o2, [128, KB], F32, "t_x")
        t_ki, o2 = ar.at(o2, [128, KB], I32, "t_ki")
        t_kf, o2 = ar.at(o2, [128, KB], F32, "t_kf")
        t_r, o2 = ar.at(o2, [128, KB], F32, "t_r")
        assert o2 <= tmpo + 2 * 16 * 512 * 2

        sch.dma("sp", "c_gp", lambda e: e.dma_start(out=gpre_bc[:], in_=g_pre.broadcast_to([128, D])), writes=[Bc])

        cnt = {"stg": 0, "stf": 0, "blk": 0}
        Bdk = [Buf(f"dk{i}") for i in range(8)]
        Bdq = [Buf(f"dq{i}") for i in range(8)]
        Bgate = [Buf(f"gate{i}") for i in range(16)]
        Bdv = [Buf(f"dv{i}") for i in range(2)]

        def rope_tables(goff):
            for c0 in range(0, Sh, KB):
                n = min(KB, Sh - c0)
                sch.dma("sp", "t_pos", lambda e, c0=c0, n=n: e.dma_start(out=t_posi[:, 0:n], in_=posrow[0:1, goff + c0:goff + c0 + n].broadcast_to([128, n])), writes=[Bwg[0], Bwg[1]])
                for which in (0, 1):
                    sch.op("dve", lambda e, n=n: e.tensor_copy(out=t_x[:, 0:n], in_=t_posi[:, 0:n]), writes=[Bwg[0], Bwg[1]])
                    if which == 0:
                        sch.op("dve", lambda e, n=n: e.tensor_scalar(out=t_x[:, 0:n], in0=t_x[:, 0:n], scalar1=ropec[:, 0:1], scalar2=None, op0=ALU.mult), reads=[Bc], writes=[Bwg[0], Bwg[1]])
                    else:
                        sch.op("dve", lambda e, n=n: e.tensor_scalar(out=t_x[:, 0:n], in0=t_x[:, 0:n], scalar1=ropec[:, 0:1], scalar2=math.pi / 2, op0=ALU.mult, op1=ALU.add), reads=[Bc], writes=[Bwg[0], Bwg[1]])
                    sch.op("dve", lambda e, n=n: e.tensor_scalar(out=t_kf[:, 0:n], in0=t_x[:, 0:n], scalar1=1.0 / TWO_PI, scalar2=None, op0=ALU.mult), writes=[Bwg[0], Bwg[1]])
                    sch.op("dve", lambda e, n=n: e.tensor_copy(out=t_ki[:, 0:n], in_=t_kf[:, 0:n]), writes=[Bwg[0], Bwg[1]])
                    sch.op("dve", lambda e, n=n: e.tensor_copy(out=t_kf[:, 0:n], in_=t_ki[:, 0:n]), writes=[Bwg[0], Bwg[1]])
                    sch.op("dve", lambda e, n=n: e.scalar_tensor_tensor(out=t_r[:, 0:n], in0=t_kf[:, 0:n], scalar=-C1, in1=t_x[:, 0:n], op0=ALU.mult, op1=ALU.add), writes=[Bwg[0], Bwg[1]])
                    sch.op("dve", lambda e, n=n: e.scalar_tensor_tensor(out=t_r[:, 0:n], in0=t_kf[:, 0:n], scalar=-C2, in1=t_r[:, 0:n], op0=ALU.mult, op1=ALU.add), writes=[Bwg[0], Bwg[1]])
                    sch.op("dve", lambda e, n=n: e.tensor_scalar(out=t_r[:, 0:n], in0=t_r[:, 0:n], scalar1=PI_SAFE, scalar2=-PI_SAFE, op0=ALU.min, op1=ALU.max), writes=[Bwg[0], Bwg[1]])
                    if which == 0:
                        sch.op("act", lambda e, c0=c0, n=n: e.activation(out=sinS[:, c0:c0 + n], in_=t_r[:, 0:n], func=AF.Sin, scale=ropec[:, 1:2]), reads=[Bwg[0], Bwg[1], Bc], writes=[Btab])
                    else:
                        sch.op("act", lambda e, c0=c0, n=n: e.activation(out=cosT[:, c0:c0 + n], in_=t_r[:, 0:n], func=AF.Sin), reads=[Bwg[0], Bwg[1]], writes=[Btab])

        def stage0(goff):
            for b_ in (Bxt[2], Bxt[3]):
                b_.w = Bwg[1].w
                b_.r = dict(Bwg[1].r)
            for tb in range(NBo):
                i = tb % 4
                i2 = tb % 2
                r0 = goff + tb * 128
                sch.dma("sp", f"xt{i}", lambda e, i=i, r0=r0: e.dma_start(out=xt[i][:], in_=xp[r0:r0 + 128, :]), writes=[Bxt[i]])
                sch.op("act", lambda e, i=i: e.activation(out=junk[:], in_=xt[i][:], func=AF.Square, accum_out=ssq[:, i:i + 1]), reads=[Bxt[i]], writes=[Bjunk, Bc])
                sch.op("act", lambda e, i=i: e.activation(out=lnc[:, i:i + 1], in_=ssq[:, i:i + 1], func=AF.Ln, scale=1.0 / D, bias=epsc[:, 0:1]), reads=[Bc], writes=[Bc])
                sch.op("act", lambda e, i=i: e.activation(out=rsc[:, i:i + 1], in_=lnc[:, i:i + 1], func=AF.Exp, scale=-0.5), reads=[Bc], writes=[Bc])
                sch.op("dve", lambda e, i=i, i2=i2: e.scalar_tensor_tensor(out=xb[i2][:], in0=xt[i][:], scalar=rsc[:, i:i + 1], in1=gpre_bc[:], op0=ALU.mult, op1=ALU.mult), reads=[Bxt[i], Bc], writes=[Bxb[i2]])
                for hb in range(2):
                    bank = 4 + 2 * i2 + hb
                    pv = ps[bank][:].bitcast(BF16).rearrange("p (c t) -> p c t", c=8)

                    def tr(e, i2=i2, hb=hb, pv=pv):
                        for c in range(8):
                            inst = e.transpose(pv[:, c, :], xb[i2][:, (hb * 8 + c) * 128:(hb * 8 + c + 1) * 128], identb[:])
                        return inst
                    sch.op("pe", tr, reads=[Bxb[i2], Bc], writes=[psB[bank]])
                    dst = hT[:, hb * 8:hb * 8 + 8, tb * 128:(tb + 1) * 128]
                    if hb == 0:
                        sch.op("act", lambda e, dst=dst, pv=pv: e.activation(out=dst, in_=pv[:, :, :], func=AF.Copy), reads=[psB[bank]], writes=[BhT[tb // 4]])
                    else:
                        sch.op("dve", lambda e, dst=dst, pv=pv: e.tensor_copy(out=dst, in_=pv[:, :, :]), reads=[psB[bank]], writes=[BhT[tb // 4]])
            for b_ in (Bxt[2], Bxt[3]):
                toks = list(b_.r.items()) + ([b_.w] if b_.w is not None else [])
                for k_, v_ in toks:
                    if Bwg[1].r.get(k_, 0) < v_:
                        Bwg[1].r[k_] = v_

        def load_group(slot, pieces):
            for (d0, s0, n) in pieces:
                sch.dma("pool", f"wg{slot}", lambda e, d0=d0, s0=s0, n=n: e.dma_start(out=wg[slot][:, :, d0:d0 + n], in_=w_in_v[:, :, s0:s0 + n]), writes=[Bwg[slot]])

        def mm16(bank, lhs_fn, rhs_fn, reads, M=128, N=512):
            def f(e):
                for k in range(16):
                    inst = e.matmul(ps[bank][0:M, 0:N], lhsT=lhs_fn(k), rhs=rhs_fn(k), start=(k == 0), stop=(k == 15))
                return inst
            sch.op("pe", f, reads=reads, writes=[psB[bank]])

        def store_bf(bank, dst_ap, dbuf, scale=None, eng="dve"):
            i = cnt["stg"] % NSTG
            cnt["stg"] += 1
            if scale is not None:
                sch.op("dve", lambda e: e.tensor_scalar(out=stg[i][:], in0=ps[bank][:], scalar1=scale, scalar2=None, op0=ALU.mult), reads=[psB[bank]], writes=[Bstg[i]])
            elif eng == "dve":
                sch.op("dve", lambda e: e.tensor_copy(out=stg[i][:], in_=ps[bank][:]), reads=[psB[bank]], writes=[Bstg[i]])
            else:
                sch.op("act", lambda e: e.activation(out=stg[i][:], in_=ps[bank][:], func=AF.Copy), reads=[psB[bank]], writes=[Bstg[i]])
            sch.dma("pool", f"stg{i}", lambda e: e.dma_start(out=dst_ap, in_=stg[i][:]), reads=[Bstg[i]], writes=[dbuf])

        def store_gate(bank, dst_ap, dbuf):
            i = cnt["stf"] % NSTF
            cnt["stf"] += 1
            sch.op("act", lambda e: e.activation(out=stf[i][:], in_=ps[bank][:], func=AF.Silu), reads=[psB[bank]], writes=[Bstf[i]])
            sch.dma("pool", f"stf{i}", lambda e: e.dma_start(out=dst_ap, in_=stf[i][:]), reads=[Bstf[i]], writes=[dbuf])

        def rms_feature_major(nchunks, nfeat, gcols, dst_fn, dbufs_fn, tile):
            for c in range(nchunks):
                sch.op("act", lambda e, c=c: e.activation(out=sq[:, c, :], in_=ps[c][:], func=AF.Square), reads=[psB[c]], writes=[Bsq])

            def f(e):
                for c in range(nchunks):
                    inst = e.matmul(ps[4][:], lhsT=onesb[:], rhs=sq[:, c, :], start=(c == 0), stop=(c == nchunks - 1))
                return inst
            sch.op("pe", f, reads=[Bsq, Bc], writes=[psB[4]])
            sch.op("act", lambda e: e.activation(out=lnt[:], in_=ps[4][:], func=AF.Ln, scale=1.0 / nfeat, bias=epsc[:, 0:1]), reads=[psB[4], Bc], writes=[Brs])
            sch.op("act", lambda e: e.activation(out=rst[:], in_=lnt[:], func=AF.Exp, scale=-0.5), reads=[Brs], writes=[Brs])
            for c in range(nchunks):
                sch.op("dve", lambda e, c=c: e.scalar_tensor_tensor(out=dst_fn(c), in0=ps[c][:], scalar=gcols[:, c:c + 1], in1=rst[:], op0=ALU.mult, op1=ALU.mult),
                       reads=[psB[c], Brs, Bc], writes=dbufs_fn())

        def rope_combine(bankA, bankB, tcols, dst_ap, dbufs):
            sch.op("dve", lambda e: e.tensor_tensor(out=rt1[0:64, :], in0=ps[bankA][0:64, :], in1=cosT[0:64, tcols], op=ALU.mult), reads=[psB[bankA], Btab], writes=[Brt])
            sch.op("dve", lambda e: e.tensor_tensor(out=rt2[0:64, :], in0=ps[bankB][0:64, :], in1=sinS[0:64, tcols], op=ALU.mult), reads=[psB[bankB], Btab], writes=[Brt])
            sch.op("dve", lambda e: e.tensor_tensor(out=dst_ap, in0=rt1[0:64, :], in1=rt2[0:64, :], op=ALU.add), reads=[Brt], writes=dbufs)

        def stage1_groups(goff, own):
            groups = []
            pb = [0]

            def g_qlat(s):
                for t in range(NT):
                    tc = slice(t * 512, (t + 1) * 512)
                    for c in range(4):
                        mm16(c, lambda k, c=c, s=s: wg[s][:, k, c * 128:(c + 1) * 128], lambda k, tc=tc: hT[:, k, tc], [Bwg[s], BhT[t]])
                    rms_feature_major(4, 512, gqa, lambda c, tc=tc: cqT[:, c, tc], lambda t=t: [Bcq[t]], t)
            if own:
                groups.append(([(0, 0, 512)], g_qlat))

            def g_kvk(s):
                for t in range(NT):
                    tc = slice(t * 512, (t + 1) * 512)
                    gt_ = (goff // 512) + t
                    gc = slice(goff + t * 512, goff + (t + 1) * 512)
                    for c in range(2):
                        mm16(c, lambda k, c=c, s=s: wg[s][:, k, c * 128:(c + 1) * 128], lambda k, tc=tc: hT[:, k, tc], [Bwg[s], BhT[t]])
                    mm16(2, lambda k, s=s: wg[s][:, k, 256:320], lambda k, tc=tc: hT[:, k, tc], [Bwg[s], BhT[t]], M=64)
                    mm16(3, lambda k, s=s: wg[s][:, k, 320:384], lambda k, tc=tc: hT[:, k, tc], [Bwg[s], BhT[t]], M=64)
                    rms_feature_major(2, 256, gkva, lambda c, gc=gc: ckvT[:, c, gc], lambda gt_=gt_: [Bckv[gt_]], t)
                    rope_combine(2, 3, tc, krT[0:64, gc], [Bkr[gt_]])
            groups.append(([(0, 512, 320), (320, 800, 32), (352, 768, 32)], g_kvk))

            fm = []
            if own:
                fm += [("gate", 832, 0), ("gate", 832 + 512, 4), ("dq", 1856, 0), ("dq", 1856 + 512, 4)]
            fm += [("dk", 2880, 0), ("dk", 2880 + 512, 4)]
            if own:
                fm += [("gate", 4928, 8), ("gate", 4928 + 512, 12)]

            def mk_fm(kind, col0, hc0):
                def g_fm(s):
                    for t in range(NT):
                        tc = slice(t * 512, (t + 1) * 512)
                        gc = slice(goff + t * 512, goff + (t + 1) * 512)
                        base = (pb[0] % 2) * 4
                        pb[0] += 1
                        for c in range(4):
                            mm16(base + c, lambda k, c=c, s=s: wg[s][:, k, c * 128:(c + 1) * 128], lambda k, tc=tc: hT[:, k, tc], [Bwg[s], BhT[t]])
                        for c in range(4):
                            hc = hc0 + c
                            if kind == "gate":
                                store_gate(base + c, gate_scr[hc, :, tc], Bgate[hc])
                            elif kind == "dq":
                                store_bf(base + c, dqT[hc, :, tc], Bdq[hc], scale=0.125)
                            else:
                                store_bf(base + c, dkT[hc, :, gc], Bdk[hc], eng=("dve" if c % 2 == 0 else "act"))
                return g_fm
            for (kind, col0, hc0) in fm:
                groups.append(([(0, col0, 512)], mk_fm(kind, col0, hc0)))

            def mk_dv(g):
                def g_dv(s):
                    for tb in range(NBo):
                        bank = pb[0] % 8
                        pb[0] += 1
                        tcb = slice(tb * 128, (tb + 1) * 128)
                        mm16(bank, lambda k, tcb=tcb: hT[:, k, tcb], lambda k, s=s: wg[s][:, k, 0:512], [Bwg[s], BhT[tb // 4]])
                        nbg = goff // 128 + tb
                        store_bf(bank, dvs[nbg, :, g * 512:(g + 1) * 512], Bdv[g], eng=("dve" if tb % 2 == 0 else "act"))
                return g_dv
            for g in range(2):
                groups.append(([(0, 3904 + g * 512, 512)], mk_dv(g)))
            return groups

        gslot = [0]
        for (goff, own) in ((Sh, False), (0, True)):
            rope_tables(goff)
            groups = stage1_groups(goff, own)
            gslot[0] = 0
            slots = []
            for gi in range(len(groups)):
                slots.append(gslot[0] % 2)
                gslot[0] += 1
            load_group(slots[0], groups[0][0])
            stage0(goff)
            for gi, (pieces, comp) in enumerate(groups):
                if gi + 1 < len(groups):
                    load_group(slots[gi + 1], groups[gi + 1][0])
                comp(slots[gi])

        sch.barrier()

        def build_steps():
            tiles = []
            for j in range(NQ):
                st = []
                for kb in range(4 * j):
                    st.append((kb, 0, None))
                    st.append((NBo + kb, 0, None))
                for i in range(4):
                    st.append((4 * j + i, i, 0))
                    st.append((NBo + 4 * j + i, i, 1 + i))
                tiles.append(st)
            return tiles
        steps_by_tile = build_steps()
        if mla_heads < 8 or diff_heads < 8:
            for hh in range(16):
                sch.op("pool", lambda e, hh=hh: e.memset(mixT[:, hh, :], 0.0), writes=[Bmix[hh]])

        def col_groups(c0, gw):
            res = []
            b = c0
            while b < 4:
                e_ = min(4, (b // gw + 1) * gw)
                res.append((b, e_))
                b = e_
            return res

        def attention(nbr, qk_fn, v_fn, exp_args_fn, gw, ST, Obank_fn, Lbank_fn, PT, BPT, fin_fn, reads_qk, reads_v, la, reads_exp=()):
            flat = [(j, t, stp) for j, st in enumerate(steps_by_tile) for t, stp in enumerate(st)]
            n = len(flat)

            def emit_qk(idx):
                j, t, (kb, c0, mk) = flat[idx]
                for br in range(nbr):
                    bank = ST[br][idx % len(ST[br])]
                    sch.op("pe", lambda e, br=br, bank=bank, kb=kb, j=j, c0=c0: qk_fn(e, br, ps[bank], kb, j, c0), reads=reads_qk, writes=[psB[bank]])

            def emit_exp(idx):
                j, t, (kb, c0, mk) = flat[idx]
                for br in range(nbr):
                    bank = ST[br][idx % len(ST[br])]
                    pi = idx % len(PT[br])
                    for (b0, b1) in col_groups(c0, gw):
                        cs = slice(b0 * 128, b1 * 128)
                        kw = exp_args_fn(kb, (4 * j + b0) // gw)
                        sch.op("act", lambda e, bank=bank, br=br, pi=pi, cs=cs, kw=kw: e.activation(out=PT[br][pi][:, cs], in_=ps[bank][:, cs], func=AF.Exp, **kw),
                               reads=[psB[bank], Bc] + list(reads_exp), writes=[BPT[br][pi]])
                    if mk is not None:
                        cs = slice(c0 * 128, (c0 + 1) * 128)
                        sch.op("dve", lambda e, br=br, pi=pi, cs=cs, mk=mk: e.tensor_tensor(out=PT[br][pi][:, cs], in0=PT[br][pi][:, cs], in1=masksb[:, mk, :], op=ALU.mult),
                               reads=[Bc], writes=[BPT[br][pi]])

            def emit_pv(idx):
                j, t, (kb, c0, mk) = flat[idx]
                first = (t == 0)
                last = (t == len(steps_by_tile[j]) - 1)
                cs = slice(c0 * 128, 512)
                for br in range(nbr):
                    pi = idx % len(PT[br])
                    ob = Obank_fn(br, j)
                    lbk = Lbank_fn(br, j)

                    def f(e, br=br, pi=pi, ob=ob, lbk=lbk, kb=kb, cs=cs, first=first, last=last):
                        e.matmul(ps[ob][:, cs], lhsT=v_fn(br, kb), rhs=PT[br][pi][:, cs], start=first, stop=last, skip_group_check=True)
                        return e.matmul(ps[lbk][:, cs], lhsT=onesb[:], rhs=PT[br][pi][:, cs], start=first, stop=last, skip_group_check=True)
                    sch.op("pe", f, reads=[BPT[br][pi], Bc] + reads_v, writes=[psB[ob], psB[lbk]])
                if last:
                    fin_fn(j)

            for idx in range(min(la, n)):
                emit_qk(idx)
            for idx in range(n):
                emit_exp(idx)
                if idx + la < n:
                    emit_qk(idx + la)
                emit_pv(idx)

        ab = Bump(ar, LAT_END, MIX_OFF)
        wq = [ab.alloc([128, 4, 256], BF16, f"wq{i}") for i in range(2)]
        wkv = [ab.alloc([128, 2, 256], BF16, f"wkv{i}") for i in range(2)]
        Bwq = [Buf("wq0"), Buf("wq1")]
        KnT = ab.alloc([128, S], BF16, "KnT")
        Vh = ab.alloc([128, NB, 128], BF16, "Vh")
        QnT = ab.alloc([128, Sh], BF16, "QnT")
        QrT = ab.alloc([128, Sh], BF16, "QrT")
        gtm = ab.alloc([128, Sh], F32, "gtm")
        BK, BV, BQ, Bg = Buf("K"), Buf("V"), Buf("Q"), Buf("g")
        PTm = [[ab.alloc([128, 512], BF16, f"PTm{i}") for i in range(4)]]
        BPTm = [[Buf(f"PTm{i}") for i in range(4)]]
        rL = [ab.alloc([128, 512], F32, f"rL{i}") for i in range(2)]
        tg = [ab.alloc([128, 512], F32, f"tg{i}") for i in range(2)]
        Bfin = [Buf("fin0"), Buf("fin1")]
        rt1m = ab.alloc([128, 512], F32, "rt1m")
        rt2m = ab.alloc([128, 512], F32, "rt2m")
        Brtm = Buf("rtm")
        sch.op("pool", lambda e: e.memset(QrT[64:128, :], 0.0), writes=[BQ])
        SCALE_MLA = 192.0 ** -0.5

        def load_mla_w(h):
            s = h % 2
            sch.dma("pool", f"wq{s}", lambda e: e.dma_start(out=wq[s][:, :, 0:192], in_=w_qb_v[:, :, h * 192:h * 192 + 192]), writes=[Bwq[s]])
            sch.dma("pool", f"wq{s}", lambda e: e.dma_start(out=wq[s][:, :, 192:224], in_=w_qb_v[:, :, h * 192 + 160:h * 192 + 192]), writes=[Bwq[s]])
            sch.dma("pool", f"wq{s}", lambda e: e.dma_start(out=wq[s][:, :, 224:256], in_=w_qb_v[:, :, h * 192 + 128:h * 192 + 160]), writes=[Bwq[s]])
            sch.dma("pool", f"wq{s}", lambda e: e.dma_start(out=wkv[s][:, :, :], in_=w_kvb_v[:, :, h * 256:(h + 1) * 256]), writes=[Bwq[s]])

        if mla_heads > 0:
            load_mla_w(0)
        for h in range(mla_heads):
            s = h % 2
            if h + 1 < mla_heads:
                load_mla_w(h + 1)
            sch.dma("sp", "gtm", lambda e, h=h: e.dma_start(out=gtm[:], in_=gate_scr[h, :, :]), reads=[Bgate[h]], writes=[Bg])
            prep_banks = [0, 1, 2, 7]
            pc = [0]

            def nb_():
                b = prep_banks[pc[0] % 4]
                pc[0] += 1
                return b
            for t in range(S // 512):
                bank = nb_()
                tc = slice(t * 512, (t + 1) * 512)

                def f(e, bank=bank, tc=tc, s=s):
                    for c in range(2):
                        inst = e.matmul(ps[bank][:], lhsT=wkv[s][:, c, 0:128], rhs=ckvT[:, c, tc], start=(c == 0), stop=(c == 1))
                    return inst
                sch.op("pe", f, reads=[Bwq[s], Bckv[t]], writes=[psB[bank]])
                if t % 2 == 0:
                    sch.op("dve", lambda e, bank=bank, tc=tc: e.tensor_copy(out=KnT[:, tc], in_=ps[bank][:]), reads=[psB[bank]], writes=[BK])
                else:
                    sch.op("act", lambda e, bank=bank, tc=tc: e.activation(out=KnT[:, tc], in_=ps[bank][:], func=AF.Copy), reads=[psB[bank]], writes=[BK])
            for q4 in range(NB // 4):
                bank = nb_()

                def f(e, bank=bank, q4=q4, s=s):
                    for jj in range(4):
                        blk = q4 * 4 + jj
                        for c in range(2):
                            inst = e.matmul(ps[bank][:, jj * 128:(jj + 1) * 128], lhsT=ckvT[:, c, blk * 128:(blk + 1) * 128], rhs=wkv[s][:, c, 128:256], start=(c == 0), stop=(c == 1))
                    return inst
                sch.op("pe", f, reads=[Bwq[s], Bckv[q4]], writes=[psB[bank]])
                dst = Vh[:, q4 * 4:q4 * 4 + 4, :]
                if q4 % 2 == 0:
                    sch.op("act", lambda e, bank=bank, dst=dst: e.activation(out=dst, in_=ps[bank][:].rearrange("p (a b) -> p a b", a=4), func=AF.Copy), reads=[psB[bank]], writes=[BV])
                else:
                    sch.op("dve", lambda e, bank=bank, dst=dst: e.tensor_copy(out=dst, in_=ps[bank][:].rearrange("p (a b) -> p a b", a=4)), reads=[psB[bank]], writes=[BV])
            for t in range(NT):
                tc = slice(t * 512, (t + 1) * 512)
                bank = nb_()

                def f(e, bank=bank, tc=tc, s=s):
                    for c in range(4):
                        inst = e.matmul(ps[bank][:], lhsT=wq[s][:, c, 0:128], rhs=cqT[:, c, tc], start=(c == 0), stop=(c == 3))
                    return inst
                sch.op("pe", f, reads=[Bwq[s], Bcq[t]], writes=[psB[bank]])
                sch.op("dve", lambda e, bank=bank, tc=tc: e.tensor_copy(out=QnT[:, tc], in_=ps[bank][:]), reads=[psB[bank]], writes=[BQ])
                bA = nb_()
                bB = nb_()

                def f2(e, bA=bA, bB=bB, tc=tc, s=s):
                    for c in range(4):
                        e.matmul(ps[bA][0:64, :], lhsT=wq[s][:, c, 128:192], rhs=cqT[:, c, tc], start=(c == 0), stop=(c == 3))
                    for c in range(4):
                        inst = e.matmul(ps[bB][0:64, :], lhsT=wq[s][:, c, 192:256], rhs=cqT[:, c, tc], start=(c == 0), stop=(c == 3))
                    return inst
                sch.op("pe", f2, reads=[Bwq[s], Bcq[t]], writes=[psB[bA], psB[bB]])
                sch.op("dve", lambda e, bA=bA, tc=tc: e.tensor_tensor(out=rt1m[0:64, :], in0=ps[bA][0:64, :], in1=cosT[0:64, tc], op=ALU.mult), reads=[psB[bA], Btab], writes=[Brtm])
                sch.op("dve", lambda e, bB=bB, tc=tc: e.tensor_tensor(out=rt2m[0:64, :], in0=ps[bB][0:64, :], in1=sinS[0:64, tc], op=ALU.mult), reads=[psB[bB], Btab], writes=[Brtm])
                sch.op("dve", lambda e, tc=tc: e.tensor_tensor(out=QrT[0:64, tc], in0=rt1m[0:64, :], in1=rt2m[0:64, :], op=ALU.add), reads=[Brtm], writes=[BQ])

            def qk_mla(e, br, pst, kb, j, c0):
                ks = slice(kb * 128, (kb + 1) * 128)
                qs = slice(j * 512 + c0 * 128, (j + 1) * 512)
                os_ = slice(c0 * 128, 512)
                e.matmul(pst[:, os_], lhsT=KnT[:, ks], rhs=QnT[:, qs], start=True, stop=False)
                return e.matmul(pst[:, os_], lhsT=krT[:, ks], rhs=QrT[:, qs], start=False, stop=True)

            def fin_mla(j, h=h):
                i = j % 2
                ob, lbk = 3 + i, 5 + i
                tc = slice(j * 512, (j + 1) * 512)
                sch.op("dve", lambda e: e.reciprocal(out=rL[i][:], in_=ps[lbk][:]), reads=[psB[lbk]], writes=[Bfin[i]])
                sch.op("dve", lambda e: e.tensor_tensor(out=tg[i][:], in0=rL[i][:], in1=gtm[:, tc], op=ALU.mult), reads=[Bg], writes=[Bfin[i]])
                sch.op("dve", lambda e: e.tensor_tensor(out=mixT[:, h, tc], in0=ps[ob][:], in1=tg[i][:], op=ALU.mult), reads=[psB[ob], Bfin[i]], writes=[Bmix[h]])

            attention(1, qk_mla, lambda br, kb: Vh[:, kb, :], lambda kb, g: dict(scale=SCALE_MLA), 4,
                      [[0, 1, 2]], lambda br, j: 3 + (j % 2), lambda br, j: 5 + (j % 2), PTm, BPTm, fin_mla,
                      [BK, BQ, Bkr[0]] + Bkr[1:], [BV], 2)

        sch.barrier()

        db = Bump(ar, CONST_END, MIX_OFF)
        wout = db.alloc([128, 16, D], BF16, "wout")
        Bwout = Buf("wout")
        fin_mark = db.p
        K12 = [db.alloc([128, S], BF16, f"K12_{i}") for i in range(2)]
        Q12 = [db.alloc([128, Sh], BF16, f"Q12_{i}") for i in range(2)]
        Vd = [db.alloc([128, NB, 128], BF16, f"Vd{i}") for i in range(2)]
        gtd = [db.alloc([128, Sh], F32, f"gtd{i}") for i in range(2)]
        NGmax = NBo
        bt = [db.alloc([128, NB, NGmax], F32, f"bt{i}") for i in range(2)]
        Bhd = [Buf("hd0"), Buf("hd1")]
        Bbt = [Buf("bt0"), Buf("bt1")]
        PTd = [[db.alloc([128, 512], BF16, f"PTd{b}_{i}") for i in range(3)] for b in range(2)]
        BPTd = [[Buf(f"PTd{b}_{i}") for i in range(3)] for b in range(2)]
        rL1 = db.alloc([128, 512], F32, "rL1")
        rL2 = db.alloc([128, 512], F32, "rL2")
        t1, t2, od, ud = rL1, rL2, rL1, rL1
        sqd = db.alloc([128, 512], BF16, "sqd")
        lnd = db.alloc([128, 512], F32, "lnd")
        rsd = lnd
        Bfd = Buf("fd")

        def load_diff(h):
            s = h % 2
            sch.dma("sp", f"hd{s}", lambda e: e.dma_start(out=K12[s][:], in_=dkT[h, :, :]), reads=[Bdk[h]], writes=[Bhd[s]])
            sch.dma("sp", f"hd{s}", lambda e: e.dma_start(out=Q12[s][:], in_=dqT[h, :, :]), reads=[Bdq[h]], writes=[Bhd[s]])
            sch.dma("sp", f"hd{s}", lambda e: e.dma_start(out=Vd[s][:], in_=dvs[:, :, h * 128:(h + 1) * 128].rearrange("nb p d -> p nb d")), reads=[Bdv[h // 4]], writes=[Bhd[s]])
            sch.dma("sp", f"hd{s}", lambda e: e.dma_start(out=gtd[s][:], in_=gate_scr[8 + h, :, :]), reads=[Bgate[8 + h]], writes=[Bhd[s]])

        if diff_heads > 0:
            load_diff(0)
        for h in range(diff_heads):
            s = h % 2
            gw = gws[h]
            NG = NBo // gw
            slope = 2.0 ** (-(h + 1))
            if h + 1 < diff_heads:
                load_diff(h + 1)
            if h == 0:
                for g in range(4):
                    sch.dma("pool", "wout", lambda e, g=g: e.dma_start(out=wout[:, :, g * 512:(g + 1) * 512], in_=w_out_v[:, :, g * 512:(g + 1) * 512]), writes=[Bwout])
            btv = bt[s][:, :, 0:NG]
            sch.op("dve", lambda e, btv=btv, NG=NG, gw=gw: e.tensor_tensor(out=btv, in0=poskf[:].unsqueeze(2).broadcast_to([128, NB, NG]), in1=pmids[gw][:].unsqueeze(1).broadcast_to([128, NB, NG]), op=ALU.subtract),
                   reads=[Bc], writes=[Bbt[s]])
            sch.op("dve", lambda e, btv=btv, slope=slope: e.tensor_scalar(out=btv, in0=btv, scalar1=slope, scalar2=40.0, op0=ALU.mult, op1=ALU.min), writes=[Bbt[s]])

            def qk_diff(e, br, pst, kb, j, c0, s=s):
                ks = slice(kb * 128, (kb + 1) * 128)
                qs = slice(j * 512 + c0 * 128, (j + 1) * 512)
                os_ = slice(c0 * 128, 512)
                pr = slice(64 * br, 64 * br + 64)
                return e.matmul(pst[:, os_], lhsT=K12[s][pr, ks], rhs=Q12[s][pr, qs], start=True, stop=True)

            def fin_diff(j, h=h, s=s):
                tc = slice(j * 512, (j + 1) * 512)
                sch.op("dve", lambda e: e.reciprocal(out=rL1[:], in_=ps[6][:]), reads=[psB[6]], writes=[Bfd])
                sch.op("dve", lambda e: e.reciprocal(out=rL2[:], in_=ps[7][:]), reads=[psB[7]], writes=[Bfd])
                sch.op("dve", lambda e: e.tensor_tensor(out=t1[:], in0=ps[4][:], in1=rL1[:], op=ALU.mult), reads=[psB[4]], writes=[Bfd])
                sch.op("dve", lambda e: e.tensor_tensor(out=t2[:], in0=ps[5][:], in1=rL2[:], op=ALU.mult), reads=[psB[5]], writes=[Bfd])
                sch.op("dve", lambda e: e.scalar_tensor_tensor(out=od[:], in0=t2[:], scalar=negl[:, 0:1], in1=t1[:], op0=ALU.mult, op1=ALU.add), reads=[Bc], writes=[Bfd])
                sch.op("act", lambda e: e.activation(out=sqd[:], in_=od[:], func=AF.Square), reads=[Bfd], writes=[Bfd])

                def f(e):
                    return e.matmul(ps[6][:], lhsT=onesb[:], rhs=sqd[:], start=True, stop=True)
                sch.op("pe", f, reads=[Bfd, Bc], writes=[psB[6]])
                sch.op("act", lambda e: e.activation(out=lnd[:], in_=ps[6][:], func=AF.Ln, scale=1.0 / 128.0, bias=epsc[:, 0:1]), reads=[psB[6], Bc], writes=[Bfd])
                sch.op("act", lambda e: e.activation(out=rsd[:], in_=lnd[:], func=AF.Exp, scale=-0.5), reads=[Bfd], writes=[Bfd])
                sch.op("dve", lambda e: e.scalar_tensor_tensor(out=ud[:], in0=od[:], scalar=gsub8[:, 0:1], in1=rsd[:], op0=ALU.mult, op1=ALU.mult), reads=[Bc], writes=[Bfd])
                sch.op("dve", lambda e: e.tensor_tensor(out=mixT[:, 8 + h, tc], in0=ud[:], in1=gtd[s][:, tc], op=ALU.mult), reads=[Bhd[s]], writes=[Bmix[8 + h], Bfd])

            attention(2, qk_diff, lambda br, kb, s=s: Vd[s][:, kb, :], lambda kb, g, s=s: dict(bias=bt[s][:, kb, g:g + 1], scale=1.0), gw,
                      [[0, 1], [2, 3]], lambda br, j: 4 + br, lambda br, j: 6 + br, PTd, BPTd, fin_diff,
                      [Bhd[s]], [Bhd[s]], 1, reads_exp=[Bbt[s]])

        sch.barrier()

        fb = Bump(ar, fin_mark, MIX_OFF)
        gpost_bc = fb.alloc([128, D], F32, "gpost_bc")
        xo = [fb.alloc([128, D], F32, f"xo{i}") for i in range(2)]
        yo = [fb.alloc([128, D], F32, f"yo{i}") for i in range(2)]
        junk2 = fb.alloc([128, 512], BF16, "junk2")
        Bxo = [Buf("xo0"), Buf("xo1")]
        Byo = [Buf("yo0"), Buf("yo1")]
        Bj2 = Buf("junk2")
        Bgp = Buf("gpost")
        ss4 = [fb.alloc([128, 4], F32, f"ss4_{i}") for i in range(2)]
        ss1 = [fb.alloc([128, 1], F32, f"ss1_{i}") for i in range(2)]
        ln1 = [fb.alloc([128, 1], F32, f"ln1_{i}") for i in range(2)]
        rs1 = [fb.alloc([128, 1], F32, f"rs1_{i}") for i in range(2)]
        Bst = [Buf("st0"), Buf("st1")]
        sch.dma("sp", "gpost", lambda e: e.dma_start(out=gpost_bc[:], in_=g_post.broadcast_to([128, D])), writes=[Bgp])
        for tb in range(NBo):
            i = tb % 2
            tcb = slice(tb * 128, (tb + 1) * 128)
            sch.dma("sp", f"xo{i}", lambda e, i=i, tb=tb: e.dma_start(out=xo[i][:], in_=xp[tb * 128:(tb + 1) * 128, :]), writes=[Bxo[i]])
            for g in range(4):
                bank = 4 * i + g

                def f(e, bank=bank, g=g, tcb=tcb):
                    for k in range(16):
                        inst = e.matmul(ps[bank][:], lhsT=mixT[:, k, tcb], rhs=wout[:, k, g * 512:(g + 1) * 512], start=(k == 0), stop=(k == 15))
                    return inst
                sch.op("pe", f, reads=Bmix + [Bwout], writes=[psB[bank]])
                sch.op("act", lambda e, bank=bank, g=g, i=i: e.activation(out=junk2[:], in_=ps[bank][:], func=AF.Square, accum_out=ss4[i][:, g:g + 1]), reads=[psB[bank]], writes=[Bj2, Bst[i]])
            sch.op("dve", lambda e, i=i: e.tensor_reduce(out=ss1[i][:], in_=ss4[i][:], axis=AX.X, op=ALU.add), reads=[Bst[i]], writes=[Bst[i]])
            sch.op("act", lambda e, i=i: e.activation(out=ln1[i][:], in_=ss1[i][:], func=AF.Ln, scale=1.0 / D, bias=epsc[:, 0:1]), reads=[Bst[i], Bc], writes=[Bst[i]])
            sch.op("act", lambda e, i=i: e.activation(out=rs1[i][:], in_=ln1[i][:], func=AF.Exp, scale=-0.5), reads=[Bst[i]], writes=[Bst[i]])
            for g in range(4):
                bank = 4 * i + g
                gs = slice(g * 512, (g + 1) * 512)
                sch.op("dve", lambda e, bank=bank, gs=gs, i=i: e.scalar_tensor_tensor(out=yo[i][:, gs], in0=ps[bank][:], scalar=rs1[i][:, 0:1], in1=gpost_bc[:, gs], op0=ALU.mult, op1=ALU.mult),
                       reads=[psB[bank], Bst[i], Bgp], writes=[Byo[i]])
            sch.op("dve", lambda e, i=i: e.tensor_tensor(out=yo[i][:], in0=yo[i][:], in1=xo[i][:], op=ALU.add), reads=[Bxo[i]], writes=[Byo[i]])
            toks_final.append(sch.dma("sp", f"yo{i}", lambda e, i=i, tb=tb: e.dma_start(out=out[tb * 128:(tb + 1) * 128, :], in_=yo[i][:]), reads=[Byo[i]]))

        sch.wait_all("sp", toks_final)
        sch.emit()
    return nc


def own_blocks(NB, parity):
    own = [g for g in range(NB) if ((g % 4) in (0, 3)) == (parity == 0)]
    other = [g for g in range(NB) if g not in own]
    return own, other


def make_core_inputs(S, parity, xb, posb, shared):
    NB = S // 128
    own, other = own_blocks(NB, parity)
    order = own + other
    tok = np.concatenate([np.arange(g * 128, (g + 1) * 128) for g in order])
    xp = np.ascontiguousarray(xb[tok])
    pp = np.ascontiguousarray(posb[tok]).astype(np.int32)
    masks = np.zeros((5, 128, 128), np.float32)
    k = np.arange(128)[:, None]
    q = np.arange(128)[None, :]
    masks[0] = (q >= k).astype(np.float32)
    flags = [0, 1, 0, 1] if parity == 0 else [1, 0, 1, 0]
    for i in range(4):
        masks[1 + i] = float(flags[i])
    d = dict(shared)
    d.update({
        "xp": xp,
        "posrow": pp.reshape(1, S),
        "postm": np.ascontiguousarray(pp.reshape(NB, 128).T),
        "posmid": np.ascontiguousarray(pp.reshape(NB, 128)[:NB // 2, 64]).reshape(1, NB // 2),
        "masks": masks,
    })
    return d, tok[:S // 2]


def make_shared(g_pre, w_in, g_q_a, w_q_b, g_kv_a, w_kv_b, lambda_q1, lambda_k1, lambda_q2, lambda_k2,
                g_diff_sub, w_out, g_post):
    f = np.float32
    freq = (1.0 / (np.float32(10000.0) ** (np.arange(0, 64, 2, dtype=np.float32) / np.float32(64)))).astype(f)
    ropec = np.zeros((128, 2), f)
    ropec[0:32, 0] = freq
    ropec[32:64, 0] = freq
    ropec[0:32, 1] = -1.0
    ropec[32:64, 1] = 1.0
    return {
        "w_in": np.ascontiguousarray(w_in[0], dtype=f),
        "w_qb": np.ascontiguousarray(w_q_b[0].reshape(512, 1536), dtype=f),
        "w_kvb": np.ascontiguousarray(w_kv_b[0].reshape(256, 2048), dtype=f),
        "w_out": np.ascontiguousarray(w_out[0], dtype=f),
        "g_pre": np.ascontiguousarray(g_pre[0].reshape(1, D), dtype=f),
        "gqa": np.ascontiguousarray(g_q_a[0].reshape(4, 128).T, dtype=f),
        "gkva": np.ascontiguousarray(g_kv_a[0].reshape(2, 128).T, dtype=f),
        "gsub": np.ascontiguousarray(g_diff_sub[0].reshape(1, 128).T, dtype=f),
        "g_post": np.ascontiguousarray(g_post[0].reshape(1, D), dtype=f),
        "lam": np.concatenate([lambda_q1[0], lambda_k1[0], lambda_q2[0], lambda_k2[0]]).reshape(1, 256).astype(f),
        "ident": np.eye(128, dtype=f),
        "ropec": ropec,
    }


_PROG_CACHE = {}


def run_layer(x, positions, params, **bkw):
    B, S, _ = x.shape
    shared = make_shared(**params)
    key = (S, tuple(sorted(bkw.items())))
    if key not in _PROG_CACHE:
        _PROG_CACHE[key] = build_program(S, **bkw)
    nc = _PROG_CACHE[key]
    in_maps, toks = [], []
    for b in range(B):
        for par in range(2):
            d, tok = make_core_inputs(S, par, x[b], positions[b], shared)
            in_maps.append(d)
            toks.append((b, tok))
    ncores = len(in_maps)
    res = run_bass_kernel_spmd(nc, in_maps, core_ids=list(range(ncores)))
    outp = np.empty((B, S, D), np.float32)
    for ci, (b, tok) in enumerate(toks):
        outp[b, tok] = res.results[ci]["out"]
    return outp


def kernel(x, positions, g_pre, w_in, g_q_a, w_q_b, g_kv_a, w_kv_b, lambda_q1, lambda_k1, lambda_q2, lambda_k2,
           g_diff_sub, w_out, g_post):
    params = dict(g_pre=np.asarray(g_pre), w_in=np.asarray(w_in), g_q_a=np.asarray(g_q_a), w_q_b=np.asarray(w_q_b),
                  g_kv_a=np.asarray(g_kv_a), w_kv_b=np.asarray(w_kv_b), lambda_q1=np.asarray(lambda_q1),
                  lambda_k1=np.asarray(lambda_k1), lambda_q2=np.asarray(lambda_q2), lambda_k2=np.asarray(lambda_k2),
                  g_diff_sub=np.asarray(g_diff_sub), w_out=np.asarray(w_out), g_post=np.asarray(g_post))
    return run_layer(np.asarray(x, dtype=np.float32), np.asarray(positions), params)
```

```python
import contextlib
import math
import numpy as np
import concourse.bass as bass
import concourse.mybir as mybir
from concourse.bass_utils import run_bass_kernel_spmd

F32 = mybir.dt.float32
BF16 = mybir.dt.bfloat16
I32 = mybir.dt.int32
AF = mybir.ActivationFunctionType
ALU = mybir.AluOpType
AX = mybir.AxisListType

D = 2048
INW = 5952
EPS = 1e-6
ENGS = ("pe", "act", "dve", "pool", "sp")
LAM_INIT = 0.8 - 0.6 * math.exp(-0.3 * 0)
TWO_PI = 2.0 * math.pi
C1 = 6.28125
C2 = float(np.float32(TWO_PI - C1))
PI_SAFE = 3.1415925


class Buf:
    __slots__ = ("name", "w", "r")

    def __init__(self, name=""):
        self.name = name
        self.w = None
        self.r = {}


class Sched:
    def __init__(self, nc, es):
        self.nc = nc
        self.es = es
        self.streams = {e: [] for e in ENGS}
        self.count = {e: 0 for e in ENGS}
        self.seen = {e: {} for e in ENGS}
        self.sem = {}
        self.dcount = {}
        for e in ENGS:
            self.sem[e] = es.enter_context(nc.semaphore("sem_" + e))

    def _dsem(self, key):
        if key not in self.sem:
            self.sem[key] = self.es.enter_context(self.nc.semaphore("dsem_" + key))
            self.dcount[key] = 0

    def _waits(self, eng, reads, writes):
        deps = {}

        def add(k, v):
            if deps.get(k, 0) < v:
                deps[k] = v
        for b in reads:
            if b.w is not None:
                add(*b.w)
        for b in writes:
            if b.w is not None:
                add(*b.w)
            for k, v in b.r.items():
                add(k, v)
        waits = []
        seen = self.seen[eng]
        for k, v in deps.items():
            if seen.get(k, 0) < v:
                seen[k] = v
                waits.append((k, v))
        return waits

    def _commit(self, tok, reads, writes):
        k, v = tok
        for b in reads:
            if b.r.get(k, 0) < v:
                b.r[k] = v
        for b in writes:
            b.w = tok
            b.r = {}

    def op(self, eng, fn, reads=(), writes=()):
        waits = self._waits(eng, reads, writes)
        self.count[eng] += 1
        tok = (eng, self.count[eng])
        self.streams[eng].append((waits, fn, (eng, 1)))
        self._commit(tok, reads, writes)
        return tok

    def dma(self, eng, key, fn, reads=(), writes=()):
        self._dsem(key)
        waits = self._waits(eng, reads, writes)
        self.dcount[key] += 16
        tok = (key, self.dcount[key])
        self.streams[eng].append((waits, fn, (key, 16)))
        self._commit(tok, reads, writes)
        return tok

    def wait_all(self, eng, toks):
        deps = {}
        for (k, v) in toks:
            if deps.get(k, 0) < v:
                deps[k] = v
        waits = [(k, v) for k, v in deps.items() if self.seen[eng].get(k, 0) < v]
        for k, v in waits:
            self.seen[eng][k] = v
        self.streams[eng].append((waits, None, None))

    def barrier(self):
        toks = [(e, self.count[e]) for e in ENGS if self.count[e] > 0]
        toks += [(k, v) for k, v in self.dcount.items() if v > 0]
        for e in ENGS:
            self.wait_all(e, toks)

    def emit(self):
        nc = self.nc
        with nc.Block() as block:
            def run(ename):
                def body(e):
                    for waits, fn, inc in self.streams[ename]:
                        for k, v in waits:
                            e.wait_ge(self.sem[k], v)
                        if fn is not None:
                            inst = fn(e)
                            inst.then_inc(self.sem[inc[0]], inc[1])
                return body
            block.tensor(run("pe"))
            block.scalar(run("act"))
            block.vector(run("dve"))
            block.gpsimd(run("pool"))
            block.sync(run("sp"))


class Arena:
    def __init__(self, nc):
        self.nc = nc
        self.base = ((nc.sbuf_base + 63) // 64) * 64
        self.top = nc.sbuf_top
        self.n = 0

    def at(self, off, shape, dtype, name=None):
        esz = 4 if dtype in (F32, I32) else 2
        n = esz
        for s in shape[1:]:
            n *= s
        assert self.base + off + n <= self.top, ("SBUF overflow", name, off, n)
        self.n += 1
        t = self.nc.alloc_sbuf_tensor_at(name or f"t{self.n}", list(shape), dtype, offset=self.base + off)
        return t, off + ((n + 63) // 64) * 64


class Bump:
    def __init__(self, arena, start, limit):
        self.a, self.p, self.limit = arena, start, limit

    def alloc(self, shape, dtype, name=None):
        t, self.p = self.a.at(self.p, shape, dtype, name)
        assert self.p <= self.limit, ("region overflow", name, self.p, self.limit)
        return t


def build_program(S, gws=(1, 1, 2, 4, 4, 4, 4, 4), mla_heads=8, diff_heads=8):
    NB = S // 128
    NBo = NB // 2
    Sh = S // 2
    NQ = Sh // 512
    NT = Sh // 512
    KB = 1024

    nc = bass.Bass("TRN2", target_bir_lowering=False)

    def din(name, shape, dt=F32):
        return nc.dram_tensor(name, list(shape), dt, kind="ExternalInput").ap()
    xp = din("xp", [S, D])
    posrow = din("posrow", [1, S], I32)
    postm = din("postm", [128, NB], I32)
    posmid = din("posmid", [1, NBo], I32)
    w_in = din("w_in", [D, INW])
    w_qb = din("w_qb", [512, 1536])
    w_kvb = din("w_kvb", [256, 2048])
    w_out = din("w_out", [D, D])
    g_pre = din("g_pre", [1, D])
    gqa_d = din("gqa", [128, 4])
    gkva_d = din("gkva", [128, 2])
    gsub_d = din("gsub", [128, 1])
    g_post = din("g_post", [1, D])
    lam_d = din("lam", [1, 256])
    ident_d = din("ident", [128, 128])
    masks_d = din("masks", [5, 128, 128])
    ropec_d = din("ropec", [128, 2])
    out = nc.dram_tensor("out", [Sh, D], F32, kind="ExternalOutput").ap()
    dkT = nc.dram_tensor("dkT_scr", [8, 128, S], BF16, kind="Internal").ap()
    dvs = nc.dram_tensor("dvs_scr", [NB, 128, 1024], BF16, kind="Internal").ap()
    dqT = nc.dram_tensor("dqT_scr", [8, 128, Sh], BF16, kind="Internal").ap()
    gate_scr = nc.dram_tensor("gate_scr", [16, 128, Sh], F32, kind="Internal").ap()

    w_in_v = w_in.rearrange("(c p) e -> p c e", p=128)
    w_qb_v = w_qb.rearrange("(c p) e -> p c e", p=128)
    w_kvb_v = w_kvb.rearrange("(c p) e -> p c e", p=128)
    w_out_v = w_out.rearrange("(c p) e -> p c e", p=128)

    with contextlib.ExitStack() as es:
        sch = Sched(nc, es)
        ar = Arena(nc)
        TOTAL = ar.top - ar.base
        ps = [es.enter_context(nc.psum_tensor(f"psb{i}", [128, 512], F32)) for i in range(8)]
        psB = [Buf(f"ps{i}") for i in range(8)]

        MIX_BYTES = 16 * Sh * 2
        CONST_END = 5120
        LAT_END = CONST_END + (2 * Sh * 4) + (2 * S * 2) + (S * 2) + (4 * Sh * 2) + 256
        MIX_OFF = ((TOTAL - MIX_BYTES) // 64) * 64
        assert LAT_END <= MIX_OFF

        cb = Bump(ar, 0, CONST_END)
        identb = cb.alloc([128, 128], BF16, "identb")
        onesb = cb.alloc([128, 128], BF16, "onesb")
        masksb = cb.alloc([128, 5, 128], BF16, "masksb")
        epsc = cb.alloc([128, 1], F32, "epsc")
        ropec = cb.alloc([128, 2], F32, "ropec")
        gqa = cb.alloc([128, 4], F32, "gqa")
        gkva = cb.alloc([128, 2], F32, "gkva")
        gsubs = cb.alloc([128, 1], F32, "gsubs")
        lamv = cb.alloc([128, 256], F32, "lamv")
        lamp = cb.alloc([128, 128], F32, "lamp")
        lams = cb.alloc([128, 2], F32, "lams")
        lame = cb.alloc([128, 2], F32, "lame")
        negl = cb.alloc([128, 1], F32, "negl")
        gsub8 = cb.alloc([128, 1], F32, "gsub8")
        poski = cb.alloc([128, NB], I32, "poski")
        poskf = cb.alloc([128, NB], F32, "poskf")
        pmidi = cb.alloc([128, NBo], I32, "pmidi")
        pmid1 = cb.alloc([128, NBo], F32, "pmid1")
        pmid2 = cb.alloc([128, NBo // 2], F32, "pmid2")
        pmid4 = cb.alloc([128, NBo // 4], F32, "pmid4")
        ssq = cb.alloc([128, 8], F32, "ssq")
        lnc = cb.alloc([128, 4], F32, "lnc")
        rsc = cb.alloc([128, 4], F32, "rsc")
        Bc = Buf("consts")

        lb = Bump(ar, CONST_END, LAT_END)
        cosT = lb.alloc([128, Sh], F32, "cosT")
        sinS = lb.alloc([128, Sh], F32, "sinS")
        ckvT = lb.alloc([128, 2, S], BF16, "ckvT")
        krT = lb.alloc([128, S], BF16, "krT")
        cqT = lb.alloc([128, 4, Sh], BF16, "cqT")
        Btab = Buf("tab")
        Bckv = [Buf(f"ckv{i}") for i in range(S // 512)]
        Bkr = [Buf(f"kr{i}") for i in range(S // 512)]
        Bcq = [Buf(f"cq{i}") for i in range(NT)]

        mixT, _ = ar.at(MIX_OFF, [128, 16, Sh], BF16, "mixT")
        Bmix = [Buf(f"mix{h}") for h in range(16)]

        toks_final = []

        sch.dma("pool", "c_id", lambda e: e.dma_start(out=identb[:], in_=ident_d[:, :]), writes=[Bc])
        sch.dma("pool", "c_mk", lambda e: e.dma_start(out=masksb[:], in_=masks_d.rearrange("m p q -> p m q")), writes=[Bc])
        sch.op("pool", lambda e: e.memset(onesb[:], 1.0), writes=[Bc])
        sch.op("pool", lambda e: e.memset(epsc[:], EPS), writes=[Bc])
        sch.op("pool", lambda e: e.memset(krT[64:128, :], 0.0), writes=[Bc])
        sch.dma("sp", "c_a", lambda e: e.dma_start(out=ropec[:], in_=ropec_d[:, :]), writes=[Bc])
        sch.dma("sp", "c_b", lambda e: e.dma_start(out=gqa[:], in_=gqa_d[:, :]), writes=[Bc])
        sch.dma("sp", "c_c", lambda e: e.dma_start(out=gkva[:], in_=gkva_d[:, :]), writes=[Bc])
        sch.dma("sp", "c_d", lambda e: e.dma_start(out=gsubs[:], in_=gsub_d[:, :]), writes=[Bc])
        sch.dma("sp", "c_e", lambda e: e.dma_start(out=lamv[:], in_=lam_d.broadcast_to([128, 256])), writes=[Bc])
        sch.dma("sp", "c_f", lambda e: e.dma_start(out=poski[:], in_=postm[:, :]), writes=[Bc])
        sch.dma("sp", "c_g", lambda e: e.dma_start(out=pmidi[:], in_=posmid.broadcast_to([128, NBo])), writes=[Bc])
        sch.op("dve", lambda e: e.tensor_tensor(out=lamp[:, 0:64], in0=lamv[:, 0:64], in1=lamv[:, 64:128], op=ALU.mult), reads=[Bc], writes=[Bc])
        sch.op("dve", lambda e: e.tensor_tensor(out=lamp[:, 64:128], in0=lamv[:, 128:192], in1=lamv[:, 192:256], op=ALU.mult), reads=[Bc], writes=[Bc])
        sch.op("dve", lambda e: e.tensor_reduce(out=lams[:], in_=lamp[:].rearrange("p (a b) -> p a b", a=2), axis=AX.X, op=ALU.add), reads=[Bc], writes=[Bc])
        sch.op("act", lambda e: e.activation(out=lame[:], in_=lams[:], func=AF.Exp), reads=[Bc], writes=[Bc])
        sch.op("dve", lambda e: e.tensor_tensor(out=negl[:], in0=lame[:, 1:2], in1=lame[:, 0:1], op=ALU.subtract), reads=[Bc], writes=[Bc])
        sch.op("dve", lambda e: e.tensor_scalar(out=negl[:], in0=negl[:], scalar1=-LAM_INIT, scalar2=None, op0=ALU.add), reads=[Bc], writes=[Bc])
        sch.op("dve", lambda e: e.tensor_scalar(out=gsub8[:], in0=gsubs[:], scalar1=(1.0 - LAM_INIT), scalar2=None, op0=ALU.mult), reads=[Bc], writes=[Bc])
        sch.op("dve", lambda e: e.tensor_copy(out=poskf[:], in_=poski[:]), reads=[Bc], writes=[Bc])
        sch.op("dve", lambda e: e.tensor_copy(out=pmid1[:], in_=pmidi[:]), reads=[Bc], writes=[Bc])
        sch.op("dve", lambda e: e.tensor_reduce(out=pmid2[:], in_=pmid1[:].rearrange("p (a b) -> p a b", b=2), axis=AX.X, op=ALU.add), reads=[Bc], writes=[Bc])
        sch.op("dve", lambda e: e.tensor_scalar(out=pmid2[:], in0=pmid2[:], scalar1=0.5, scalar2=None, op0=ALU.mult), reads=[Bc], writes=[Bc])
        sch.op("dve", lambda e: e.tensor_reduce(out=pmid4[:], in_=pmid1[:].rearrange("p (a b) -> p a b", b=4), axis=AX.X, op=ALU.add), reads=[Bc], writes=[Bc])
        sch.op("dve", lambda e: e.tensor_scalar(out=pmid4[:], in0=pmid4[:], scalar1=0.25, scalar2=None, op0=ALU.mult), reads=[Bc], writes=[Bc])
        pmids = {1: pmid1, 2: pmid2, 4: pmid4}

        s1 = Bump(ar, LAT_END, TOTAL)
        hT = s1.alloc([128, 16, Sh], BF16, "hT")
        BhT = [Buf(f"hT{i}") for i in range(NT)]
        wg = [s1.alloc([128, 16, 512], BF16, f"wg{i}") for i in range(2)]
        Bwg = [Buf("wg0"), Buf("wg1")]
        gpre_bc = s1.alloc([128, D], F32, "gpre_bc")
        xt = [s1.alloc([128, D], F32, f"xt{i}") for i in range(2)]
        wg1_off = LAT_END + 16 * Sh * 2 + 16 * 512 * 2
        xt2, _ = ar.at(wg1_off, [128, D], F32, "xt2")
        xt3, _ = ar.at(wg1_off + D * 4, [128, D], F32, "xt3")
        xt += [xt2, xt3]
        Bxt = [Buf("xt0"), Buf("xt1"), Buf("xt2"), Buf("xt3")]
        xb = [s1.alloc([128, D], BF16, f"xb{i}") for i in range(2)]
        Bxb = [Buf("xb0"), Buf("xb1")]
        NSTG, NSTF = 3, 2
        stg = [s1.alloc([128, 512], BF16, f"stg{i}") for i in range(NSTG)]
        Bstg = [Buf(f"stg{i}") for i in range(NSTG)]
        stf = [s1.alloc([128, 512], F32, f"stf{i}") for i in range(NSTF)]
        Bstf = [Buf(f"stf{i}") for i in range(NSTF)]
        sq = s1.alloc([128, 4, 512], BF16, "sq")
        Bsq = Buf("sq")
        junk, _ = ar.at(s1.p - 4 * 512 * 2, [128, D], BF16, "junk")
        Bjunk = Bsq
        rst = s1.alloc([128, 512], F32, "rst")
        lnt = rst
        Brs = Buf("rs")
        rt1 = s1.alloc([128, 512], F32, "rt1")
        rt2 = s1.alloc([128, 512], F32, "rt2")
        Brt = Buf("rt")
        tmpo = LAT_END + 16 * Sh * 2
        t_posi, o2 = ar.at(tmpo, [128, KB], I32, "t_posi")
        t_x, o2 = ar.at(o2, [128, KB], F32, "t_x")
        t_ki, o2 = ar.at(o2, [128, KB], I32, "t_ki")
        t_kf, o2 = ar.at(o2, [128, KB], F32, "t_kf")
        t_r, o2 = ar.at(o2, [128, KB], F32, "t_r")
        assert o2 <= tmpo + 2 * 16 * 512 * 2

        sch.dma("sp", "c_gp", lambda e: e.dma_start(out=gpre_bc[:], in_=g_pre.broadcast_to([128, D])), writes=[Bc])

        cnt = {"stg": 0, "stf": 0, "blk": 0}
        Bdk = [Buf(f"dk{i}") for i in range(8)]
        Bdq = [Buf(f"dq{i}") for i in range(8)]
        Bgate = [Buf(f"gate{i}") for i in range(16)]
        Bdv = [Buf(f"dv{i}") for i in range(2)]

        def rope_tables(goff):
            for c0 in range(0, Sh, KB):
                n = min(KB, Sh - c0)
                sch.dma("sp", "t_pos", lambda e, c0=c0, n=n: e.dma_start(out=t_posi[:, 0:n], in_=posrow[0:1, goff + c0:goff + c0 + n].broadcast_to([128, n])), writes=[Bwg[0], Bwg[1]])
                for which in (0, 1):
                    sch.op("dve", lambda e, n=n: e.tensor_copy(out=t_x[:, 0:n], in_=t_posi[:, 0:n]), writes=[Bwg[0], Bwg[1]])
                    if which == 0:
                        sch.op("dve", lambda e, n=n: e.tensor_scalar(out=t_x[:, 0:n], in0=t_x[:, 0:n], scalar1=ropec[:, 0:1], scalar2=None, op0=ALU.mult), reads=[Bc], writes=[Bwg[0], Bwg[1]])
                    else:
                        sch.op("dve", lambda e, n=n: e.tensor_scalar(out=t_x[:, 0:n], in0=t_x[:, 0:n], scalar1=ropec[:, 0:1], scalar2=math.pi / 2, op0=ALU.mult, op1=ALU.add), reads=[Bc], writes=[Bwg[0], Bwg[1]])
                    sch.op("dve", lambda e, n=n: e.tensor_scalar(out=t_kf[:, 0:n], in0=t_x[:, 0:n], scalar1=1.0 / TWO_PI, scalar2=None, op0=ALU.mult), writes=[Bwg[0], Bwg[1]])
                    sch.op("dve", lambda e, n=n: e.tensor_copy(out=t_ki[:, 0:n], in_=t_kf[:, 0:n]), writes=[Bwg[0], Bwg[1]])
                    sch.op("dve", lambda e, n=n: e.tensor_copy(out=t_kf[:, 0:n], in_=t_ki[:, 0:n]), writes=[Bwg[0], Bwg[1]])
                    sch.op("dve", lambda e, n=n: e.scalar_tensor_tensor(out=t_r[:, 0:n], in0=t_kf[:, 0:n], scalar=-C1, in1=t_x[:, 0:n], op0=ALU.mult, op1=ALU.add), writes=[Bwg[0], Bwg[1]])
                    sch.op("dve", lambda e, n=n: e.scalar_tensor_tensor(out=t_r[:, 0:n], in0=t_kf[:, 0:n], scalar=-C2, in1=t_r[:, 0:n], op0=ALU.mult, op1=ALU.add), writes=[Bwg[0], Bwg[1]])
                    sch.op("dve", lambda e, n=n: e.tensor_scalar(out=t_r[:, 0:n], in0=t_r[:, 0:n], scalar1=PI_SAFE, scalar2=-PI_SAFE, op0=ALU.min, op1=ALU.max), writes=[Bwg[0], Bwg[1]])
                    if which == 0:
                        sch.op("act", lambda e, c0=c0, n=n: e.activation(out=sinS[:, c0:c0 + n], in_=t_r[:, 0:n], func=AF.Sin, scale=ropec[:, 1:2]), reads=[Bwg[0], Bwg[1], Bc], writes=[Btab])
                    else:
                        sch.op("act", lambda e, c0=c0, n=n: e.activation(out=cosT[:, c0:c0 + n], in_=t_r[:, 0:n], func=AF.Sin), reads=[Bwg[0], Bwg[1]], writes=[Btab])

        def stage0(goff):
            for b_ in (Bxt[2], Bxt[3]):
                b_.w = Bwg[1].w
                b_.r = dict(Bwg[1].r)
            def phaseA(tb):
                i = tb % 4
                i2 = tb % 2
                r0 = goff + tb * 128
                sch.dma("sp", f"xt{i}", lambda e, i=i, r0=r0: e.dma_start(out=xt[i][:], in_=xp[r0:r0 + 128, :]), writes=[Bxt[i]])
                sch.op("act", lambda e, i=i: e.activation(out=junk[:], in_=xt[i][:], func=AF.Square, accum_out=ssq[:, i:i + 1]), reads=[Bxt[i]], writes=[Bjunk, Bc])
                sch.op("act", lambda e, i=i: e.activation(out=lnc[:, i:i + 1], in_=ssq[:, i:i + 1], func=AF.Ln, scale=1.0 / D, bias=epsc[:, 0:1]), reads=[Bc], writes=[Bc])
                sch.op("act", lambda e, i=i: e.activation(out=rsc[:, i:i + 1], in_=lnc[:, i:i + 1], func=AF.Exp, scale=-0.5), reads=[Bc], writes=[Bc])
                sch.op("dve", lambda e, i=i, i2=i2: e.scalar_tensor_tensor(out=xb[i2][:], in0=xt[i][:], scalar=rsc[:, i:i + 1], in1=gpre_bc[:], op0=ALU.mult, op1=ALU.mult), reads=[Bxt[i], Bc], writes=[Bxb[i2]])

            def phaseB(tb):
                i2 = tb % 2
                for hb in range(2):
                    bank = 4 + 2 * i2 + hb
                    pv = ps[bank][:].bitcast(BF16).rearrange("p (c t) -> p c t", c=8)

                    def tr(e, i2=i2, hb=hb, pv=pv):
                        for c in range(8):
                            inst = e.transpose(pv[:, c, :], xb[i2][:, (hb * 8 + c) * 128:(hb * 8 + c + 1) * 128], identb[:])
                        return inst
                    sch.op("pe", tr, reads=[Bxb[i2], Bc], writes=[psB[bank]])
                    dst = hT[:, hb * 8:hb * 8 + 8, tb * 128:(tb + 1) * 128]
                    if hb == 0:
                        sch.op("act", lambda e, dst=dst, pv=pv: e.activation(out=dst, in_=pv[:, :, :], func=AF.Copy), reads=[psB[bank]], writes=[BhT[tb // 4]])
                    else:
                        sch.op("dve", lambda e, dst=dst, pv=pv: e.tensor_copy(out=dst, in_=pv[:, :, :]), reads=[psB[bank]], writes=[BhT[tb // 4]])

            phaseA(0)
            for tb in range(NBo):
                if tb + 1 < NBo:
                    phaseA(tb + 1)
                phaseB(tb)
            for b_ in (Bxt[2], Bxt[3]):
                toks = list(b_.r.items()) + ([b_.w] if b_.w is not None else [])
                for k_, v_ in toks:
                    if Bwg[1].r.get(k_, 0) < v_:
                        Bwg[1].r[k_] = v_

        def load_group(slot, pieces):
            for (d0, s0, n) in pieces:
                sch.dma("pool", f"wg{slot}", lambda e, d0=d0, s0=s0, n=n: e.dma_start(out=wg[slot][:, :, d0:d0 + n], in_=w_in_v[:, :, s0:s0 + n]), writes=[Bwg[slot]])

        def mm16(bank, lhs_fn, rhs_fn, reads, M=128, N=512):
            def f(e):
                for k in range(16):
                    inst = e.matmul(ps[bank][0:M, 0:N], lhsT=lhs_fn(k), rhs=rhs_fn(k), start=(k == 0), stop=(k == 15))
                return inst
            sch.op("pe", f, reads=reads, writes=[psB[bank]])

        def store_bf(bank, dst_ap, dbuf, scale=None, eng="dve"):
            i = cnt["stg"] % NSTG
            cnt["stg"] += 1
            if scale is not None:
                sch.op("dve", lambda e: e.tensor_scalar(out=stg[i][:], in0=ps[bank][:], scalar1=scale, scalar2=None, op0=ALU.mult), reads=[psB[bank]], writes=[Bstg[i]])
            elif eng == "dve":
                sch.op("dve", lambda e: e.tensor_copy(out=stg[i][:], in_=ps[bank][:]), reads=[psB[bank]], writes=[Bstg[i]])
            else:
                sch.op("act", lambda e: e.activation(out=stg[i][:], in_=ps[bank][:], func=AF.Copy), reads=[psB[bank]], writes=[Bstg[i]])
            sch.dma("pool", f"stg{i}", lambda e: e.dma_start(out=dst_ap, in_=stg[i][:]), reads=[Bstg[i]], writes=[dbuf])

        def store_gate(bank, dst_ap, dbuf):
            i = cnt["stf"] % NSTF
            cnt["stf"] += 1
            sch.op("act", lambda e: e.activation(out=stf[i][:], in_=ps[bank][:], func=AF.Silu), reads=[psB[bank]], writes=[Bstf[i]])
            sch.dma("pool", f"stf{i}", lambda e: e.dma_start(out=dst_ap, in_=stf[i][:]), reads=[Bstf[i]], writes=[dbuf])

        def rms_feature_major(nchunks, nfeat, gcols, dst_fn, dbufs_fn, tile):
            for c in range(nchunks):
                sch.op("act", lambda e, c=c: e.activation(out=sq[:, c, :], in_=ps[c][:], func=AF.Square), reads=[psB[c]], writes=[Bsq])

            def f(e):
                for c in range(nchunks):
                    inst = e.matmul(ps[4][:], lhsT=onesb[:], rhs=sq[:, c, :], start=(c == 0), stop=(c == nchunks - 1))
                return inst
            sch.op("pe", f, reads=[Bsq, Bc], writes=[psB[4]])
            sch.op("act", lambda e: e.activation(out=lnt[:], in_=ps[4][:], func=AF.Ln, scale=1.0 / nfeat, bias=epsc[:, 0:1]), reads=[psB[4], Bc], writes=[Brs])
            sch.op("act", lambda e: e.activation(out=rst[:], in_=lnt[:], func=AF.Exp, scale=-0.5), reads=[Brs], writes=[Brs])
            for c in range(nchunks):
                sch.op("dve", lambda e, c=c: e.scalar_tensor_tensor(out=dst_fn(c), in0=ps[c][:], scalar=gcols[:, c:c + 1], in1=rst[:], op0=ALU.mult, op1=ALU.mult),
                       reads=[psB[c], Brs, Bc], writes=dbufs_fn())

        def rope_combine(bankA, bankB, tcols, dst_ap, dbufs):
            sch.op("dve", lambda e: e.tensor_tensor(out=rt1[0:64, :], in0=ps[bankA][0:64, :], in1=cosT[0:64, tcols], op=ALU.mult), reads=[psB[bankA], Btab], writes=[Brt])
            sch.op("dve", lambda e: e.tensor_tensor(out=rt2[0:64, :], in0=ps[bankB][0:64, :], in1=sinS[0:64, tcols], op=ALU.mult), reads=[psB[bankB], Btab], writes=[Brt])
            sch.op("dve", lambda e: e.tensor_tensor(out=dst_ap, in0=rt1[0:64, :], in1=rt2[0:64, :], op=ALU.add), reads=[Brt], writes=dbufs)

        def stage1_groups(goff, own):
            groups = []
            pb = [0]

            def g_qlat(s):
                for t in range(NT):
                    tc = slice(t * 512, (t + 1) * 512)
                    for c in range(4):
                        mm16(c, lambda k, c=c, s=s: wg[s][:, k, c * 128:(c + 1) * 128], lambda k, tc=tc: hT[:, k, tc], [Bwg[s], BhT[t]])
                    rms_feature_major(4, 512, gqa, lambda c, tc=tc: cqT[:, c, tc], lambda t=t: [Bcq[t]], t)
            if own:
                groups.append(([(0, 0, 512)], g_qlat))

            def g_kvk(s):
                for t in range(NT):
                    tc = slice(t * 512, (t + 1) * 512)
                    gt_ = (goff // 512) + t
                    gc = slice(goff + t * 512, goff + (t + 1) * 512)
                    for c in range(2):
                        mm16(c, lambda k, c=c, s=s: wg[s][:, k, c * 128:(c + 1) * 128], lambda k, tc=tc: hT[:, k, tc], [Bwg[s], BhT[t]])
                    mm16(2, lambda k, s=s: wg[s][:, k, 256:320], lambda k, tc=tc: hT[:, k, tc], [Bwg[s], BhT[t]], M=64)
                    mm16(3, lambda k, s=s: wg[s][:, k, 320:384], lambda k, tc=tc: hT[:, k, tc], [Bwg[s], BhT[t]], M=64)
                    rms_feature_major(2, 256, gkva, lambda c, gc=gc: ckvT[:, c, gc], lambda gt_=gt_: [Bckv[gt_]], t)
                    rope_combine(2, 3, tc, krT[0:64, gc], [Bkr[gt_]])
            groups.append(([(0, 512, 320), (320, 800, 32), (352, 768, 32)], g_kvk))

            fm = []
            if own:
                fm += [("gate", 832, 0), ("gate", 832 + 512, 4), ("dq", 1856, 0), ("dq", 1856 + 512, 4)]
            fm += [("dk", 2880, 0), ("dk", 2880 + 512, 4)]
            if own:
                fm += [("gate", 4928, 8), ("gate", 4928 + 512, 12)]

            def mk_fm(kind, col0, hc0):
                def g_fm(s):
                    for t in range(NT):
                        tc = slice(t * 512, (t + 1) * 512)
                        gc = slice(goff + t * 512, goff + (t + 1) * 512)
                        base = (pb[0] % 2) * 4
                        pb[0] += 1
                        for c in range(4):
                            mm16(base + c, lambda k, c=c, s=s: wg[s][:, k, c * 128:(c + 1) * 128], lambda k, tc=tc: hT[:, k, tc], [Bwg[s], BhT[t]])
                        for c in range(4):
                            hc = hc0 + c
                            if kind == "gate":
                                store_gate(base + c, gate_scr[hc, :, tc], Bgate[hc])
                            elif kind == "dq":
                                store_bf(base + c, dqT[hc, :, tc], Bdq[hc], scale=0.125)
                            else:
                                store_bf(base + c, dkT[hc, :, gc], Bdk[hc], eng=("dve" if c % 2 == 0 else "act"))
                return g_fm
            for (kind, col0, hc0) in fm:
                groups.append(([(0, col0, 512)], mk_fm(kind, col0, hc0)))

            def mk_dv(g):
                def g_dv(s):
                    for tb in range(NBo):
                        bank = pb[0] % 8
                        pb[0] += 1
                        tcb = slice(tb * 128, (tb + 1) * 128)
                        mm16(bank, lambda k, tcb=tcb: hT[:, k, tcb], lambda k, s=s: wg[s][:, k, 0:512], [Bwg[s], BhT[tb // 4]])
                        nbg = goff // 128 + tb
                        store_bf(bank, dvs[nbg, :, g * 512:(g + 1) * 512], Bdv[g], eng=("dve" if tb % 2 == 0 else "act"))
                return g_dv
            for g in range(2):
                groups.append(([(0, 3904 + g * 512, 512)], mk_dv(g)))
            return groups

        gslot = [0]
        for (goff, own) in ((Sh, False), (0, True)):
            rope_tables(goff)
            groups = stage1_groups(goff, own)
            gslot[0] = 0
            slots = []
            for gi in range(len(groups)):
                slots.append(gslot[0] % 2)
                gslot[0] += 1
            load_group(slots[0], groups[0][0])
            stage0(goff)
            for gi, (pieces, comp) in enumerate(groups):
                if gi + 1 < len(groups):
                    load_group(slots[gi + 1], groups[gi + 1][0])
                comp(slots[gi])

        sch.barrier()

        def build_steps():
            tiles = []
            for j in range(NQ):
                st = []
                for kb in range(4 * j):
                    st.append((kb, 0, None))
                    st.append((NBo + kb, 0, None))
                for i in range(4):
                    st.append((4 * j + i, i, 0))
                    st.append((NBo + 4 * j + i, i, 1 + i))
                tiles.append(st)
            return tiles
        steps_by_tile = build_steps()
        if mla_heads < 8 or diff_heads < 8:
            for hh in range(16):
                sch.op("pool", lambda e, hh=hh: e.memset(mixT[:, hh, :], 0.0), writes=[Bmix[hh]])

        def col_groups(c0, gw):
            res = []
            b = c0
            while b < 4:
                e_ = min(4, (b // gw + 1) * gw)
                res.append((b, e_))
                b = e_
            return res

        def attention(nbr, qk_fn, v_fn, exp_args_fn, gw, ST, Obank_fn, Lbank_fn, PT, BPT, fin_fn, reads_qk, reads_v, la, reads_exp=()):
            flat = [(j, t, stp) for j, st in enumerate(steps_by_tile) for t, stp in enumerate(st)]
            n = len(flat)

            def emit_qk(idx):
                j, t, (kb, c0, mk) = flat[idx]
                for br in range(nbr):
                    bank = ST[br][idx % len(ST[br])]
                    sch.op("pe", lambda e, br=br, bank=bank, kb=kb, j=j, c0=c0: qk_fn(e, br, ps[bank], kb, j, c0), reads=reads_qk, writes=[psB[bank]])

            def emit_exp(idx):
                j, t, (kb, c0, mk) = flat[idx]
                for br in range(nbr):
                    bank = ST[br][idx % len(ST[br])]
                    pi = idx % len(PT[br])
                    for (b0, b1) in col_groups(c0, gw):
                        cs = slice(b0 * 128, b1 * 128)
                        kw = exp_args_fn(kb, (4 * j + b0) // gw)
                        sch.op("act", lambda e, bank=bank, br=br, pi=pi, cs=cs, kw=kw: e.activation(out=PT[br][pi][:, cs], in_=ps[bank][:, cs], func=AF.Exp, **kw),
                               reads=[psB[bank], Bc] + list(reads_exp), writes=[BPT[br][pi]])
                    if mk is not None:
                        cs = slice(c0 * 128, (c0 + 1) * 128)
                        sch.op("dve", lambda e, br=br, pi=pi, cs=cs, mk=mk: e.tensor_tensor(out=PT[br][pi][:, cs], in0=PT[br][pi][:, cs], in1=masksb[:, mk, :], op=ALU.mult),
                               reads=[Bc], writes=[BPT[br][pi]])

            def emit_pv(idx):
                j, t, (kb, c0, mk) = flat[idx]
                first = (t == 0)
                last = (t == len(steps_by_tile[j]) - 1)
                cs = slice(c0 * 128, 512)
                for br in range(nbr):
                    pi = idx % len(PT[br])
                    ob = Obank_fn(br, j)
                    lbk = Lbank_fn(br, j)

                    def f(e, br=br, pi=pi, ob=ob, lbk=lbk, kb=kb, cs=cs, first=first, last=last):
                        e.matmul(ps[ob][:, cs], lhsT=v_fn(br, kb), rhs=PT[br][pi][:, cs], start=first, stop=last, skip_group_check=True)
                        return e.matmul(ps[lbk][:, cs], lhsT=onesb[:], rhs=PT[br][pi][:, cs], start=first, stop=last, skip_group_check=True)
                    sch.op("pe", f, reads=[BPT[br][pi], Bc] + reads_v, writes=[psB[ob], psB[lbk]])
                if last:
                    fin_fn(j)

            for idx in range(min(la, n)):
                emit_qk(idx)
            for idx in range(n):
                emit_exp(idx)
                if idx + la < n:
                    emit_qk(idx + la)
                emit_pv(idx)

        ab = Bump(ar, LAT_END, MIX_OFF)
        wq = [ab.alloc([128, 4, 256], BF16, f"wq{i}") for i in range(2)]
        wkv = [ab.alloc([128, 2, 256], BF16, f"wkv{i}") for i in range(2)]
        Bwq = [Buf("wq0"), Buf("wq1")]
        KnT = ab.alloc([128, S], BF16, "KnT")
        Vh = ab.alloc([128, NB, 128], BF16, "Vh")
        QnT = ab.alloc([128, Sh], BF16, "QnT")
        QrT = ab.alloc([128, Sh], BF16, "QrT")
        gtm = ab.alloc([128, Sh], F32, "gtm")
        BK, BV, BQ, Bg = Buf("K"), Buf("V"), Buf("Q"), Buf("g")
        PTm = [[ab.alloc([128, 512], BF16, f"PTm{i}") for i in range(4)]]
        BPTm = [[Buf(f"PTm{i}") for i in range(4)]]
        rL = [ab.alloc([128, 512], F32, f"rL{i}") for i in range(2)]
        tg = [ab.alloc([128, 512], F32, f"tg{i}") for i in range(2)]
        Bfin = [Buf("fin0"), Buf("fin1")]
        rt1m = ab.alloc([128, 512], F32, "rt1m")
        rt2m = ab.alloc([128, 512], F32, "rt2m")
        Brtm = Buf("rtm")
        sch.op("pool", lambda e: e.memset(QrT[64:128, :], 0.0), writes=[BQ])
        SCALE_MLA = 192.0 ** -0.5

        def load_mla_w(h):
            s = h % 2
            sch.dma("pool", f"wq{s}", lambda e: e.dma_start(out=wq[s][:, :, 0:192], in_=w_qb_v[:, :, h * 192:h * 192 + 192]), writes=[Bwq[s]])
            sch.dma("pool", f"wq{s}", lambda e: e.dma_start(out=wq[s][:, :, 192:224], in_=w_qb_v[:, :, h * 192 + 160:h * 192 + 192]), writes=[Bwq[s]])
            sch.dma("pool", f"wq{s}", lambda e: e.dma_start(out=wq[s][:, :, 224:256], in_=w_qb_v[:, :, h * 192 + 128:h * 192 + 160]), writes=[Bwq[s]])
            sch.dma("pool", f"wq{s}", lambda e: e.dma_start(out=wkv[s][:, :, :], in_=w_kvb_v[:, :, h * 256:(h + 1) * 256]), writes=[Bwq[s]])

        if mla_heads > 0:
            load_mla_w(0)
        for h in range(mla_heads):
            s = h % 2
            if h + 1 < mla_heads:
                load_mla_w(h + 1)
            sch.dma("sp", "gtm", lambda e, h=h: e.dma_start(out=gtm[:], in_=gate_scr[h, :, :]), reads=[Bgate[h]], writes=[Bg])
            prep_banks = [0, 1, 2, 7]
            pc = [0]

            def nb_():
                b = prep_banks[pc[0] % 4]
                pc[0] += 1
                return b
            for t in range(S // 512):
                bank = nb_()
                tc = slice(t * 512, (t + 1) * 512)

                def f(e, bank=bank, tc=tc, s=s):
                    for c in range(2):
                        inst = e.matmul(ps[bank][:], lhsT=wkv[s][:, c, 0:128], rhs=ckvT[:, c, tc], start=(c == 0), stop=(c == 1))
                    return inst
                sch.op("pe", f, reads=[Bwq[s], Bckv[t]], writes=[psB[bank]])
                if t % 2 == 0:
                    sch.op("dve", lambda e, bank=bank, tc=tc: e.tensor_copy(out=KnT[:, tc], in_=ps[bank][:]), reads=[psB[bank]], writes=[BK])
                else:
                    sch.op("act", lambda e, bank=bank, tc=tc: e.activation(out=KnT[:, tc], in_=ps[bank][:], func=AF.Copy), reads=[psB[bank]], writes=[BK])
            for q4 in range(NB // 4):
                bank = nb_()

                def f(e, bank=bank, q4=q4, s=s):
                    for jj in range(4):
                        blk = q4 * 4 + jj
                        for c in range(2):
                            inst = e.matmul(ps[bank][:, jj * 128:(jj + 1) * 128], lhsT=ckvT[:, c, blk * 128:(blk + 1) * 128], rhs=wkv[s][:, c, 128:256], start=(c == 0), stop=(c == 1))
                    return inst
                sch.op("pe", f, reads=[Bwq[s], Bckv[q4]], writes=[psB[bank]])
                dst = Vh[:, q4 * 4:q4 * 4 + 4, :]
                if q4 % 2 == 0:
                    sch.op("act", lambda e, bank=bank, dst=dst: e.activation(out=dst, in_=ps[bank][:].rearrange("p (a b) -> p a b", a=4), func=AF.Copy), reads=[psB[bank]], writes=[BV])
                else:
                    sch.op("dve", lambda e, bank=bank, dst=dst: e.tensor_copy(out=dst, in_=ps[bank][:].rearrange("p (a b) -> p a b", a=4)), reads=[psB[bank]], writes=[BV])
            for t in range(NT):
                tc = slice(t * 512, (t + 1) * 512)
                bank = nb_()

                def f(e, bank=bank, tc=tc, s=s):
                    for c in range(4):
                        inst = e.matmul(ps[bank][:], lhsT=wq[s][:, c, 0:128], rhs=cqT[:, c, tc], start=(c == 0), stop=(c == 3))
                    return inst
                sch.op("pe", f, reads=[Bwq[s], Bcq[t]], writes=[psB[bank]])
                sch.op("dve", lambda e, bank=bank, tc=tc: e.tensor_copy(out=QnT[:, tc], in_=ps[bank][:]), reads=[psB[bank]], writes=[BQ])
                bA = nb_()
                bB = nb_()

                def f2(e, bA=bA, bB=bB, tc=tc, s=s):
                    for c in range(4):
                        e.matmul(ps[bA][0:64, :], lhsT=wq[s][:, c, 128:192], rhs=cqT[:, c, tc], start=(c == 0), stop=(c == 3))
                    for c in range(4):
                        inst = e.matmul(ps[bB][0:64, :], lhsT=wq[s][:, c, 192:256], rhs=cqT[:, c, tc], start=(c == 0), stop=(c == 3))
                    return inst
                sch.op("pe", f2, reads=[Bwq[s], Bcq[t]], writes=[psB[bA], psB[bB]])
                sch.op("dve", lambda e, bA=bA, tc=tc: e.tensor_tensor(out=rt1m[0:64, :], in0=ps[bA][0:64, :], in1=cosT[0:64, tc], op=ALU.mult), reads=[psB[bA], Btab], writes=[Brtm])
                sch.op("dve", lambda e, bB=bB, tc=tc: e.tensor_tensor(out=rt2m[0:64, :], in0=ps[bB][0:64, :], in1=sinS[0:64, tc], op=ALU.mult), reads=[psB[bB], Btab], writes=[Brtm])
                sch.op("dve", lambda e, tc=tc: e.tensor_tensor(out=QrT[0:64, tc], in0=rt1m[0:64, :], in1=rt2m[0:64, :], op=ALU.add), reads=[Brtm], writes=[BQ])

            def qk_mla(e, br, pst, kb, j, c0):
                ks = slice(kb * 128, (kb + 1) * 128)
                qs = slice(j * 512 + c0 * 128, (j + 1) * 512)
                os_ = slice(c0 * 128, 512)
                e.matmul(pst[:, os_], lhsT=KnT[:, ks], rhs=QnT[:, qs], start=True, stop=False)
                return e.matmul(pst[:, os_], lhsT=krT[:, ks], rhs=QrT[:, qs], start=False, stop=True)

            def fin_mla(j, h=h):
                i = j % 2
                ob, lbk = 3 + i, 5 + i
                tc = slice(j * 512, (j + 1) * 512)
                sch.op("dve", lambda e: e.reciprocal(out=rL[i][:], in_=ps[lbk][:]), reads=[psB[lbk]], writes=[Bfin[i]])
                sch.op("dve", lambda e: e.tensor_tensor(out=tg[i][:], in0=rL[i][:], in1=gtm[:, tc], op=ALU.mult), reads=[Bg], writes=[Bfin[i]])
                sch.op("dve", lambda e: e.tensor_tensor(out=mixT[:, h, tc], in0=ps[ob][:], in1=tg[i][:], op=ALU.mult), reads=[psB[ob], Bfin[i]], writes=[Bmix[h]])

            attention(1, qk_mla, lambda br, kb: Vh[:, kb, :], lambda kb, g: dict(scale=SCALE_MLA), 4,
                      [[0, 1, 2]], lambda br, j: 3 + (j % 2), lambda br, j: 5 + (j % 2), PTm, BPTm, fin_mla,
                      [BK, BQ, Bkr[0]] + Bkr[1:], [BV], 2)

        sch.barrier()

        db = Bump(ar, CONST_END, MIX_OFF)
        wout = db.alloc([128, 16, D], BF16, "wout")
        Bwout = Buf("wout")
        fin_mark = db.p
        K12 = [db.alloc([128, S], BF16, f"K12_{i}") for i in range(2)]
        Q12 = [db.alloc([128, Sh], BF16, f"Q12_{i}") for i in range(2)]
        Vd = [db.alloc([128, NB, 128], BF16, f"Vd{i}") for i in range(2)]
        gtd = [db.alloc([128, Sh], F32, f"gtd{i}") for i in range(2)]
        NGmax = NBo
        bt = [db.alloc([128, NB, NGmax], F32, f"bt{i}") for i in range(2)]
        Bhd = [Buf("hd0"), Buf("hd1")]
        Bbt = [Buf("bt0"), Buf("bt1")]
        PTd = [[db.alloc([128, 512], BF16, f"PTd{b}_{i}") for i in range(3)] for b in range(2)]
        BPTd = [[Buf(f"PTd{b}_{i}") for i in range(3)] for b in range(2)]
        rL1 = db.alloc([128, 512], F32, "rL1")
        rL2 = db.alloc([128, 512], F32, "rL2")
        t1, t2, od, ud = rL1, rL2, rL1, rL1
        sqd = db.alloc([128, 512], BF16, "sqd")
        lnd = db.alloc([128, 512], F32, "lnd")
        rsd = lnd
        Bfd = Buf("fd")

        def load_diff(h):
            s = h % 2
            sch.dma("sp", f"hd{s}", lambda e: e.dma_start(out=K12[s][:], in_=dkT[h, :, :]), reads=[Bdk[h]], writes=[Bhd[s]])
            sch.dma("sp", f"hd{s}", lambda e: e.dma_start(out=Q12[s][:], in_=dqT[h, :, :]), reads=[Bdq[h]], writes=[Bhd[s]])
            sch.dma("sp", f"hd{s}", lambda e: e.dma_start(out=Vd[s][:], in_=dvs[:, :, h * 128:(h + 1) * 128].rearrange("nb p d -> p nb d")), reads=[Bdv[h // 4]], writes=[Bhd[s]])
            sch.dma("sp", f"hd{s}", lambda e: e.dma_start(out=gtd[s][:], in_=gate_scr[8 + h, :, :]), reads=[Bgate[8 + h]], writes=[Bhd[s]])

        if diff_heads > 0:
            load_diff(0)
        for h in range(diff_heads):
            s = h % 2
            gw = gws[h]
            NG = NBo // gw
            slope = 2.0 ** (-(h + 1))
            if h + 1 < diff_heads:
                load_diff(h + 1)
            if h == 0:
                for g in range(4):
                    sch.dma("pool", "wout", lambda e, g=g: e.dma_start(out=wout[:, :, g * 512:(g + 1) * 512], in_=w_out_v[:, :, g * 512:(g + 1) * 512]), writes=[Bwout])
            btv = bt[s][:, :, 0:NG]
            sch.op("dve", lambda e, btv=btv, NG=NG, gw=gw: e.tensor_tensor(out=btv, in0=poskf[:].unsqueeze(2).broadcast_to([128, NB, NG]), in1=pmids[gw][:].unsqueeze(1).broadcast_to([128, NB, NG]), op=ALU.subtract),
                   reads=[Bc], writes=[Bbt[s]])
            sch.op("dve", lambda e, btv=btv, slope=slope: e.tensor_scalar(out=btv, in0=btv, scalar1=slope, scalar2=40.0, op0=ALU.mult, op1=ALU.min), writes=[Bbt[s]])

            def qk_diff(e, br, pst, kb, j, c0, s=s):
                ks = slice(kb * 128, (kb + 1) * 128)
                qs = slice(j * 512 + c0 * 128, (j + 1) * 512)
                os_ = slice(c0 * 128, 512)
                pr = slice(64 * br, 64 * br + 64)
                return e.matmul(pst[:, os_], lhsT=K12[s][pr, ks], rhs=Q12[s][pr, qs], start=True, stop=True)

            def fin_diff(j, h=h, s=s):
                tc = slice(j * 512, (j + 1) * 512)
                sch.op("dve", lambda e: e.reciprocal(out=rL1[:], in_=ps[6][:]), reads=[psB[6]], writes=[Bfd])
                sch.op("dve", lambda e: e.reciprocal(out=rL2[:], in_=ps[7][:]), reads=[psB[7]], writes=[Bfd])
                sch.op("dve", lambda e: e.tensor_tensor(out=t1[:], in0=ps[4][:], in1=rL1[:], op=ALU.mult), reads=[psB[4]], writes=[Bfd])
                sch.op("dve", lambda e: e.tensor_tensor(out=t2[:], in0=ps[5][:], in1=rL2[:], op=ALU.mult), reads=[psB[5]], writes=[Bfd])
                sch.op("dve", lambda e: e.scalar_tensor_tensor(out=od[:], in0=t2[:], scalar=negl[:, 0:1], in1=t1[:], op0=ALU.mult, op1=ALU.add), reads=[Bc], writes=[Bfd])
                sch.op("act", lambda e: e.activation(out=sqd[:], in_=od[:], func=AF.Square), reads=[Bfd], writes=[Bfd])

                def f(e):
                    return e.matmul(ps[6][:], lhsT=onesb[:], rhs=sqd[:], start=True, stop=True)
                sch.op("pe", f, reads=[Bfd, Bc], writes=[psB[6]])
                sch.op("act", lambda e: e.activation(out=lnd[:], in_=ps[6][:], func=AF.Ln, scale=1.0 / 128.0, bias=epsc[:, 0:1]), reads=[psB[6], Bc], writes=[Bfd])
                sch.op("act", lambda e: e.activation(out=rsd[:], in_=lnd[:], func=AF.Exp, scale=-0.5), reads=[Bfd], writes=[Bfd])
                sch.op("dve", lambda e: e.scalar_tensor_tensor(out=ud[:], in0=od[:], scalar=gsub8[:, 0:1], in1=rsd[:], op0=ALU.mult, op1=ALU.mult), reads=[Bc], writes=[Bfd])
                sch.op("dve", lambda e: e.tensor_tensor(out=mixT[:, 8 + h, tc], in0=ud[:], in1=gtd[s][:, tc], op=ALU.mult), reads=[Bhd[s]], writes=[Bmix[8 + h], Bfd])

            attention(2, qk_diff, lambda br, kb, s=s: Vd[s][:, kb, :], lambda kb, g, s=s: dict(bias=bt[s][:, kb, g:g + 1], scale=1.0), gw,
                      [[0, 1], [2, 3]], lambda br, j: 4 + br, lambda br, j: 6 + br, PTd, BPTd, fin_diff,
                      [Bhd[s]], [Bhd[s]], 1, reads_exp=[Bbt[s]])

        sch.barrier()

        fb = Bump(ar, fin_mark, MIX_OFF)
        gpost_bc = fb.alloc([128, D], F32, "gpost_bc")
        xo = [fb.alloc([128, D], F32, f"xo{i}") for i in range(2)]
        yo = [fb.alloc([128, D], F32, f"yo{i}") for i in range(2)]
        junk2 = fb.alloc([128, 512], BF16, "junk2")
        Bxo = [Buf("xo0"), Buf("xo1")]
        Byo = [Buf("yo0"), Buf("yo1")]
        Bj2 = Buf("junk2")
        Bgp = Buf("gpost")
        ss4 = [fb.alloc([128, 4], F32, f"ss4_{i}") for i in range(2)]
        ss1 = [fb.alloc([128, 1], F32, f"ss1_{i}") for i in range(2)]
        ln1 = [fb.alloc([128, 1], F32, f"ln1_{i}") for i in range(2)]
        rs1 = [fb.alloc([128, 1], F32, f"rs1_{i}") for i in range(2)]
        Bst = [Buf("st0"), Buf("st1")]
        sch.dma("sp", "gpost", lambda e: e.dma_start(out=gpost_bc[:], in_=g_post.broadcast_to([128, D])), writes=[Bgp])
        for tb in range(NBo):
            i = tb % 2
            tcb = slice(tb * 128, (tb + 1) * 128)
            sch.dma("sp", f"xo{i}", lambda e, i=i, tb=tb: e.dma_start(out=xo[i][:], in_=xp[tb * 128:(tb + 1) * 128, :]), writes=[Bxo[i]])
            for g in range(4):
                bank = 4 * i + g

                def f(e, bank=bank, g=g, tcb=tcb):
                    for k in range(16):
                        inst = e.matmul(ps[bank][:], lhsT=mixT[:, k, tcb], rhs=wout[:, k, g * 512:(g + 1) * 512], start=(k == 0), stop=(k == 15))
                    return inst
                sch.op("pe", f, reads=Bmix + [Bwout], writes=[psB[bank]])
                sch.op("act", lambda e, bank=bank, g=g, i=i: e.activation(out=junk2[:], in_=ps[bank][:], func=AF.Square, accum_out=ss4[i][:, g:g + 1]), reads=[psB[bank]], writes=[Bj2, Bst[i]])
            sch.op("dve", lambda e, i=i: e.tensor_reduce(out=ss1[i][:], in_=ss4[i][:], axis=AX.X, op=ALU.add), reads=[Bst[i]], writes=[Bst[i]])
            sch.op("act", lambda e, i=i: e.activation(out=ln1[i][:], in_=ss1[i][:], func=AF.Ln, scale=1.0 / D, bias=epsc[:, 0:1]), reads=[Bst[i], Bc], writes=[Bst[i]])
            sch.op("act", lambda e, i=i: e.activation(out=rs1[i][:], in_=ln1[i][:], func=AF.Exp, scale=-0.5), reads=[Bst[i]], writes=[Bst[i]])
            for g in range(4):
                bank = 4 * i + g
                gs = slice(g * 512, (g + 1) * 512)
                sch.op("dve", lambda e, bank=bank, gs=gs, i=i: e.scalar_tensor_tensor(out=yo[i][:, gs], in0=ps[bank][:], scalar=rs1[i][:, 0:1], in1=gpost_bc[:, gs], op0=ALU.mult, op1=ALU.mult),
                       reads=[psB[bank], Bst[i], Bgp], writes=[Byo[i]])
            sch.op("dve", lambda e, i=i: e.tensor_tensor(out=yo[i][:], in0=yo[i][:], in1=xo[i][:], op=ALU.add), reads=[Bxo[i]], writes=[Byo[i]])
            toks_final.append(sch.dma("sp", f"yo{i}", lambda e, i=i, tb=tb: e.dma_start(out=out[tb * 128:(tb + 1) * 128, :], in_=yo[i][:]), reads=[Byo[i]]))

        sch.wait_all("sp", toks_final)
        sch.emit()
    return nc


def own_blocks(NB, parity):
    own = [g for g in range(NB) if ((g % 4) in (0, 3)) == (parity == 0)]
    other = [g for g in range(NB) if g not in own]
    return own, other


def make_core_inputs(S, parity, xb, posb, shared):
    NB = S // 128
    own, other = own_blocks(NB, parity)
    order = own + other
    tok = np.concatenate([np.arange(g * 128, (g + 1) * 128) for g in order])
    xp = np.ascontiguousarray(xb[tok])
    pp = np.ascontiguousarray(posb[tok]).astype(np.int32)
    masks = np.zeros((5, 128, 128), np.float32)
    k = np.arange(128)[:, None]
    q = np.arange(128)[None, :]
    masks[0] = (q >= k).astype(np.float32)
    flags = [0, 1, 0, 1] if parity == 0 else [1, 0, 1, 0]
    for i in range(4):
        masks[1 + i] = float(flags[i])
    d = dict(shared)
    d.update({
        "xp": xp,
        "posrow": pp.reshape(1, S),
        "postm": np.ascontiguousarray(pp.reshape(NB, 128).T),
        "posmid": np.ascontiguousarray(pp.reshape(NB, 128)[:NB // 2, 64]).reshape(1, NB // 2),
        "masks": masks,
    })
    return d, tok[:S // 2]


def make_shared(g_pre, w_in, g_q_a, w_q_b, g_kv_a, w_kv_b, lambda_q1, lambda_k1, lambda_q2, lambda_k2,
                g_diff_sub, w_out, g_post):
    f = np.float32
    freq = (1.0 / (np.float32(10000.0) ** (np.arange(0, 64, 2, dtype=np.float32) / np.float32(64)))).astype(f)
    ropec = np.zeros((128, 2), f)
    ropec[0:32, 0] = freq
    ropec[32:64, 0] = freq
    ropec[0:32, 1] = -1.0
    ropec[32:64, 1] = 1.0
    return {
        "w_in": np.ascontiguousarray(w_in[0], dtype=f),
        "w_qb": np.ascontiguousarray(w_q_b[0].reshape(512, 1536), dtype=f),
        "w_kvb": np.ascontiguousarray(w_kv_b[0].reshape(256, 2048), dtype=f),
        "w_out": np.ascontiguousarray(w_out[0], dtype=f),
        "g_pre": np.ascontiguousarray(g_pre[0].reshape(1, D), dtype=f),
        "gqa": np.ascontiguousarray(g_q_a[0].reshape(4, 128).T, dtype=f),
        "gkva": np.ascontiguousarray(g_kv_a[0].reshape(2, 128).T, dtype=f),
        "gsub": np.ascontiguousarray(g_diff_sub[0].reshape(1, 128).T, dtype=f),
        "g_post": np.ascontiguousarray(g_post[0].reshape(1, D), dtype=f),
        "lam": np.concatenate([lambda_q1[0], lambda_k1[0], lambda_q2[0], lambda_k2[0]]).reshape(1, 256).astype(f),
        "ident": np.eye(128, dtype=f),
        "ropec": ropec,
    }


_PROG_CACHE = {}


def run_layer(x, positions, params, **bkw):
    B, S, _ = x.shape
    shared = make_shared(**params)
    key = (S, tuple(sorted(bkw.items())))
    if key not in _PROG_CACHE:
        _PROG_CACHE[key] = build_program(S, **bkw)
    nc = _PROG_CACHE[key]
    in_maps, toks = [], []
    for b in range(B):
        for par in range(2):
            d, tok = make_core_inputs(S, par, x[b], positions[b], shared)
            in_maps.append(d)
            toks.append((b, tok))
    ncores = len(in_maps)
    res = run_bass_kernel_spmd(nc, in_maps, core_ids=list(range(ncores)))
    outp = np.empty((B, S, D), np.float32)
    for ci, (b, tok) in enumerate(toks):
        outp[b, tok] = res.results[ci]["out"]
    return outp


def kernel(x, positions, g_pre, w_in, g_q_a, w_q_b, g_kv_a, w_kv_b, lambda_q1, lambda_k1, lambda_q2, lambda_k2,
           g_diff_sub, w_out, g_post):
    params = dict(g_pre=np.asarray(g_pre), w_in=np.asarray(w_in), g_q_a=np.asarray(g_q_a), w_q_b=np.asarray(w_q_b),
                  g_kv_a=np.asarray(g_kv_a), w_kv_b=np.asarray(w_kv_b), lambda_q1=np.asarray(lambda_q1),
                  lambda_k1=np.asarray(lambda_k1), lambda_q2=np.asarray(lambda_q2), lambda_k2=np.asarray(lambda_k2),
                  g_diff_sub=np.asarray(g_diff_sub), w_out=np.asarray(w_out), g_post=np.asarray(g_post))
    return run_layer(np.asarray(x, dtype=np.float32), np.asarray(positions), params)
```
